# Optimizing a Trainium2 kernel written in Bass

```python
import math
import jax, jax.numpy as jnp
from jax import lax
import numpy as np

D_MODEL = 2048
BATCH = 4
SEQ = 2048
DEPTH = 1
DEC_BATCH = 4
DEC_SEQ = 4096
PAST_LEN = 128

H_A = 8
DK_A = 128
DV_A = 128
W_A = H_A * DV_A
CONV_DIM = 2 * H_A * DK_A + H_A * DV_A
KCONV = 5
CHUNK = 64
H_B = 8
Q_LORA = 1536
KV_LORA = 512
D_NOPE = 128
D_ROPE = 64
DV_B = 128
W_B = H_B * DV_B
Q_BLOCK = 128
ROPE_THETA = 10000.0
EPS = 1e-6
IN_SPLITS = (CONV_DIM, W_A, H_A, H_A, H_A, H_A, Q_LORA, KV_LORA, D_ROPE, W_B, D_MODEL, D_MODEL)
N_IN = sum(IN_SPLITS)

kernel_name = 'hybrid_gdn_mla_gated_encoder'


def rmsnorm(x, g):
    xf = x.astype(jnp.float32)
    y = xf * lax.rsqrt(jnp.mean(xf * xf, axis=-1, keepdims=True) + EPS)
    return (y * g.astype(jnp.float32)).astype(x.dtype)


def l2norm(x):
    xf = x.astype(jnp.float32)
    return xf * lax.rsqrt(jnp.sum(xf * xf, axis=-1, keepdims=True) + EPS)


def depthwise_conv(x, w):
    c = x.shape[-1]
    return lax.conv_general_dilated(x, w[:, None, :].astype(x.dtype), window_strides=(1,), padding='SAME',
                                    dimension_numbers=('NWC', 'WIO', 'NWC'), feature_group_count=c)


def rotary(t, cos, sin):
    half = D_ROPE // 2
    t1, t2 = t[..., :half], t[..., half:]
    return jnp.concatenate([t1 * cos - t2 * sin, t1 * sin + t2 * cos], axis=-1)


def gated_delta_chunked(q, k, v, g, beta):
    b, s, h, dk = q.shape
    dv = v.shape[-1]
    n = s // CHUNK
    f32 = jnp.float32

    def chunks(t):
        return t.astype(f32).reshape(b, n, CHUNK, h, -1).transpose(0, 3, 1, 2, 4)

    qc = chunks(q) * (dk ** -0.5)
    kc = chunks(k)
    vc = chunks(v)
    bc = chunks(beta[..., None])
    gc = jnp.cumsum(chunks(g[..., None])[..., 0], axis=-1)
    lower = jnp.tril(jnp.ones((CHUNK, CHUNK), dtype=bool))
    strict = jnp.tril(jnp.ones((CHUNK, CHUNK), dtype=bool), k=-1)
    diff = gc[..., :, None] - gc[..., None, :]
    decay = jnp.where(lower, jnp.exp(jnp.where(lower, diff, 0.0)), 0.0)
    k_beta = kc * bc
    kk = jnp.einsum('bhnid,bhnjd->bhnij', k_beta, kc) * decay
    a = jnp.where(strict, kk, 0.0) + jnp.eye(CHUNK, dtype=f32)
    rhs = jnp.concatenate([vc * bc, k_beta * jnp.exp(gc)[..., None]], axis=-1)
    sol = lax.linalg.triangular_solve(a, rhs, left_side=True, lower=True, unit_diagonal=True)
    u, w = sol[..., :dv], sol[..., dv:]
    attn = jnp.where(lower, jnp.einsum('bhnid,bhnjd->bhnij', qc, kc) * decay, 0.0)

    def step(state, xs):
        q_i, k_i, u_i, w_i, g_i, a_i = xs
        v_new = u_i - jnp.einsum('bhcd,bhde->bhce', w_i, state)
        o = (jnp.einsum('bhcd,bhde->bhce', q_i * jnp.exp(g_i)[..., None], state)
             + jnp.einsum('bhij,bhje->bhie', a_i, v_new))
        g_last = g_i[..., -1]
        k_dec = k_i * jnp.exp(g_last[..., None] - g_i)[..., None]
        state = state * jnp.exp(g_last)[..., None, None] + jnp.einsum('bhcd,bhce->bhde', k_dec, v_new)
        return state, o

    xs = tuple(jnp.moveaxis(t, 2, 0) for t in (qc, kc, u, w, gc, attn))
    state0 = jnp.zeros((b, h, dk, dv), f32)
    _, o = lax.scan(step, state0, xs)
    return o.transpose(1, 0, 3, 2, 4).reshape(b, s, h, dv)


def latent_attention_blocks(q, k, v):
    b, s, h, d = q.shape
    scale = d ** -0.5
    qb = q.reshape(b, s // Q_BLOCK, Q_BLOCK, h, d).transpose(1, 0, 2, 3, 4)

    def attend(q_blk):
        sc = jnp.einsum('bqhd,bkhd->bhqk', q_blk, k, preferred_element_type=jnp.float32) * scale
        p = jax.nn.softmax(sc, axis=-1).astype(v.dtype)
        return jnp.einsum('bhqk,bkhe->bqhe', p, v)

    o = lax.map(attend, qb)
    return o.transpose(1, 0, 2, 3, 4).reshape(b, s, h * v.shape[-1])


def encoder_layer(x, norm_in, w_in, conv_w, a_log_f, dt_bias_f, a_log_b, dt_bias_b, o_norm_a,
                  q_a_norm, w_q_b, kv_a_norm, w_kv_b, w_pa, w_pb, w_out):
    b, s, _ = x.shape
    f32 = jnp.float32
    xn = rmsnorm(x, norm_in)
    proj = xn @ w_in
    offsets = [int(o) for o in np.cumsum(IN_SPLITS)[:-1]]
    (qkv, gate_a, a_f, a_b, b_f, b_b, q_lat, kv_lat, k_rope, gate_b, gm_a, gm_b) = jnp.split(proj, offsets, axis=-1)

    qkv = jax.nn.silu(depthwise_conv(qkv, conv_w))
    q_a, k_a, v_a = jnp.split(qkv, [H_A * DK_A, 2 * H_A * DK_A], axis=-1)
    q_a = l2norm(q_a.reshape(b, s, H_A, DK_A))
    k_a = l2norm(k_a.reshape(b, s, H_A, DK_A))
    v_a = v_a.reshape(b, s, H_A, DV_A)
    g_f = -jnp.exp(a_log_f.astype(f32)) * jax.nn.softplus(a_f.astype(f32) + dt_bias_f.astype(f32))
    g_b = -jnp.exp(a_log_b.astype(f32)) * jax.nn.softplus(a_b.astype(f32) + dt_bias_b.astype(f32))
    beta_f = jax.nn.sigmoid(b_f.astype(f32))
    beta_b = jax.nn.sigmoid(b_b.astype(f32))
    o_fwd = gated_delta_chunked(q_a, k_a, v_a, g_f, beta_f)
    flip = lambda t: jnp.flip(t, axis=1)
    o_bwd = flip(gated_delta_chunked(flip(q_a), flip(k_a), flip(v_a), flip(g_b), flip(beta_b)))
    o_a = rmsnorm((o_fwd + o_bwd).astype(x.dtype), o_norm_a).reshape(b, s, W_A)
    o_a = o_a * jax.nn.silu(gate_a)

    cq = rmsnorm(q_lat, q_a_norm)
    qh = (cq @ w_q_b).reshape(b, s, H_B, D_NOPE + D_ROPE)
    ckv = rmsnorm(kv_lat, kv_a_norm)
    kvh = (ckv @ w_kv_b).reshape(b, s, H_B, D_NOPE + DV_B)
    pos = jnp.arange(s, dtype=f32)
    inv_freq = ROPE_THETA ** (-jnp.arange(0, D_ROPE, 2, dtype=f32) / D_ROPE)
    ang = pos[:, None] * inv_freq[None, :]
    cos = jnp.cos(ang)[:, None, :].astype(x.dtype)
    sin = jnp.sin(ang)[:, None, :].astype(x.dtype)
    q_pe = rotary(qh[..., D_NOPE:], cos, sin)
    k_pe = rotary(k_rope.reshape(b, s, 1, D_ROPE), cos, sin)
    q_full = jnp.concatenate([qh[..., :D_NOPE], q_pe], axis=-1)
    k_full = jnp.concatenate([kvh[..., :D_NOPE], jnp.broadcast_to(k_pe, (b, s, H_B, D_ROPE))], axis=-1)
    v_mla = kvh[..., D_NOPE:]
    o_b = latent_attention_blocks(q_full, k_full, v_mla) * jax.nn.silu(gate_b)

    m = jax.nn.sigmoid(gm_a) * (o_a @ w_pa) + jax.nn.sigmoid(gm_b) * (o_b @ w_pb)
    return x + m @ w_out


def setup_inputs(seed: int = 0) -> dict:
    key = jax.random.key(seed)
    ks = jax.random.split(key, 24)
    nrm = lambda k, shape, fan: jax.random.normal(k, shape, jnp.float32) * (fan ** -0.5)
    gain = lambda k, shape: 1.0 + 0.02 * jax.random.normal(k, shape, jnp.float32)

    def dt_bias(k):
        dt = jnp.exp(jax.random.uniform(k, (DEPTH, H_A), jnp.float32) * (math.log(0.1) - math.log(0.001)) + math.log(0.001))
        return dt + jnp.log(-jnp.expm1(-dt))

    def a_log(k):
        return jnp.log(jax.random.uniform(k, (DEPTH, H_A), jnp.float32, 1.0, 16.0))

    return {
        'x_prompt': jax.random.normal(ks[0], (BATCH, SEQ, D_MODEL), jnp.float32),
        'x_sample': jax.random.normal(ks[1], (DEC_BATCH, DEC_SEQ, D_MODEL), jnp.float32),
        'norm_in': gain(ks[2], (DEPTH, D_MODEL)),
        'w_in': nrm(ks[3], (DEPTH, D_MODEL, N_IN), D_MODEL),
        'conv_w': nrm(ks[4], (DEPTH, KCONV, CONV_DIM), KCONV),
        'a_log_f': a_log(ks[5]),
        'dt_bias_f': dt_bias(ks[6]),
        'a_log_b': a_log(ks[7]),
        'dt_bias_b': dt_bias(ks[8]),
        'o_norm_a': gain(ks[9], (DEPTH, DV_A)),
        'q_a_norm': gain(ks[10], (DEPTH, Q_LORA)),
        'w_q_b': nrm(ks[11], (DEPTH, Q_LORA, H_B * (D_NOPE + D_ROPE)), Q_LORA),
        'kv_a_norm': gain(ks[12], (DEPTH, KV_LORA)),
        'w_kv_b': nrm(ks[13], (DEPTH, KV_LORA, H_B * (D_NOPE + DV_B)), KV_LORA),
        'w_pa': nrm(ks[14], (DEPTH, W_A, D_MODEL), W_A),
        'w_pb': nrm(ks[15], (DEPTH, W_B, D_MODEL), W_B),
        'w_out': nrm(ks[16], (DEPTH, D_MODEL, D_MODEL), D_MODEL),
        'norm_f': gain(ks[17], (D_MODEL,)),
    }


def reference(x_prompt, x_sample, norm_in, w_in, conv_w, a_log_f, dt_bias_f, a_log_b, dt_bias_b, o_norm_a,
              q_a_norm, w_q_b, kv_a_norm, w_kv_b, w_pa, w_pb, w_out, norm_f):
    def trunk(x):
        for l in range(DEPTH):
            x = encoder_layer(x, norm_in[l], w_in[l], conv_w[l], a_log_f[l], dt_bias_f[l], a_log_b[l], dt_bias_b[l],
                              o_norm_a[l], q_a_norm[l], w_q_b[l], kv_a_norm[l], w_kv_b[l], w_pa[l], w_pb[l], w_out[l])
        return rmsnorm(x, norm_f)

    y_prompt = trunk(x_prompt)
    y_sample = trunk(x_sample)
    return (y_prompt, y_sample)
```

```python
import math
from contextlib import ExitStack

import numpy as np
import ml_dtypes
import concourse.bass as bass
import concourse.mybir as mybir
from concourse.bass_utils import run_bass_kernel_spmd

F32 = mybir.dt.float32
BF16 = mybir.dt.bfloat16
AF = mybir.ActivationFunctionType
ALU = mybir.AluOpType
AX = mybir.AxisListType

D_MODEL = 2048
H = 8
DK = 128
CONV_DIM = 3072
KCONV = 5
Q_LORA = 1536
KV_LORA = 512
D_ROPE = 64
D_NOPE = 128
N_IN = 11360
EPS = 1e-6
LMAX = 4096
BIG = 30000.0

C_QKV = 0
C_GA = 3072
C_SM = 4096
C_QL = 4128
C_KVL = 5664
C_KR = 6176
C_GB = 6240
C_GMA = 7264
C_GMB = 9312

ENGS = ("sp", "act", "pe", "dve", "pool")
EPOCH = 12000


class Buf:
    __slots__ = ("t", "lw", "rd", "dsem", "dcnt", "name", "chain")

    def __init__(self, t, name):
        self.t = t
        self.name = name
        self.lw = None
        self.rd = []
        self.dsem = None
        self.dcnt = 0
        self.chain = None

    def __getitem__(self, idx):
        return self.t[idx]


class Ring:
    def __init__(self, bufs):
        self.bufs = bufs
        self.i = 0

    def next(self):
        b = self.bufs[self.i % len(self.bufs)]
        self.i += 1
        return b


class Prog:
    def __init__(self, nc):
        self.nc = nc
        self.ops = {e: [] for e in ENGS}
        self.cnt = {e: 0 for e in ENGS}
        self.sems = {}
        self.seen = {e: {} for e in ENGS}
        self.dma_bufs = []
        self.nbuf = 0
        self.stack = None
        self.ndsem = 0
        self.free_dsems = {"d": [], "w": []}
        self.mute = False
        self.phase_no = 0
        self.only = 0
        self.mute_set = set()

    def sb(self, shape, dtype=F32, name=None):
        self.nbuf += 1
        name = name or f"sb{self.nbuf}"
        t = self.stack.enter_context(self.nc.sbuf_tensor(name, list(shape), dtype))
        return Buf(t, name)

    def ps(self, shape, dtype=F32, name=None):
        self.nbuf += 1
        name = name or f"ps{self.nbuf}"
        t = self.stack.enter_context(self.nc.psum_tensor(name, list(shape), dtype))
        return Buf(t, name)

    def ring(self, n, shape, dtype=F32, psum=False):
        return Ring([(self.ps if psum else self.sb)(shape, dtype) for _ in range(n)])

    def _sem(self, key):
        if key not in self.sems:
            self.sems[key] = self.nc.alloc_semaphore("s_" + "_".join(str(k) for k in key))
        return self.sems[key]

    def _engkey(self, e, n):
        ep = (n - 1) // EPOCH
        return (("e", e, ep), n - ep * EPOCH)

    def _deps(self, eng, reads, writes, skip_key=None, is_dma=False):
        deps = {}

        def add(ev, kind):
            if ev is None:
                return
            key, val, src = ev
            if kind == "waw" and skip_key is not None and key == skip_key:
                return
            if src == eng and not is_dma:
                if eng == "pe":
                    return
                if kind != "raw":
                    return
            if deps.get(key, 0) < val:
                deps[key] = val
        for b in reads:
            add(b.lw, "raw")
        for b in writes:
            add(b.lw, "waw")
            for r in b.rd:
                add(r, "war")
        return deps

    def _prune(self, eng, deps):
        waits = []
        seen = self.seen[eng]
        for key, val in deps.items():
            if seen.get(key, 0) >= val:
                continue
            seen[key] = val
            waits.append((key, val))
        return waits

    def _collect(self, eng, reads, writes):
        return self._prune(eng, self._deps(eng, reads, writes))

    def op(self, eng, meth, reads, writes, *args, **kw):
        if self.mute:
            return None
        waits = self._collect(eng, reads, writes)
        self.cnt[eng] += 1
        key, val = self._engkey(eng, self.cnt[eng])
        self._sem(key)
        self.ops[eng].append((waits, meth, args, kw, (key, 1)))
        ev = (key, val, eng)
        for b in writes:
            b.lw = ev
            b.rd = []
            b.chain = None
        for b in reads:
            if b not in writes:
                b.rd.append(ev)
        return ev

    def dma(self, eng, out_ap, in_ap, reads=(), writes=(), owner=None, **kw):
        if self.mute:
            return None
        if owner is None:
            owner = (list(writes) + list(reads))[0]
        kind = "w" if eng == "pool" else "d"
        st = owner.dsem
        if st is None:
            st = owner.dsem = {}
        if kind not in st:
            fl = self.free_dsems[kind]
            if fl:
                st[kind] = list(fl.pop())
            else:
                self.ndsem += 1
                key = (kind, self.ndsem)
                self._sem(key)
                st[kind] = [key, 0]
            self.dma_bufs.append((owner, kind))
        deps = self._deps(eng, reads, writes, skip_key=st[kind][0], is_dma=True)
        for b in writes:
            if b.lw is not None and b.lw[0] == st[kind][0] and b.chain:
                for k_, v_ in b.chain.items():
                    if deps.get(k_, 0) < v_:
                        deps[k_] = v_
        for b in writes:
            b.chain = dict(deps)
        waits = self._prune(eng, deps)
        st[kind][1] += 16
        key = st[kind][0]
        ev = (key, st[kind][1], "dma")
        kw = dict(kw)
        kw["out"] = out_ap
        kw["in_"] = in_ap
        self.ops[eng].append((waits, "dma_start", (), kw, (key, 16)))
        for b in writes:
            b.lw = ev
            b.rd = []
        for b in reads:
            b.rd.append(ev)
        return ev

    def barrier(self):
        evs = []
        for e in ENGS:
            if self.cnt[e] > 0:
                k, v = self._engkey(e, self.cnt[e])
                evs.append((k, v))
        for b, kind in self.dma_bufs:
            evs.append(tuple(b.dsem[kind]))
        for e in ENGS:
            waits = []
            for k, v in evs:
                if k[0] == "e" and k[1] == e:
                    continue
                if self.seen[e].get(k, 0) >= v:
                    continue
                self.seen[e][k] = v
                waits.append((k, v))
            if waits:
                self.ops[e].append((waits, None, None, None, None))

    def flush(self):
        nc = self.nc
        engobj = {"sp": "sync", "act": "scalar", "pe": "tensor", "dve": "vector", "pool": "gpsimd"}
        with nc.Block() as block:
            for e in ENGS:
                lst = self.ops[e]

                def body(engine, lst=lst):
                    for waits, meth, args, kw, inc in lst:
                        for k, v in waits:
                            engine.wait_ge(self.sems[k], v)
                        if meth is not None:
                            ins = getattr(engine, meth)(*args, **kw)
                            ins.then_inc(self.sems[inc[0]], inc[1])
                getattr(block, engobj[e])(body)
        self.ops = {e: [] for e in ENGS}

    def begin(self):
        self.stack = ExitStack()
        self.phase_no += 1
        self.mute = (bool(self.only) and self.phase_no != self.only) or (self.phase_no in self.mute_set)

    def end(self):
        self.barrier()
        self.flush()
        for b, kind in self.dma_bufs:
            self.free_dsems[kind].append(tuple(b.dsem[kind]))
        self.dma_bufs = []
        self.stack.close()
        self.stack = None


def host_consts():
    c = {}
    c["ident"] = np.eye(128, dtype=np.float32)
    j = np.arange(128)[:, None]
    i = np.arange(128)[None, :]
    tri = np.zeros((2, 128, 128), np.float32)
    tri[0] = (j <= i)
    tri[1] = (j >= i)
    c["tri"] = tri
    nm = np.zeros((2, 128, 128), np.float32)
    nm[0] = np.where(i >= j, 0.0, -BIG)
    nm[1] = np.where(i <= j, 0.0, -BIG)
    c["negmask"] = nm
    lv = np.zeros((2, 128, 7, 128), np.float32)
    for si, s in enumerate([1, 2, 4, 8, 16, 32, 64]):
        same2 = (i // (2 * s)) == (j // (2 * s))
        diff1 = (i // s) != (j // s)
        lv[0, :, si, :] = -1.0 * (same2 & diff1 & (i > j))
        lv[1, :, si, :] = -1.0 * (same2 & diff1 & (i < j))
    c["lvl"] = lv
    pos = np.arange(LMAX, dtype=np.float32)
    inv_freq = (10000.0 ** (-np.arange(0, D_ROPE, 2, dtype=np.float32) / D_ROPE)).astype(np.float32)
    ang = pos[None, :] * inv_freq[:, None]
    c["cos4"] = np.tile(np.cos(ang).astype(np.float32), (4, 1))
    c["sin4"] = np.tile(np.sin(ang).astype(np.float32), (4, 1))
    return c


def build(L, dbg=(), _STOP=0):
    nc = bass.Bass("TRN2", target_bir_lowering=False)
    NT = L // 128
    NB = L // 512
    PART = min(L, 2048)
    NPART = L // PART

    def din(name, shape, dt=F32):
        return nc.dram_tensor(name, list(shape), dt, kind="ExternalInput").ap()

    def dscr(name, shape, dt=F32):
        kind = "ExternalOutput" if name in dbg else "Internal"
        return nc.dram_tensor(name, list(shape), dt, kind=kind).ap()

    x = din("x", [L, D_MODEL])
    norm_in = din("norm_in", [D_MODEL])
    w_in = din("w_in", [D_MODEL, N_IN])
    conv_w = din("conv_w", [KCONV * 24, 128])
    a_log = din("a_log", [16])
    dt_bias = din("dt_bias", [16])
    o_norm_a = din("o_norm_a", [128])
    q_a_norm = din("q_a_norm", [Q_LORA])
    w_q_b = din("w_q_b", [Q_LORA, 1536])
    kv_a_norm = din("kv_a_norm", [KV_LORA])
    w_kv_b = din("w_kv_b", [KV_LORA, 2048])
    w_pa = din("w_pa", [1024, D_MODEL])
    w_pb = din("w_pb", [1024, D_MODEL])
    w_out = din("w_out", [D_MODEL, D_MODEL])
    norm_f = din("norm_f", [D_MODEL])
    kbias = din("kbias", [128, LMAX // 128])
    tmask_d = din("tmask", [128, LMAX])
    c_ident = din("ident", [128, 128])
    c_tri = din("tri", [2, 128, 128])
    c_negmask = din("negmask", [2, 128, 128])
    c_lvl = din("lvl", [2, 128, 7, 128])
    c_cos4 = din("cos4", [128, LMAX])
    c_sin4 = din("sin4", [128, LMAX])

    y = nc.dram_tensor("y", [L, D_MODEL], F32, kind="ExternalOutput").ap()

    QF_T = dscr("QF_T", [8 * 192, L], BF16)
    KN_T = dscr("KN_T", [1024, L], BF16)
    KPE_T = dscr("KPE_T", [64, L], BF16)
    VB = dscr("VB", [L, 1024], BF16)
    OB_T = dscr("OB_T", [1024, L], BF16)
    QKV_T = dscr("QKV_T", [CONV_DIM, L], BF16)
    GA_T = dscr("GA_T", [1024, L], BF16)
    GB_T = dscr("GB_T", [1024, L], BF16)
    GMA_T = dscr("GMA_T", [2048, L], BF16)
    GMB_T = dscr("GMB_T", [2048, L], BF16)
    QL_T = dscr("QL_T", [Q_LORA, L])
    KVL_T = dscr("KVL_T", [KV_LORA, L])
    KR_T = dscr("KR_T", [64, L])
    AB = dscr("AB", [L, 32])
    QT = dscr("QT", [1024, L], BF16)
    KT = dscr("KT", [1024, L], BF16)
    VT = dscr("VT", [1024, L], BF16)
    OA_T = dscr("OA_T", [1024, L], BF16)

    P = Prog(nc)

    P.begin()
    identf = P.sb([128, 128], F32)
    identb = P.sb([128, 128], BF16)
    gbc = P.sb([128, D_MODEL], F32)
    P.dma("sp", identf[:], c_ident, writes=[identf])
    P.dma("sp", gbc[:], norm_in.partition_broadcast(128), writes=[gbc])
    P.op("dve", "tensor_copy", [identf], [identb], out=identb[:], in_=identf[:])
    xnT = P.sb([128, 16, PART], BF16)
    xring = P.ring(2, [128, D_MODEL], F32)
    xsring = P.ring(2, [128, D_MODEL], BF16)
    junk = P.sb([128, D_MODEL], BF16)
    stat = P.ring(4, [128, 4], F32)
    ptr = P.ring(2, [128, 8, 128], BF16, psum=True)
    pmm = P.ring(4, [128, 512], F32, psum=True)
    wring = P.ring(2, [128, 16, 512], BF16)
    oring = P.ring(3, [128, PART], F32)
    oringb = P.ring(2, [128, PART], BF16)
    wsm = P.sb([128, 16, 32], BF16)
    absg = P.sb([128, PART // 128, 32], F32)
    P.dma("pool", wsm[:], w_in[:, C_SM:C_SM + 32].rearrange("(c p) n -> p c n", p=128), writes=[wsm])

    groups = []

    def add_group(c0, n, dest, func):
        o = 0
        while o < n:
            w = min(512, n - o)
            groups.append((c0 + o, w, dest, o, func))
            o += w
    add_group(C_QKV, 3072, QKV_T, None)
    add_group(C_GA, 1024, GA_T, AF.Silu)
    add_group(C_QL, Q_LORA, QL_T, None)
    add_group(C_KVL, KV_LORA, KVL_T, None)
    add_group(C_KR, 64, KR_T, None)
    add_group(C_GB, 1024, GB_T, AF.Silu)
    add_group(C_GMA, 2048, GMA_T, AF.Sigmoid)
    add_group(C_GMB, 2048, GMB_T, AF.Sigmoid)
    groups.sort(key=lambda g: {None: 0, AF.Silu: 1, AF.Sigmoid: 2}[g[4]])

    evq = 0
    for part in range(NPART):
        t0 = part * PART
        for tt in range(PART // 128):
            xt = xring.next()
            P.dma("sp", xt[:], x[t0 + tt * 128: t0 + (tt + 1) * 128, :], writes=[xt])
            st = stat.next()
            P.op("act", "activation", [xt], [junk, st], out=junk[:], in_=xt[:], func=AF.Square, accum_out=st[:, 0:1])
            P.op("dve", "tensor_scalar", [st], [st], out=st[:, 1:2], in0=st[:, 0:1], scalar1=1.0 / D_MODEL, scalar2=EPS,
                 op0=ALU.mult, op1=ALU.add)
            P.op("act", "activation", [st], [st], out=st[:, 2:3], in_=st[:, 1:2], func=AF.Sqrt)
            P.op("dve", "reciprocal", [st], [st], out=st[:, 3:4], in_=st[:, 2:3])
            xs = xsring.next()
            P.op("dve", "scalar_tensor_tensor", [xt, st, gbc], [xs], out=xs[:], in0=xt[:], scalar=st[:, 3:4], in1=gbc[:],
                 op0=ALU.mult, op1=ALU.mult)
            for hh in range(2):
                pt = ptr.next()
                for c in range(8):
                    cc = hh * 8 + c
                    P.op("pe", "transpose", [xs, identb], [pt], out=pt[:, c, :], in_=xs[:, cc * 128:(cc + 1) * 128],
                         identity=identb[:])
                if hh == 0:
                    P.op("act", "copy", [pt], [xnT], out=xnT[:, 0:8, tt * 128:(tt + 1) * 128], in_=pt[:])
                else:
                    P.op("dve", "tensor_copy", [pt], [xnT], out=xnT[:, 8:16, tt * 128:(tt + 1) * 128], in_=pt[:])
        for tt in range(PART // 128):
            pm = pmm.next()
            for c in range(16):
                P.op("pe", "matmul", [xnT, wsm], [pm], pm[:, 0:32], lhsT=xnT[:, c, tt * 128:(tt + 1) * 128], rhs=wsm[:, c, :],
                     start=(c == 0), stop=(c == 15))
            P.op("dve", "tensor_copy", [pm], [absg], out=absg[:, tt, :], in_=pm[:, 0:32])
        for a0 in range(0, PART // 128, 8):
            a1 = min(PART // 128, a0 + 8)
            P.dma("sp", AB[t0 + a0 * 128:t0 + a1 * 128, :].rearrange("(t p) c -> p t c", p=128), absg[:, a0:a1, :], reads=[absg])
        for (c0, ncol, dest, r0, func) in groups:
            wt = wring.next()
            P.dma("pool", wt[:, :, 0:ncol], w_in[:, c0:c0 + ncol].rearrange("(c p) n -> p c n", p=128), writes=[wt])
            for ct in range((ncol + 127) // 128):
                m = min(128, ncol - ct * 128)
                ot = oringb.next() if (func is not None or dest is QKV_T) else oring.next()
                for tb in range(PART // 512):
                    pm = pmm.next()
                    for c in range(16):
                        P.op("pe", "matmul", [xnT, wt], [pm], pm[0:m, :], lhsT=wt[:, c, ct * 128:ct * 128 + m],
                             rhs=xnT[:, c, tb * 512:(tb + 1) * 512], start=(c == 0), stop=(c == 15))
                    if func is not None:
                        P.op("act", "activation", [pm], [ot], out=ot[0:m, tb * 512:(tb + 1) * 512], in_=pm[0:m, :], func=func)
                    else:
                        evq += 1
                        if evq % 2 == 0:
                            P.op("act", "copy", [pm], [ot], out=ot[0:m, tb * 512:(tb + 1) * 512], in_=pm[0:m, :])
                        else:
                            P.op("dve", "tensor_copy", [pm], [ot], out=ot[0:m, tb * 512:(tb + 1) * 512], in_=pm[0:m, :])
                rr = r0 + ct * 128
                P.dma("sp", dest[rr:rr + m, t0:t0 + PART], ot[0:m, :], reads=[ot])
    P.end()
    if _STOP == 1:
        nc._P = P
        return nc

    P.begin()
    identf = P.sb([128, 128], F32)
    P.dma("sp", identf[:], c_ident, writes=[identf])
    onesb = P.sb([128, 128], BF16)
    P.op("dve", "memset", [], [onesb], onesb[:], 1.0)
    cwr = P.sb([120, 128], F32)
    P.dma("sp", cwr[:], conv_w, writes=[cwr])
    cw = P.sb([128, 120], F32)
    pmm = P.ring(4, [128, 512], F32, psum=True)
    pm = pmm.next()
    P.op("pe", "matmul", [cwr, identf], [pm], pm[:, 0:120], lhsT=cwr[:], rhs=identf[0:120, 0:120], start=True, stop=True)
    P.op("dve", "tensor_copy", [pm], [cw], out=cw[:], in_=pm[:, 0:120])
    tmask = P.sb([128, L], F32)
    P.dma("act", tmask[:], tmask_d[:, 0:L], writes=[tmask])
    epst = P.sb([128, 2], F32)
    P.op("dve", "memset", [], [epst], epst[:, 0:1], EPS)
    P.op("dve", "memset", [], [epst], epst[:, 1:2], EPS * DK)
    identb2 = P.sb([128, 128], BF16)
    P.op("dve", "tensor_copy", [identf], [identb2], out=identb2[:], in_=identf[:])
    xpr = P.ring(2, [128, L + 4], BF16)
    for b in xpr.bufs:
        P.op("pool", "memset", [], [b], b[:, 0:2], 0.0)
        P.op("pool", "memset", [], [b], b[:, L + 2:L + 4], 0.0)
    dgw_r = P.ring(2, [128, KCONV, 128], BF16)
    slr = P.ring(2, [128, L], F32)
    sqr = P.ring(2, [128, L], BF16)
    outr = P.ring(2, [128, L], BF16)
    rnr = P.ring(3, [128, 512], F32)
    cw3 = cw[:].rearrange("p (k c) -> p k c", c=24)
    for cc in range(24):
        kind = cc // 8
        xp = xpr.next()
        P.dma("sp", xp[:, 2:L + 2], QKV_T[cc * 128:(cc + 1) * 128, :], writes=[xp])
        dgw = dgw_r.next()
        P.op("dve", "tensor_tensor", [identb2, cw], [dgw], out=dgw[:], in0=identb2[:].unsqueeze(1).broadcast_to([128, KCONV, 128]),
             in1=cw3[:, :, cc:cc + 1].broadcast_to([128, KCONV, 128]), op=ALU.mult)
        sl = slr.next()
        for tb in range(NB):
            pc = pmm.next()
            for k in range(KCONV):
                P.op("pe", "matmul", [dgw, xp], [pc], pc[:], lhsT=dgw[:, k, :], rhs=xp[:, tb * 512 + k:tb * 512 + k + 512],
                     start=(k == 0), stop=(k == KCONV - 1))
            P.op("act", "activation", [pc], [sl], out=sl[:, tb * 512:(tb + 1) * 512], in_=pc[:], func=AF.Silu)
        P.op("dve", "tensor_tensor", [sl, tmask], [sl], out=sl[:], in0=sl[:], in1=tmask[:], op=ALU.mult)
        ob = outr.next()
        if kind == 2:
            P.op("pool", "tensor_copy", [sl], [ob], out=ob[:], in_=sl[:])
            P.dma("pool", VT[(cc - 16) * 128:(cc - 15) * 128, :], ob[:], reads=[ob])
        else:
            sq = sqr.next()
            P.op("pool", "tensor_tensor", [sl], [sq], out=sq[:], in0=sl[:], in1=sl[:], op=ALU.mult)
            for tb in range(NB):
                pm = pmm.next()
                P.op("pe", "matmul", [onesb, sq], [pm], pm[:], lhsT=onesb[:], rhs=sq[:, tb * 512:(tb + 1) * 512], start=True, stop=True)
                rn = rnr.next()
                if kind == 0:
                    P.op("act", "activation", [pm, epst], [rn], out=rn[:], in_=pm[:], func=AF.Sqrt, bias=epst[:, 1:2], scale=float(DK))
                else:
                    P.op("act", "activation", [pm, epst], [rn], out=rn[:], in_=pm[:], func=AF.Sqrt, bias=epst[:, 0:1])
                P.op("dve", "reciprocal", [rn], [rn], out=rn[:], in_=rn[:])
                P.op("dve", "tensor_tensor", [sl, rn], [ob], out=ob[:, tb * 512:(tb + 1) * 512],
                     in0=sl[:, tb * 512:(tb + 1) * 512], in1=rn[:], op=ALU.mult)
            dst = QT if kind == 0 else KT
            hh = cc % 8
            P.dma("pool", dst[hh * 128:(hh + 1) * 128, :], ob[:], reads=[ob])
    P.end()
    if _STOP == 2:
        nc._P = P
        return nc


    P.begin()
    identf = P.sb([128, 128], F32)
    P.dma("sp", identf[:], c_ident, writes=[identf])
    onesb = P.sb([128, 128], BF16)
    P.op("dve", "memset", [], [onesb], onesb[:], 1.0)
    wq = P.sb([128, 12, 1536], BF16)
    P.dma("pool", wq[:], w_q_b.rearrange("(c p) n -> p c n", p=128), writes=[wq])
    wqt = P.sb([128, 2, 12, 256], BF16)
    wq4 = wq[:].rearrange("p c (h r) -> p c h r", r=192)
    P.op("dve", "tensor_copy", [wq], [wqt], out=wqt[:, 0, :, :].rearrange("p c (h r) -> p c h r", r=32), in_=wq4[:, :, :, 128:160])
    P.op("pool", "tensor_copy", [wq], [wqt], out=wqt[:, 1, :, :].rearrange("p c (h r) -> p c h r", r=32), in_=wq4[:, :, :, 160:192])
    wkv = P.sb([128, 4, 2048], BF16)
    P.dma("pool", wkv[:], w_kv_b.rearrange("(c p) n -> p c n", p=128), writes=[wkv])
    pm_r = P.ring(6, [128, 512], F32, psum=True)
    nrm_rows = P.sb([16, 128], F32)
    P.dma("sp", nrm_rows[0:12, :], q_a_norm.rearrange("(c p) -> c p", p=128), writes=[nrm_rows])
    P.dma("sp", nrm_rows[12:16, :], kv_a_norm.rearrange("(c p) -> c p", p=128), writes=[nrm_rows])
    nrm = P.sb([128, 16], F32)
    pm = pm_r.next()
    P.op("pe", "matmul", [nrm_rows, identf], [pm], pm[:, 0:16], lhsT=nrm_rows[:], rhs=identf[0:16, 0:16], start=True, stop=True)
    P.op("dve", "tensor_copy", [pm], [nrm], out=nrm[:], in_=pm[:, 0:16])
    wkv3 = wkv[:].rearrange("p c (h r) -> p c h r", r=256)
    ql_r = P.ring(1, [128, 12, 512], F32)
    sq_r = P.ring(1, [128, 12, 512], BF16)
    cq_r = P.ring(2, [128, 12, 512], BF16)
    kvl_r = P.ring(2, [128, 4, 512], F32)
    ckv_r = P.ring(2, [128, 4, 512], BF16)
    rr_r = P.ring(2, [128, 512], F32)
    cs_r = P.ring(2, [128, 2, 512], F32)
    st_r = P.ring(4, [128, 512], BF16)
    tmp_r = P.ring(4, [128, 512], F32)
    vb_r = P.ring(2, [128, 8, 128], BF16)
    kr_r = P.ring(2, [32, 2, 512], F32)
    evq = 0
    for tb in range(NB):
        blk = slice(tb * 512, (tb + 1) * 512)
        cs = cs_r.next()
        P.dma("sp", cs[:, 0, :], c_cos4[:, blk], writes=[cs])
        P.dma("sp", cs[:, 1, :], c_sin4[:, blk], writes=[cs])

        def latent_norm(src, nch, ncol0, ring_in, ring_out, dim):
            lt = ring_in.next()
            P.dma("act", lt[:], src.rearrange("(c p) t -> p c t", p=128)[:, :, blk], writes=[lt])
            sq = sq_r.next()
            P.op("pool", "tensor_tensor", [lt], [sq], out=sq[:, 0:nch, :], in0=lt[:], in1=lt[:], op=ALU.mult)
            pss = pm_r.next()
            for c in range(nch):
                P.op("pe", "matmul", [onesb, sq], [pss], pss[:], lhsT=onesb[:], rhs=sq[:, c, :], start=(c == 0), stop=(c == nch - 1))
            rr = rr_r.next()
            P.op("dve", "tensor_scalar", [pss], [rr], out=rr[:], in0=pss[:], scalar1=1.0 / dim, scalar2=EPS, op0=ALU.mult, op1=ALU.add)
            P.op("act", "activation", [rr], [rr], out=rr[:], in_=rr[:], func=AF.Sqrt)
            P.op("dve", "reciprocal", [rr], [rr], out=rr[:], in_=rr[:])
            ct_ = ring_out.next()
            for c in range(nch):
                P.op("dve", "scalar_tensor_tensor", [lt, nrm, rr], [ct_], out=ct_[:, c, :], in0=lt[:, c, :],
                     scalar=nrm[:, ncol0 + c:ncol0 + c + 1], in1=rr[:], op0=ALU.mult, op1=ALU.mult)
            return ct_

        def rope(pT1, pT2, np_, dst1, dst2):
            a, b, c_, d_ = tmp_r.next(), tmp_r.next(), tmp_r.next(), tmp_r.next()
            P.op("dve", "tensor_tensor", [pT1[0], cs], [a], out=a[0:np_, :], in0=pT1[1], in1=cs[0:np_, 0, :], op=ALU.mult)
            P.op("dve", "tensor_tensor", [pT2[0], cs], [b], out=b[0:np_, :], in0=pT2[1], in1=cs[0:np_, 1, :], op=ALU.mult)
            P.op("dve", "tensor_tensor", [pT1[0], cs], [c_], out=c_[0:np_, :], in0=pT1[1], in1=cs[0:np_, 1, :], op=ALU.mult)
            P.op("dve", "tensor_tensor", [pT2[0], cs], [d_], out=d_[0:np_, :], in0=pT2[1], in1=cs[0:np_, 0, :], op=ALU.mult)
            o1, o2 = st_r.next(), st_r.next()
            P.op("pool", "tensor_tensor", [a, b], [o1], out=o1[0:np_, :], in0=a[0:np_, :], in1=b[0:np_, :], op=ALU.subtract)
            P.op("pool", "tensor_tensor", [c_, d_], [o2], out=o2[0:np_, :], in0=c_[0:np_, :], in1=d_[0:np_, :], op=ALU.add)
            for (o, dst) in ((o1, dst1), (o2, dst2)):
                for (psl, dap) in dst:
                    P.dma("sp", dap, o[psl, :], reads=[o])

        cq = latent_norm(QL_T, 12, 0, ql_r, cq_r, Q_LORA)
        for h in range(8):
            pn = pm_r.next()
            for c in range(12):
                P.op("pe", "matmul", [wq, cq], [pn], pn[:], lhsT=wq[:, c, h * 192:h * 192 + 128], rhs=cq[:, c, :], start=(c == 0), stop=(c == 11))
            qn = st_r.next()
            evq += 1
            if evq % 2:
                P.op("act", "copy", [pn], [qn], out=qn[:], in_=pn[:])
            else:
                P.op("dve", "tensor_copy", [pn], [qn], out=qn[:], in_=pn[:])
            P.dma("pool", QF_T[h * 192:h * 192 + 128, blk], qn[:], reads=[qn])
        for g in range(2):
            pT1, pT2 = pm_r.next(), pm_r.next()
            for (pt_, half) in ((pT1, 0), (pT2, 1)):
                for c in range(12):
                    P.op("pe", "matmul", [wqt, cq], [pt_], pt_[:], lhsT=wqt[:, half, c, g * 128:(g + 1) * 128], rhs=cq[:, c, :],
                         start=(c == 0), stop=(c == 11))
            QF3 = QF_T.rearrange("(h r) t -> h r t", r=192)
            dst1 = [(slice(32 * hh, 32 * hh + 32), QF3[4 * g + hh, 128:160, blk]) for hh in range(4)]
            dst2 = [(slice(32 * hh, 32 * hh + 32), QF3[4 * g + hh, 160:192, blk]) for hh in range(4)]
            rope((pT1, pT1[:]), (pT2, pT2[:]), 128, dst1, dst2)
        ckv = latent_norm(KVL_T, 4, 12, kvl_r, ckv_r, KV_LORA)
        for h in range(8):
            pn = pm_r.next()
            for c in range(4):
                P.op("pe", "matmul", [wkv, ckv], [pn], pn[:], lhsT=wkv[:, c, h * 256:h * 256 + 128], rhs=ckv[:, c, :], start=(c == 0), stop=(c == 3))
            kn = st_r.next()
            evq += 1
            if evq % 2:
                P.op("act", "copy", [pn], [kn], out=kn[:], in_=pn[:])
            else:
                P.op("dve", "tensor_copy", [pn], [kn], out=kn[:], in_=pn[:])
            P.dma("pool", KN_T[h * 128:(h + 1) * 128, blk], kn[:], reads=[kn])
        for st in range(4):
            vb = vb_r.next()
            for g in range(2):
                pv_ = pm_r.next()
                for c in range(4):
                    P.op("pe", "matmul", [ckv, wkv], [pv_], pv_[:].rearrange("p (h e) -> p h e", e=128), lhsT=ckv[:, c, st * 128:(st + 1) * 128],
                         rhs=wkv3[:, c, 4 * g:4 * g + 4, 128:256], start=(c == 0), stop=(c == 3))
                if g == 0:
                    P.op("act", "copy", [pv_], [vb], out=vb[:, 0:4, :], in_=pv_[:].rearrange("p (h e) -> p h e", e=128))
                else:
                    P.op("dve", "tensor_copy", [pv_], [vb], out=vb[:, 4:8, :], in_=pv_[:].rearrange("p (h e) -> p h e", e=128))
            r0 = tb * 512 + st * 128
            P.dma("pool", VB[r0:r0 + 128, :].rearrange("p (h e) -> p h e", e=128), vb[:], reads=[vb])
        kr = kr_r.next()
        P.dma("act", kr[:, 0, :], KR_T[0:32, blk], writes=[kr])
        P.dma("act", kr[:, 1, :], KR_T[32:64, blk], writes=[kr])
        rope((kr, kr[:, 0, :]), (kr, kr[:, 1, :]), 32, [(slice(0, 32), KPE_T[0:32, blk])], [(slice(0, 32), KPE_T[32:64, blk])])
    P.end()
    if _STOP == 4:
        nc._P = P
        return nc

    OF = dscr("OF", [L, 1024])
    P.begin()
    identf = P.sb([128, 128], F32)
    identb = P.sb([128, 128], BF16)
    onesf = P.sb([128, 128], F32)
    P.dma("sp", identf[:], c_ident, writes=[identf])
    P.op("dve", "tensor_copy", [identf], [identb], out=identb[:], in_=identf[:])
    P.op("dve", "memset", [], [onesf], onesf[:], 1.0)
    tri = P.sb([128, 2, 128], F32)
    P.dma("sp", tri[:], c_tri.rearrange("d p i -> p d i"), writes=[tri])
    negm = P.sb([128, 2, 128], F32)
    P.dma("sp", negm[:], c_negmask.rearrange("d p i -> p d i"), writes=[negm])
    lvl = P.sb([128, 2, 7, 128], BF16)
    P.dma("pool", lvl[:], c_lvl.rearrange("d p s i -> p d s i"), writes=[lvl])
    onorm = P.sb([128, 1], F32)
    P.dma("sp", onorm[:], o_norm_a.rearrange("(p o) -> p o", o=1), writes=[onorm])
    ABt = P.sb([128, NT, 32], F32)
    for t0_ in range(0, NT, 8):
        t1_ = min(NT, t0_ + 8)
        P.dma("sp", ABt[:, t0_:t1_, :], AB[t0_ * 128:t1_ * 128, :].rearrange("(t p) c -> p t c", p=128), writes=[ABt])
    dtb = P.sb([128, 16], F32)
    alg = P.sb([128, 16], F32)
    P.dma("sp", dtb[:], dt_bias.partition_broadcast(128), writes=[dtb])
    P.dma("sp", alg[:], a_log.partition_broadcast(128), writes=[alg])
    negA = P.sb([128, 16], F32)
    P.op("act", "activation", [alg], [negA], out=negA[:], in_=alg[:], func=AF.Exp)
    P.op("dve", "tensor_scalar", [negA], [negA], out=negA[:], in0=negA[:], scalar1=-1.0, scalar2=None, op0=ALU.mult)
    gt = P.sb([128, NT, 16], F32)
    beta = P.sb([128, NT, 16], F32)
    gcs = P.sb([128, NT, 16], F32)
    ngc = P.sb([128, NT, 16], F32)
    eg = P.sb([128, NT, 16], F32)
    kd = P.sb([128, NT, 16], F32)
    egl = P.sb([128, NT, 16], F32)
    pbf = P.ring(2, [128, 8, 128], BF16, psum=True)
    pf = P.ring(6, [128, 4, 128], F32, psum=True)

    def bc16(ap16):
        return ap16.unsqueeze(1).broadcast_to([128, NT, 16])
    P.op("dve", "tensor_tensor", [ABt, dtb], [gt], out=gt[:], in0=ABt[:, :, 0:16], in1=bc16(dtb[:]), op=ALU.add)
    P.op("act", "activation", [gt], [gt], out=gt[:], in_=gt[:], func=AF.Exp)
    P.op("dve", "tensor_scalar", [gt], [gt], out=gt[:], in0=gt[:], scalar1=1.0, scalar2=None, op0=ALU.add)
    P.op("act", "activation", [gt], [gt], out=gt[:], in_=gt[:], func=AF.Ln)
    P.op("dve", "tensor_tensor", [gt, negA], [gt], out=gt[:], in0=gt[:], in1=bc16(negA[:]), op=ALU.mult)
    P.op("act", "activation", [ABt], [beta], out=beta[:], in_=ABt[:, :, 16:32], func=AF.Exp, scale=-1.0)
    P.op("dve", "tensor_scalar", [beta], [beta], out=beta[:], in0=beta[:], scalar1=1.0, scalar2=None, op0=ALU.add)
    P.op("dve", "reciprocal", [beta], [beta], out=beta[:], in_=beta[:])
    pg = pf.next()
    pgv = pg[:].rearrange("p a b -> p (a b)")
    for d in range(2):
        P.op("pe", "matmul", [tri, gt], [pg], pgv[:, d * NT * 8:(d + 1) * NT * 8], lhsT=tri[:, d, :], rhs=gt[:, :, d * 8:(d + 1) * 8],
             start=True, stop=True)
    for d in range(2):
        P.op("dve", "tensor_copy", [pg], [gcs], out=gcs[:, :, d * 8:(d + 1) * 8],
             in_=pgv[:, d * NT * 8:(d + 1) * NT * 8].rearrange("p (t h) -> p t h", h=8))
    ptot = pf.next()
    ptv = ptot[:].rearrange("p a b -> p (a b)")
    gtv = gt[:].rearrange("p t c -> p (t c)")
    for c0_ in range(0, NT * 16, 256):
        c1_ = min(NT * 16, c0_ + 256)
        P.op("pe", "matmul", [onesf, gt], [ptot], ptv[:, c0_:c1_], lhsT=onesf[:], rhs=gtv[:, c0_:c1_], start=True, stop=True)
    ptv3 = ptv[:, 0:NT * 16].rearrange("p (t c) -> p t c", c=16)
    P.op("act", "activation", [ptot], [egl], out=egl[:], in_=ptv3, func=AF.Exp)
    P.op("dve", "tensor_tensor", [ptot, gcs], [kd], out=kd[:], in0=ptv3, in1=gcs[:], op=ALU.subtract)
    P.op("act", "activation", [kd], [kd], out=kd[:], in_=kd[:], func=AF.Exp)
    P.op("act", "activation", [gcs], [eg], out=eg[:], in_=gcs[:], func=AF.Exp)
    P.op("dve", "tensor_scalar", [gcs], [ngc], out=ngc[:], in0=gcs[:], scalar1=-1.0, scalar2=None, op0=ALU.mult)

    S32 = P.sb([128, 8, 128], F32)
    Sb = P.sb([128, 8, 128], BF16)
    kt_r = P.ring(2, [128, 8, 128], BF16)
    qt_r = P.ring(2, [128, 8, 128], BF16)
    vt_r = P.ring(2, [128, 8, 128], BF16)
    kdec_r = P.ring(2, [128, 8, 128], BF16)
    vtok_r = P.ring(2, [128, 8, 128], BF16)
    dg_r = P.ring(2, [128, 8, 128], F32)
    de_r = P.ring(2, [128, 8, 128], F32)
    E_r = P.ring(4, [128, 4, 128], F32)
    AT_r = P.ring(4, [128, 4, 128], BF16)
    attn_r = P.ring(4, [128, 4, 128], BF16)
    kgt_r = P.ring(4, [128, 4, 128], BF16)
    qgt_r = P.ring(4, [128, 4, 128], BF16)
    nat_r = P.ring(4, [128, 4, 7, 128], BF16)
    D_r = P.ring(4, [128, 4, 128], BF16)
    DT_r = P.ring(4, [128, 4, 128], BF16)
    p1_r = P.ring(4, [128, 4, 128], BF16)
    TT_r = P.ring(4, [128, 4, 128], BF16)
    R_r = P.ring(4, [128, 4, 128], BF16)
    VN_r = P.ring(4, [128, 4, 128], BF16)
    O_r = P.ring(2, [128, 8, 128], F32)
    of_r = P.ring(2, [128, 8, 128], F32)
    ga_r = P.ring(2, [128, 8, 128], BF16)
    osum_r = P.ring(2, [128, 4, 128], F32)
    osq_r = P.ring(2, [128, 4, 128], F32)
    on_r = P.ring(2, [128, 4, 128], F32)
    ost_r = P.ring(4, [128, 4, 4], F32)
    oa_r = P.ring(2, [128, 8, 128], BF16)
    identb_bc = identb[:].unsqueeze(1).broadcast_to([128, 4, 128])

    for d in range(2):
        P.op("dve", "memset", [], [S32], S32[:], 0.0)
        P.op("pool", "memset", [], [Sb], Sb[:], 0.0)
        order = range(NT) if d == 0 else range(NT - 1, -1, -1)
        d8 = d * 8
        def tile_gen(t):
            tok = slice(t * 128, (t + 1) * 128)
            KTt, QTt, VTt = kt_r.next(), qt_r.next(), vt_r.next()
            P.dma("sp", KTt[:], KT.rearrange("(h p) t -> p h t", p=128)[:, :, tok], writes=[KTt])
            P.dma("sp", QTt[:], QT.rearrange("(h p) t -> p h t", p=128)[:, :, tok], writes=[QTt])
            P.dma("sp", VTt[:], VT.rearrange("(h p) t -> p h t", p=128)[:, :, tok], writes=[VTt])
            if d == 1:
                OFt, GAt = of_r.next(), ga_r.next()
                P.dma("sp", OFt[:], OF[tok, :].rearrange("p (h e) -> p h e", h=8), writes=[OFt])
                P.dma("sp", GAt[:], GA_T.rearrange("(h p) t -> p h t", p=128)[:, :, tok], writes=[GAt])
                OAt = oa_r.next()
            else:
                Ot = O_r.next()
            pk, pv = pbf.next(), pbf.next()
            for h in range(8):
                P.op("pe", "transpose", [KTt, identb], [pk], out=pk[:, h, :], in_=KTt[:, h, :], identity=identb[:])
            for h in range(8):
                P.op("pe", "transpose", [VTt, identb], [pv], out=pv[:, h, :], in_=VTt[:, h, :], identity=identb[:])
            kdec, vtok = kdec_r.next(), vtok_r.next()
            P.op("dve", "tensor_tensor", [pk, kd], [kdec], out=kdec[:], in0=pk[:],
                 in1=kd[:, t, d8:d8 + 8].unsqueeze(2).broadcast_to([128, 8, 128]), op=ALU.mult)
            P.op("act", "copy", [pv], [vtok], out=vtok[:], in_=pv[:])
            dg, de = dg_r.next(), de_r.next()
            idbc8 = identf[:].unsqueeze(1).broadcast_to([128, 8, 128])
            P.op("pool", "tensor_tensor", [identf, gcs], [dg], out=dg[:], in0=idbc8,
                 in1=gcs[:, t, d8:d8 + 8].unsqueeze(2).broadcast_to([128, 8, 128]), op=ALU.mult)
            P.op("pool", "tensor_tensor", [identf, eg], [de], out=de[:], in0=idbc8,
                 in1=eg[:, t, d8:d8 + 8].unsqueeze(2).broadcast_to([128, 8, 128]), op=ALU.mult)
            GR = (0, 1)
            hsl = [range(G * 4, G * 4 + 4) for G in GR]
            gsls = [slice(G * 4, G * 4 + 4) for G in GR]
            st_ = [dict() for _ in GR]
            for G in GR:
                hs = hsl[G]
                pKK, pBC = pf.next(), pf.next()
                for hh, h in enumerate(hs):
                    P.op("pe", "matmul", [KTt], [pKK], pKK[:, hh, :], lhsT=KTt[:, h, :], rhs=KTt[:, h, :], start=True, stop=True)
                for hh, h in enumerate(hs):
                    P.op("pe", "matmul", [onesf, dg], [pBC], pBC[:, hh, :], lhsT=onesf[:], rhs=dg[:, h, :], start=True, stop=False)
                    P.op("pe", "matmul", [identf, negm], [pBC], pBC[:, hh, :], lhsT=identf[:], rhs=negm[:, d, :], start=False, stop=True)
                E = E_r.next()
                for hh, h in enumerate(hs):
                    P.op("act", "activation", [pBC, ngc], [E], out=E[:, hh, :], in_=pBC[:, hh, :], func=AF.Exp,
                         bias=ngc[:, t, d8 + h:d8 + h + 1])
                AT = AT_r.next()
                for hh, h in enumerate(hs):
                    P.op("dve", "scalar_tensor_tensor", [pKK, beta, E], [AT], out=AT[:, hh, :], in0=pKK[:, hh, :],
                         scalar=beta[:, t, d8 + h:d8 + h + 1], in1=E[:, hh, :], op0=ALU.mult, op1=ALU.mult)
                NAT = nat_r.next()
                P.op("pool", "tensor_tensor", [AT, lvl], [NAT], out=NAT[:],
                     in0=AT[:].unsqueeze(2).broadcast_to([128, 4, 7, 128]),
                     in1=lvl[:, d, :, :].unsqueeze(1).broadcast_to([128, 4, 7, 128]), op=ALU.mult)
                st_[G].update(E=E, NAT=NAT)
            for G in GR:
                hs = hsl[G]
                gsl = gsls[G]
                E = st_[G]["E"]
                pQK, pEG = pf.next(), pf.next()
                for hh, h in enumerate(hs):
                    P.op("pe", "matmul", [KTt, QTt], [pQK], pQK[:, hh, :], lhsT=KTt[:, h, :], rhs=QTt[:, h, :], start=True, stop=True)
                for hh, h in enumerate(hs):
                    P.op("pe", "matmul", [onesf, de], [pEG], pEG[:, hh, :], lhsT=onesf[:], rhs=de[:, h, :], start=True, stop=True)
                attnT, KGT, QGT = attn_r.next(), kgt_r.next(), qgt_r.next()
                P.op("dve", "tensor_tensor", [pQK, E], [attnT], out=attnT[:], in0=pQK[:], in1=E[:], op=ALU.mult)
                P.op("dve", "tensor_tensor", [KTt, pEG], [KGT], out=KGT[:], in0=KTt[:, gsl, :], in1=pEG[:], op=ALU.mult)
                P.op("dve", "tensor_tensor", [QTt, pEG], [QGT], out=QGT[:], in0=QTt[:, gsl, :], in1=pEG[:], op=ALU.mult)
                st_[G].update(attnT=attnT, KGT=KGT, QGT=QGT)
            for G in GR:
                NAT = st_[G]["NAT"]
                pP1 = pf.next()
                for hh in range(4):
                    P.op("pe", "matmul", [NAT, identb], [pP1], pP1[:, hh, :], lhsT=NAT[:, hh, 0, :], rhs=identb[:], start=True, stop=True)
                Dm, DT = D_r.next(), DT_r.next()
                P.op("dve", "tensor_tensor", [identb, pP1], [Dm], out=Dm[:], in0=identb_bc, in1=pP1[:], op=ALU.add)
                P.op("pool", "tensor_tensor", [identb, NAT], [DT], out=DT[:], in0=identb_bc, in1=NAT[:, :, 0, :], op=ALU.add)
                st_[G].update(Dm=Dm, DT=DT)
            for lv in range(1, 7):
                for G in GR:
                    NAT, Dm = st_[G]["NAT"], st_[G]["Dm"]
                    pP1 = pf.next()
                    for hh in range(4):
                        P.op("pe", "matmul", [NAT, Dm], [pP1], pP1[:, hh, :], lhsT=NAT[:, hh, lv, :], rhs=Dm[:, hh, :], start=True, stop=True)
                    P1s = p1_r.next()
                    P.op("act", "copy", [pP1], [P1s], out=P1s[:], in_=pP1[:])
                    st_[G]["P1s"] = P1s
                for G in GR:
                    Dm, DT, P1s = st_[G]["Dm"], st_[G]["DT"], st_[G]["P1s"]
                    if lv < 6:
                        pY = pf.next()
                        for hh in range(4):
                            P.op("pe", "matmul", [DT, P1s], [pY], pY[:, hh, :], lhsT=DT[:, hh, :], rhs=P1s[:, hh, :], start=True, stop=True)
                    pYT = pf.next()
                    for hh in range(4):
                        P.op("pe", "matmul", [P1s, DT], [pYT], pYT[:, hh, :], lhsT=P1s[:, hh, :], rhs=DT[:, hh, :], start=True, stop=True)
                    if lv < 6:
                        Dn = D_r.next()
                        P.op("dve", "tensor_tensor", [Dm, pY], [Dn], out=Dn[:], in0=Dm[:], in1=pY[:], op=ALU.add)
                        st_[G]["Dm"] = Dn
                    DTn = DT_r.next() if lv < 6 else TT_r.next()
                    P.op("dve", "tensor_tensor", [DT, pYT], [DTn], out=DTn[:], in0=DT[:], in1=pYT[:], op=ALU.add)
                    st_[G]["DT"] = DTn
            yield
            for G in GR:
                hs, gsl = hsl[G], gsls[G]
                KGT = st_[G]["KGT"]
                pR = pf.next()
                for hh, h in enumerate(hs):
                    P.op("pe", "matmul", [KGT, Sb], [pR], pR[:, hh, :], lhsT=KGT[:, hh, :], rhs=Sb[:, h, :], start=True, stop=True)
                Rt = R_r.next()
                P.op("dve", "tensor_tensor", [vtok, pR], [Rt], out=Rt[:], in0=vtok[:, gsl, :], in1=pR[:], op=ALU.subtract)
                st_[G]["Rt"] = Rt
            for G in GR:
                TT, Rt = st_[G]["DT"], st_[G]["Rt"]
                pVN = pf.next()
                for hh in range(4):
                    P.op("pe", "matmul", [TT, Rt], [pVN], pVN[:, hh, :], lhsT=TT[:, hh, :], rhs=Rt[:, hh, :], start=True, stop=True)
                VN = VN_r.next()
                P.op("dve", "tensor_tensor", [pVN, beta], [VN], out=VN[:], in0=pVN[:],
                     in1=beta[:, t, d8 + G * 4:d8 + G * 4 + 4].unsqueeze(2).broadcast_to([128, 4, 128]), op=ALU.mult)
                st_[G]["VN"] = VN
            for G in GR:
                hs, gsl = hsl[G], gsls[G]
                QGT, attnT, VN = st_[G]["QGT"], st_[G]["attnT"], st_[G]["VN"]
                pO = pf.next()
                for hh, h in enumerate(hs):
                    P.op("pe", "matmul", [QGT, Sb], [pO], pO[:, hh, :], lhsT=QGT[:, hh, :], rhs=Sb[:, h, :], start=True, stop=False)
                    P.op("pe", "matmul", [attnT, VN], [pO], pO[:, hh, :], lhsT=attnT[:, hh, :], rhs=VN[:, hh, :], start=False, stop=True)
                pS = pf.next()
                for hh, h in enumerate(hs):
                    P.op("pe", "matmul", [kdec, VN], [pS], pS[:, hh, :], lhsT=kdec[:, h, :], rhs=VN[:, hh, :], start=True, stop=True)
                P.op("dve", "tensor_tensor", [S32, egl], [S32], out=S32[:, gsl, :], in0=S32[:, gsl, :],
                     in1=egl[:, t, d8 + G * 4:d8 + G * 4 + 4].unsqueeze(2).broadcast_to([128, 4, 128]), op=ALU.mult)
                P.op("dve", "tensor_tensor", [S32, pS], [S32], out=S32[:, gsl, :], in0=S32[:, gsl, :], in1=pS[:], op=ALU.add)
                P.op("act", "copy", [S32], [Sb], out=Sb[:, gsl, :], in_=S32[:, gsl, :])
                if d == 0:
                    P.op("act", "copy", [pO], [Ot], out=Ot[:, gsl, :], in_=pO[:])
                else:
                    osum, osq, on, ost = osum_r.next(), osq_r.next(), on_r.next(), ost_r.next()
                    P.op("dve", "tensor_tensor", [pO, OFt], [osum], out=osum[:], in0=pO[:], in1=OFt[:, gsl, :], op=ALU.add)
                    P.op("pool", "tensor_tensor", [osum], [osq], out=osq[:], in0=osum[:], in1=osum[:], op=ALU.mult)
                    P.op("dve", "tensor_reduce", [osq], [ost], out=ost[:, :, 0], in_=osq[:], axis=AX.X, op=ALU.add)
                    P.op("dve", "tensor_scalar", [ost], [ost], out=ost[:, :, 1], in0=ost[:, :, 0], scalar1=1.0 / 128, scalar2=EPS,
                         op0=ALU.mult, op1=ALU.add)
                    P.op("act", "activation", [ost], [ost], out=ost[:, :, 2], in_=ost[:, :, 1], func=AF.Sqrt)
                    P.op("dve", "reciprocal", [ost], [ost], out=ost[:, :, 3], in_=ost[:, :, 2])
                    P.op("dve", "tensor_tensor", [osum, ost], [on], out=on[:], in0=osum[:],
                         in1=ost[:, :, 3:4].broadcast_to([128, 4, 128]), op=ALU.mult)
                    pT = pf.next()
                    for hh in range(4):
                        P.op("pe", "matmul", [on, identf], [pT], pT[:, hh, :], lhsT=on[:, hh, :], rhs=identf[:], start=True, stop=True)
                    P.op("dve", "scalar_tensor_tensor", [pT, onorm, GAt], [OAt], out=OAt[:, gsl, :], in0=pT[:], scalar=onorm[:, 0:1],
                         in1=GAt[:, gsl, :], op0=ALU.mult, op1=ALU.mult)
            if d == 0:
                P.dma("pool", OF[tok, :].rearrange("p (h e) -> p h e", h=8), Ot[:], reads=[Ot])
            else:
                P.dma("pool", OA_T.rearrange("(h p) t -> p h t", p=128)[:, :, tok], OAt[:], reads=[OAt])
        prev_g = None
        for t in order:
            g_ = tile_gen(t)
            next(g_)
            if prev_g is not None:
                for _ in prev_g:
                    pass
            prev_g = g_
        if prev_g is not None:
            for _ in prev_g:
                pass
        if d == 0:
            P.barrier()
    P.end()
    if _STOP == 3:
        nc._P = P
        return nc


    P.begin()
    onesb = P.sb([128, 128], BF16)
    P.op("dve", "memset", [], [onesb], onesb[:], 1.0)
    kb = P.sb([128, LMAX // 128], F32)
    P.dma("sp", kb[:], kbias, writes=[kb])
    kpe = P.sb([128, L], BF16)
    P.op("pool", "memset", [], [kpe], kpe[64:128, :], 0.0)
    P.dma("sp", kpe[0:64, :], KPE_T, writes=[kpe])
    kn_r = P.ring(2, [128, L], BF16)
    vh_r = P.ring(2, [128, NT, 128], BF16)
    qn_r = P.ring(2, [128, 512], BF16)
    qp_r = P.ring(2, [128, 512], BF16)
    for b_ in qp_r.bufs:
        P.op("pool", "memset", [], [b_], b_[64:128, :], 0.0)
    gb_r = P.ring(2, [128, 512], BF16)
    pt_r = P.ring(4, [128, 512], BF16)
    pS_r = P.ring(4, [128, 512], F32, psum=True)
    pO_r = P.ring(2, [128, 512], F32, psum=True)
    pZ_r = P.ring(2, [128, 512], F32, psum=True)
    rs_r = P.ring(2, [128, 512], F32)
    o1_r = P.ring(2, [128, 512], F32)
    ob_r = P.ring(2, [128, 512], BF16)
    sm_scale = float((D_NOPE + D_ROPE) ** -0.5)
    for h in range(8):
        knh, vh = kn_r.next(), vh_r.next()
        P.dma("sp", knh[:], KN_T[h * 128:(h + 1) * 128, :], writes=[knh])
        for t0_ in range(0, NT, 8):
            t1_ = min(NT, t0_ + 8)
            P.dma("act", vh[:, t0_:t1_, :], VB[t0_ * 128:t1_ * 128, h * 128:(h + 1) * 128].rearrange("(t p) e -> p t e", p=128), writes=[vh])
        for qb in range(NB):
            blk = slice(qb * 512, (qb + 1) * 512)
            qn, qp, gb = qn_r.next(), qp_r.next(), gb_r.next()
            P.dma("sp", qn[:], QF_T[h * 192:h * 192 + 128, blk], writes=[qn])
            P.dma("sp", qp[0:64, :], QF_T[h * 192 + 128:h * 192 + 192, blk], writes=[qp])
            P.dma("act", gb[:], GB_T[h * 128:(h + 1) * 128, blk], writes=[gb])
            pO, pZ = pO_r.next(), pZ_r.next()

            def scores(kt):
                ps_ = pS_r.next()
                P.op("pe", "matmul", [knh, qn], [ps_], ps_[:], lhsT=knh[:, kt * 128:(kt + 1) * 128], rhs=qn[:], start=True, stop=False)
                P.op("pe", "matmul", [kpe, qp], [ps_], ps_[:], lhsT=kpe[:, kt * 128:(kt + 1) * 128], rhs=qp[:], start=False, stop=True)
                return ps_
            pend = [scores(0)]
            if NT > 1:
                pend.append(scores(1))
            for kt in range(NT):
                cur = pend.pop(0)
                if kt + 2 < NT:
                    pend.append(scores(kt + 2))
                pt_ = pt_r.next()
                P.op("act", "activation", [cur, kb], [pt_], out=pt_[:], in_=cur[:], func=AF.Exp, bias=kb[:, kt:kt + 1], scale=sm_scale)
                P.op("pe", "matmul", [onesb, pt_], [pZ], pZ[:], lhsT=onesb[:], rhs=pt_[:], start=(kt == 0), stop=(kt == NT - 1))
                P.op("pe", "matmul", [vh, pt_], [pO], pO[:], lhsT=vh[:, kt, :], rhs=pt_[:], start=(kt == 0), stop=(kt == NT - 1))
            rs, o1, ob = rs_r.next(), o1_r.next(), ob_r.next()
            P.op("dve", "reciprocal", [pZ], [rs], out=rs[:], in_=pZ[:])
            P.op("dve", "tensor_tensor", [pO, rs], [o1], out=o1[:], in0=pO[:], in1=rs[:], op=ALU.mult)
            P.op("pool", "tensor_tensor", [o1, gb], [ob], out=ob[:], in0=o1[:], in1=gb[:], op=ALU.mult)
            P.dma("pool", OB_T[h * 128:(h + 1) * 128, blk], ob[:], reads=[ob])
    P.end()
    if _STOP == 5:
        nc._P = P
        return nc

    M_T = dscr("M_T", [NT, 128, 16, 128], BF16)
    P.begin()
    wpa = P.sb([128, 8, 2048], BF16)
    wpb = P.sb([128, 8, 2048], BF16)
    P.dma("pool", wpa[:], w_pa.rearrange("(c p) n -> p c n", p=128), writes=[wpa])
    P.dma("pool", wpb[:], w_pb.rearrange("(c p) n -> p c n", p=128), writes=[wpb])
    oa_r2 = P.ring(2, [128, 8, 512], BF16)
    ob_r2 = P.ring(2, [128, 8, 512], BF16)
    gm_r = P.ring(3, [128, 2, 512], BF16)
    m_r = P.ring(4, [128, 512], F32)
    mt_r = P.ring(3, [128, 512], BF16)
    pm_r = P.ring(6, [128, 512], F32, psum=True)
    for tb in range(NB):
        blk = slice(tb * 512, (tb + 1) * 512)
        oat, obt = oa_r2.next(), ob_r2.next()
        P.dma("sp", oat[:], OA_T.rearrange("(c p) t -> p c t", p=128)[:, :, blk], writes=[oat])
        P.dma("act", obt[:], OB_T.rearrange("(c p) t -> p c t", p=128)[:, :, blk], writes=[obt])
        for ct in range(16):
            gm = gm_r.next()
            P.dma("sp", gm[:, 0, :], GMA_T[ct * 128:(ct + 1) * 128, blk], writes=[gm])
            P.dma("act", gm[:, 1, :], GMB_T[ct * 128:(ct + 1) * 128, blk], writes=[gm])
            pA, pB = pm_r.next(), pm_r.next()
            for c in range(8):
                P.op("pe", "matmul", [wpa, oat], [pA], pA[:], lhsT=wpa[:, c, ct * 128:(ct + 1) * 128], rhs=oat[:, c, :], start=(c == 0), stop=(c == 7))
            for c in range(8):
                P.op("pe", "matmul", [wpb, obt], [pB], pB[:], lhsT=wpb[:, c, ct * 128:(ct + 1) * 128], rhs=obt[:, c, :], start=(c == 0), stop=(c == 7))
            m1, m2, mt = m_r.next(), m_r.next(), mt_r.next()
            P.op("dve", "tensor_tensor", [pA, gm], [m1], out=m1[:], in0=pA[:], in1=gm[:, 0, :], op=ALU.mult)
            P.op("dve", "tensor_tensor", [pB, gm], [m2], out=m2[:], in0=pB[:], in1=gm[:, 1, :], op=ALU.mult)
            P.op("pool", "tensor_tensor", [m1, m2], [mt], out=mt[:], in0=m1[:], in1=m2[:], op=ALU.add)
            P.dma("pool", M_T[tb * 4:tb * 4 + 4, :, ct, :].rearrange("j p t -> p j t"), mt[:].rearrange("p (j t) -> p j t", t=128), reads=[mt])
    P.end()
    if _STOP == 6:
        nc._P = P
        return nc

    P.begin()
    wout = P.sb([128, 16, 2048], BF16)
    P.dma("pool", wout[:], w_out.rearrange("(c p) n -> p c n", p=128), writes=[wout])
    nfb = P.sb([128, D_MODEL], F32)
    P.dma("sp", nfb[:], norm_f.partition_broadcast(128), writes=[nfb])
    mt_r2 = P.ring(2, [128, 16, 128], BF16)
    x_r = P.ring(2, [128, D_MODEL], F32)
    z_r = P.ring(2, [128, D_MODEL], F32)
    y_r = P.ring(2, [128, D_MODEL], F32)
    junk = P.sb([128, D_MODEL], BF16)
    st_r2 = P.ring(4, [128, 4], F32)
    pm_r = P.ring(6, [128, 512], F32, psum=True)
    for tt in range(NT):
        tok = slice(tt * 128, (tt + 1) * 128)
        mtt, xt = mt_r2.next(), x_r.next()
        P.dma("sp", mtt[:], M_T[tt], writes=[mtt])
        P.dma("act", xt[:], x[tok, :], writes=[xt])
        z = z_r.next()
        for cg in range(4):
            py_ = pm_r.next()
            for c in range(16):
                P.op("pe", "matmul", [mtt, wout], [py_], py_[:], lhsT=mtt[:, c, :], rhs=wout[:, c, cg * 512:(cg + 1) * 512], start=(c == 0), stop=(c == 15))
            P.op("dve", "tensor_tensor", [py_, xt], [z], out=z[:, cg * 512:(cg + 1) * 512], in0=py_[:], in1=xt[:, cg * 512:(cg + 1) * 512], op=ALU.add)
        st = st_r2.next()
        P.op("act", "activation", [z], [junk, st], out=junk[:], in_=z[:], func=AF.Square, accum_out=st[:, 0:1])
        P.op("dve", "tensor_scalar", [st], [st], out=st[:, 1:2], in0=st[:, 0:1], scalar1=1.0 / D_MODEL, scalar2=EPS, op0=ALU.mult, op1=ALU.add)
        P.op("act", "activation", [st], [st], out=st[:, 2:3], in_=st[:, 1:2], func=AF.Sqrt)
        P.op("dve", "reciprocal", [st], [st], out=st[:, 3:4], in_=st[:, 2:3])
        yv = y_r.next()
        P.op("dve", "scalar_tensor_tensor", [z, st, nfb], [yv], out=yv[:], in0=z[:], scalar=st[:, 3:4], in1=nfb[:], op0=ALU.mult, op1=ALU.mult)
        P.dma("pool", y[tok, :], yv[:], reads=[yv])
    P.end()
    if _STOP == 7:
        nc._P = P
        return nc

    nc._P = P
    return nc


_NC_CACHE = {}


def _core_map(consts, shared, xseq, L):
    valid = xseq.shape[0]
    xp = np.zeros((L, D_MODEL), np.float32)
    xp[:valid] = xseq
    kb = np.zeros((LMAX,), np.float32)
    kb[valid:] = -BIG
    m = dict(shared)
    m.update(consts)
    m["x"] = xp
    m["kbias"] = np.ascontiguousarray(kb.reshape(LMAX // 128, 128).T)
    tm = np.zeros((128, LMAX), np.float32)
    tm[:, :valid] = 1.0
    m["tmask"] = tm
    return m


def kernel(x_prompt, x_sample, norm_in, w_in, conv_w, a_log_f, dt_bias_f, a_log_b, dt_bias_b, o_norm_a,
           q_a_norm, w_q_b, kv_a_norm, w_kv_b, w_pa, w_pb, w_out, norm_f):
    f = lambda a: np.ascontiguousarray(np.asarray(a, dtype=np.float32))
    x_prompt, x_sample = f(x_prompt), f(x_sample)
    L = LMAX
    shared = {
        "norm_in": f(norm_in)[0], "w_in": f(w_in)[0],
        "conv_w": np.ascontiguousarray(f(conv_w)[0].reshape(KCONV * 24, 128)),
        "a_log": np.concatenate([f(a_log_f)[0], f(a_log_b)[0]]),
        "dt_bias": np.concatenate([f(dt_bias_f)[0], f(dt_bias_b)[0]]),
        "o_norm_a": f(o_norm_a)[0], "q_a_norm": f(q_a_norm)[0], "w_q_b": f(w_q_b)[0],
        "kv_a_norm": f(kv_a_norm)[0], "w_kv_b": f(w_kv_b)[0], "w_pa": f(w_pa)[0], "w_pb": f(w_pb)[0],
        "w_out": f(w_out)[0], "norm_f": f(norm_f),
    }
    consts = host_consts()
    seqs = [x_prompt[i] for i in range(4)] + [x_sample[i] for i in range(4)]
    in_maps = [_core_map(consts, shared, s, L) for s in seqs]
    if L not in _NC_CACHE:
        _NC_CACHE[L] = build(L)
    nc = _NC_CACHE[L]
    res = run_bass_kernel_spmd(nc, in_maps, core_ids=list(range(8)))
    ys = [np.asarray(r["y"], dtype=np.float32) for r in res.results]
    y_prompt = np.stack([ys[i][:x_prompt.shape[1]] for i in range(4)], axis=0)
    y_sample = np.stack([ys[4 + i] for i in range(4)], axis=0)
    return (y_prompt, y_sample)
```

```python
import math
from contextlib import ExitStack

import numpy as np
import ml_dtypes
import concourse.bass as bass
import concourse.mybir as mybir
from concourse.bass_utils import run_bass_kernel_spmd

F32 = mybir.dt.float32
BF16 = mybir.dt.bfloat16
AF = mybir.ActivationFunctionType
ALU = mybir.AluOpType
AX = mybir.AxisListType

D_MODEL = 2048
H = 8
DK = 128
CONV_DIM = 3072
KCONV = 5
Q_LORA = 1536
KV_LORA = 512
D_ROPE = 64
D_NOPE = 128
N_IN = 11360
EPS = 1e-6
LMAX = 4096
BIG = 30000.0

C_QKV = 0
C_GA = 3072
C_SM = 4096
C_QL = 4128
C_KVL = 5664
C_KR = 6176
C_GB = 6240
C_GMA = 7264
C_GMB = 9312

ENGS = ("sp", "act", "pe", "dve", "pool")
EPOCH = 12000


class Buf:
    __slots__ = ("t", "lw", "rd", "dsem", "dcnt", "name", "chain")

    def __init__(self, t, name):
        self.t = t
        self.name = name
        self.lw = None
        self.rd = []
        self.dsem = None
        self.dcnt = 0
        self.chain = None

    def __getitem__(self, idx):
        return self.t[idx]


class Ring:
    def __init__(self, bufs):
        self.bufs = bufs
        self.i = 0

    def next(self):
        b = self.bufs[self.i % len(self.bufs)]
        self.i += 1
        return b


class Prog:
    def __init__(self, nc):
        self.nc = nc
        self.ops = {e: [] for e in ENGS}
        self.cnt = {e: 0 for e in ENGS}
        self.sems = {}
        self.seen = {e: {} for e in ENGS}
        self.dma_bufs = []
        self.nbuf = 0
        self.stack = None
        self.ndsem = 0
        self.free_dsems = {"d": [], "w": []}
        self.mute = False
        self.phase_no = 0
        self.only = 0
        self.mute_set = set()

    def sb(self, shape, dtype=F32, name=None):
        self.nbuf += 1
        name = name or f"sb{self.nbuf}"
        t = self.stack.enter_context(self.nc.sbuf_tensor(name, list(shape), dtype))
        return Buf(t, name)

    def ps(self, shape, dtype=F32, name=None):
        self.nbuf += 1
        name = name or f"ps{self.nbuf}"
        t = self.stack.enter_context(self.nc.psum_tensor(name, list(shape), dtype))
        return Buf(t, name)

    def ring(self, n, shape, dtype=F32, psum=False):
        return Ring([(self.ps if psum else self.sb)(shape, dtype) for _ in range(n)])

    def _sem(self, key):
        if key not in self.sems:
            self.sems[key] = self.nc.alloc_semaphore("s_" + "_".join(str(k) for k in key))
        return self.sems[key]

    def _engkey(self, e, n):
        ep = (n - 1) // EPOCH
        return (("e", e, ep), n - ep * EPOCH)

    def _deps(self, eng, reads, writes, skip_key=None, is_dma=False):
        deps = {}

        def add(ev, kind):
            if ev is None:
                return
            key, val, src = ev
            if kind == "waw" and skip_key is not None and key == skip_key:
                return
            if src == eng and not is_dma:
                if eng == "pe":
                    return
                if kind != "raw":
                    return
            if deps.get(key, 0) < val:
                deps[key] = val
        for b in reads:
            add(b.lw, "raw")
        for b in writes:
            add(b.lw, "waw")
            for r in b.rd:
                add(r, "war")
        return deps

    def _prune(self, eng, deps):
        waits = []
        seen = self.seen[eng]
        for key, val in deps.items():
            if seen.get(key, 0) >= val:
                continue
            seen[key] = val
            waits.append((key, val))
        return waits

    def _collect(self, eng, reads, writes):
        return self._prune(eng, self._deps(eng, reads, writes))

    def op(self, eng, meth, reads, writes, *args, **kw):
        if self.mute:
            return None
        waits = self._collect(eng, reads, writes)
        self.cnt[eng] += 1
        key, val = self._engkey(eng, self.cnt[eng])
        self._sem(key)
        self.ops[eng].append((waits, meth, args, kw, (key, 1)))
        ev = (key, val, eng)
        for b in writes:
            b.lw = ev
            b.rd = []
            b.chain = None
        for b in reads:
            if b not in writes:
                b.rd.append(ev)
        return ev

    def dma(self, eng, out_ap, in_ap, reads=(), writes=(), owner=None, **kw):
        if self.mute:
            return None
        if owner is None:
            owner = (list(writes) + list(reads))[0]
        kind = "w" if eng == "pool" else "d"
        st = owner.dsem
        if st is None:
            st = owner.dsem = {}
        if kind not in st:
            fl = self.free_dsems[kind]
            if fl:
                st[kind] = list(fl.pop())
            else:
                self.ndsem += 1
                key = (kind, self.ndsem)
                self._sem(key)
                st[kind] = [key, 0]
            self.dma_bufs.append((owner, kind))
        deps = self._deps(eng, reads, writes, skip_key=st[kind][0], is_dma=True)
        for b in writes:
            if b.lw is not None and b.lw[0] == st[kind][0] and b.chain:
                for k_, v_ in b.chain.items():
                    if deps.get(k_, 0) < v_:
                        deps[k_] = v_
        for b in writes:
            b.chain = dict(deps)
        waits = self._prune(eng, deps)
        st[kind][1] += 16
        key = st[kind][0]
        ev = (key, st[kind][1], "dma")
        kw = dict(kw)
        kw["out"] = out_ap
        kw["in_"] = in_ap
        self.ops[eng].append((waits, "dma_start", (), kw, (key, 16)))
        for b in writes:
            b.lw = ev
            b.rd = []
        for b in reads:
            b.rd.append(ev)
        return ev

    def barrier(self):
        evs = []
        for e in ENGS:
            if self.cnt[e] > 0:
                k, v = self._engkey(e, self.cnt[e])
                evs.append((k, v))
        for b, kind in self.dma_bufs:
            evs.append(tuple(b.dsem[kind]))
        for e in ENGS:
            waits = []
            for k, v in evs:
                if k[0] == "e" and k[1] == e:
                    continue
                if self.seen[e].get(k, 0) >= v:
                    continue
                self.seen[e][k] = v
                waits.append((k, v))
            if waits:
                self.ops[e].append((waits, None, None, None, None))

    def flush(self):
        nc = self.nc
        engobj = {"sp": "sync", "act": "scalar", "pe": "tensor", "dve": "vector", "pool": "gpsimd"}
        with nc.Block() as block:
            for e in ENGS:
                lst = self.ops[e]

                def body(engine, lst=lst):
                    for waits, meth, args, kw, inc in lst:
                        for k, v in waits:
                            engine.wait_ge(self.sems[k], v)
                        if meth is not None:
                            ins = getattr(engine, meth)(*args, **kw)
                            ins.then_inc(self.sems[inc[0]], inc[1])
                getattr(block, engobj[e])(body)
        self.ops = {e: [] for e in ENGS}

    def begin(self):
        self.stack = ExitStack()
        self.phase_no += 1
        self.mute = (bool(self.only) and self.phase_no != self.only) or (self.phase_no in self.mute_set)

    def end(self):
        self.barrier()
        self.flush()
        for b, kind in self.dma_bufs:
            self.free_dsems[kind].append(tuple(b.dsem[kind]))
        self.dma_bufs = []
        self.stack.close()
        self.stack = None


def host_consts():
    c = {}
    c["ident"] = np.eye(128, dtype=np.float32)
    j = np.arange(128)[:, None]
    i = np.arange(128)[None, :]
    tri = np.zeros((2, 128, 128), np.float32)
    tri[0] = (j <= i)
    tri[1] = (j >= i)
    c["tri"] = tri
    nm = np.zeros((2, 128, 128), np.float32)
    nm[0] = np.where(i >= j, 0.0, -BIG)
    nm[1] = np.where(i <= j, 0.0, -BIG)
    c["negmask"] = nm
    lv = np.zeros((2, 128, 7, 128), np.float32)
    for si, s in enumerate([1, 2, 4, 8, 16, 32, 64]):
        same2 = (i // (2 * s)) == (j // (2 * s))
        diff1 = (i // s) != (j // s)
        lv[0, :, si, :] = -1.0 * (same2 & diff1 & (i > j))
        lv[1, :, si, :] = -1.0 * (same2 & diff1 & (i < j))
    c["lvl"] = lv
    pos = np.arange(LMAX, dtype=np.float32)
    inv_freq = (10000.0 ** (-np.arange(0, D_ROPE, 2, dtype=np.float32) / D_ROPE)).astype(np.float32)
    ang = pos[None, :] * inv_freq[:, None]
    c["cos4"] = np.tile(np.cos(ang).astype(np.float32), (4, 1))
    c["sin4"] = np.tile(np.sin(ang).astype(np.float32), (4, 1))
    return c


def build(L, dbg=(), _STOP=0):
    nc = bass.Bass("TRN2", target_bir_lowering=False)
    NT = L // 128
    NB = L // 512
    PART = min(L, 2048)
    NPART = L // PART

    def din(name, shape, dt=F32):
        return nc.dram_tensor(name, list(shape), dt, kind="ExternalInput").ap()

    def dscr(name, shape, dt=F32):
        kind = "ExternalOutput" if name in dbg else "Internal"
        return nc.dram_tensor(name, list(shape), dt, kind=kind).ap()

    x = din("x", [L, D_MODEL])
    norm_in = din("norm_in", [D_MODEL])
    w_in = din("w_in", [D_MODEL, N_IN])
    conv_w = din("conv_w", [KCONV * 24, 128])
    a_log = din("a_log", [16])
    dt_bias = din("dt_bias", [16])
    o_norm_a = din("o_norm_a", [128])
    q_a_norm = din("q_a_norm", [Q_LORA])
    w_q_b = din("w_q_b", [Q_LORA, 1536])
    kv_a_norm = din("kv_a_norm", [KV_LORA])
    w_kv_b = din("w_kv_b", [KV_LORA, 2048])
    w_pa = din("w_pa", [1024, D_MODEL])
    w_pb = din("w_pb", [1024, D_MODEL])
    w_out = din("w_out", [D_MODEL, D_MODEL])
    norm_f = din("norm_f", [D_MODEL])
    kbias = din("kbias", [128, LMAX // 128])
    tmask_d = din("tmask", [128, LMAX])
    c_ident = din("ident", [128, 128])
    c_tri = din("tri", [2, 128, 128])
    c_negmask = din("negmask", [2, 128, 128])
    c_lvl = din("lvl", [2, 128, 7, 128])
    c_cos4 = din("cos4", [128, LMAX])
    c_sin4 = din("sin4", [128, LMAX])

    y = nc.dram_tensor("y", [L, D_MODEL], F32, kind="ExternalOutput").ap()

    QF_T = dscr("QF_T", [8 * 192, L], BF16)
    KN_T = dscr("KN_T", [1024, L], BF16)
    KPE_T = dscr("KPE_T", [64, L], BF16)
    VB = dscr("VB", [L, 1024], BF16)
    OB_T = dscr("OB_T", [1024, L], BF16)
    QKV_T = dscr("QKV_T", [CONV_DIM, L], BF16)
    GA_T = dscr("GA_T", [1024, L], BF16)
    GB_T = dscr("GB_T", [1024, L], BF16)
    GMA_T = dscr("GMA_T", [2048, L], BF16)
    GMB_T = dscr("GMB_T", [2048, L], BF16)
    QL_T = dscr("QL_T", [Q_LORA, L])
    KVL_T = dscr("KVL_T", [KV_LORA, L])
    KR_T = dscr("KR_T", [64, L])
    AB = dscr("AB", [L, 32])
    QT = dscr("QT", [1024, L], BF16)
    KT = dscr("KT", [1024, L], BF16)
    VT = dscr("VT", [1024, L], BF16)
    OA_T = dscr("OA_T", [1024, L], BF16)

    P = Prog(nc)

    P.begin()
    identf = P.sb([128, 128], F32)
    identb = P.sb([128, 128], BF16)
    gbc = P.sb([128, D_MODEL], F32)
    P.dma("sp", identf[:], c_ident, writes=[identf])
    P.dma("sp", gbc[:], norm_in.partition_broadcast(128), writes=[gbc])
    P.op("dve", "tensor_copy", [identf], [identb], out=identb[:], in_=identf[:])
    xnT = P.sb([128, 16, PART], BF16)
    xring = P.ring(2, [128, D_MODEL], F32)
    xsring = P.ring(2, [128, D_MODEL], BF16)
    junk = P.sb([128, D_MODEL], BF16)
    stat = P.ring(4, [128, 4], F32)
    ptr = P.ring(2, [128, 8, 128], BF16, psum=True)
    pmm = P.ring(4, [128, 512], F32, psum=True)
    wring = P.ring(2, [128, 16, 512], BF16)
    oring = P.ring(3, [128, PART], F32)
    oringb = P.ring(2, [128, PART], BF16)
    wsm = P.sb([128, 16, 32], BF16)
    absg = P.sb([128, PART // 128, 32], F32)
    P.dma("pool", wsm[:], w_in[:, C_SM:C_SM + 32].rearrange("(c p) n -> p c n", p=128), writes=[wsm])

    groups = []

    def add_group(c0, n, dest, func):
        o = 0
        while o < n:
            w = min(512, n - o)
            groups.append((c0 + o, w, dest, o, func))
            o += w
    add_group(C_QKV, 3072, QKV_T, None)
    add_group(C_GA, 1024, GA_T, AF.Silu)
    add_group(C_QL, Q_LORA, QL_T, None)
    add_group(C_KVL, KV_LORA, KVL_T, None)
    add_group(C_KR, 64, KR_T, None)
    add_group(C_GB, 1024, GB_T, AF.Silu)
    add_group(C_GMA, 2048, GMA_T, AF.Sigmoid)
    add_group(C_GMB, 2048, GMB_T, AF.Sigmoid)
    groups.sort(key=lambda g: {None: 0, AF.Silu: 1, AF.Sigmoid: 2}[g[4]])

    evq = 0
    for part in range(NPART):
        t0 = part * PART
        for tt in range(PART // 128):
            xt = xring.next()
            P.dma("sp", xt[:], x[t0 + tt * 128: t0 + (tt + 1) * 128, :], writes=[xt])
            st = stat.next()
            P.op("act", "activation", [xt], [junk, st], out=junk[:], in_=xt[:], func=AF.Square, accum_out=st[:, 0:1])
            P.op("dve", "tensor_scalar", [st], [st], out=st[:, 1:2], in0=st[:, 0:1], scalar1=1.0 / D_MODEL, scalar2=EPS,
                 op0=ALU.mult, op1=ALU.add)
            P.op("act", "activation", [st], [st], out=st[:, 2:3], in_=st[:, 1:2], func=AF.Sqrt)
            P.op("dve", "reciprocal", [st], [st], out=st[:, 3:4], in_=st[:, 2:3])
            xs = xsring.next()
            P.op("dve", "scalar_tensor_tensor", [xt, st, gbc], [xs], out=xs[:], in0=xt[:], scalar=st[:, 3:4], in1=gbc[:],
                 op0=ALU.mult, op1=ALU.mult)
            for hh in range(2):
                pt = ptr.next()
                for c in range(8):
                    cc = hh * 8 + c
                    P.op("pe", "transpose", [xs, identb], [pt], out=pt[:, c, :], in_=xs[:, cc * 128:(cc + 1) * 128],
                         identity=identb[:])
                if hh == 0:
                    P.op("act", "copy", [pt], [xnT], out=xnT[:, 0:8, tt * 128:(tt + 1) * 128], in_=pt[:])
                else:
                    P.op("dve", "tensor_copy", [pt], [xnT], out=xnT[:, 8:16, tt * 128:(tt + 1) * 128], in_=pt[:])
        for tt in range(PART // 128):
            pm = pmm.next()
            for c in range(16):
                P.op("pe", "matmul", [xnT, wsm], [pm], pm[:, 0:32], lhsT=xnT[:, c, tt * 128:(tt + 1) * 128], rhs=wsm[:, c, :],
                     start=(c == 0), stop=(c == 15))
            P.op("dve", "tensor_copy", [pm], [absg], out=absg[:, tt, :], in_=pm[:, 0:32])
        for a0 in range(0, PART // 128, 8):
            a1 = min(PART // 128, a0 + 8)
            P.dma("sp", AB[t0 + a0 * 128:t0 + a1 * 128, :].rearrange("(t p) c -> p t c", p=128), absg[:, a0:a1, :], reads=[absg])
        for (c0, ncol, dest, r0, func) in groups:
            wt = wring.next()
            P.dma("pool", wt[:, :, 0:ncol], w_in[:, c0:c0 + ncol].rearrange("(c p) n -> p c n", p=128), writes=[wt])
            for ct in range((ncol + 127) // 128):
                m = min(128, ncol - ct * 128)
                ot = oringb.next() if (func is not None or dest is QKV_T) else oring.next()
                for tb in range(PART // 512):
                    pm = pmm.next()
                    for c in range(16):
                        P.op("pe", "matmul", [xnT, wt], [pm], pm[0:m, :], lhsT=wt[:, c, ct * 128:ct * 128 + m],
                             rhs=xnT[:, c, tb * 512:(tb + 1) * 512], start=(c == 0), stop=(c == 15))
                    if func is not None:
                        P.op("act", "activation", [pm], [ot], out=ot[0:m, tb * 512:(tb + 1) * 512], in_=pm[0:m, :], func=func)
                    else:
                        evq += 1
                        if evq % 2 == 0:
                            P.op("act", "copy", [pm], [ot], out=ot[0:m, tb * 512:(tb + 1) * 512], in_=pm[0:m, :])
                        else:
                            P.op("dve", "tensor_copy", [pm], [ot], out=ot[0:m, tb * 512:(tb + 1) * 512], in_=pm[0:m, :])
                rr = r0 + ct * 128
                P.dma("act", dest[rr:rr + m, t0:t0 + PART], ot[0:m, :], reads=[ot])
    P.end()
    if _STOP == 1:
        nc._P = P
        return nc

    P.begin()
    identf = P.sb([128, 128], F32)
    P.dma("sp", identf[:], c_ident, writes=[identf])
    onesb = P.sb([128, 128], BF16)
    P.op("dve", "memset", [], [onesb], onesb[:], 1.0)
    cwr = P.sb([120, 128], F32)
    P.dma("sp", cwr[:], conv_w, writes=[cwr])
    cw = P.sb([128, 120], F32)
    pmm = P.ring(4, [128, 512], F32, psum=True)
    pm = pmm.next()
    P.op("pe", "matmul", [cwr, identf], [pm], pm[:, 0:120], lhsT=cwr[:], rhs=identf[0:120, 0:120], start=True, stop=True)
    P.op("dve", "tensor_copy", [pm], [cw], out=cw[:], in_=pm[:, 0:120])
    tmask = P.sb([128, L], F32)
    P.dma("act", tmask[:], tmask_d[:, 0:L], writes=[tmask])
    epst = P.sb([128, 2], F32)
    P.op("dve", "memset", [], [epst], epst[:, 0:1], EPS)
    P.op("dve", "memset", [], [epst], epst[:, 1:2], EPS * DK)
    identb2 = P.sb([128, 128], BF16)
    P.op("dve", "tensor_copy", [identf], [identb2], out=identb2[:], in_=identf[:])
    xpr = P.ring(2, [128, L + 4], BF16)
    for b in xpr.bufs:
        P.op("pool", "memset", [], [b], b[:, 0:2], 0.0)
        P.op("pool", "memset", [], [b], b[:, L + 2:L + 4], 0.0)
    dgw_r = P.ring(2, [128, KCONV, 128], BF16)
    slr = P.ring(2, [128, L], F32)
    sqr = P.ring(2, [128, L], BF16)
    outr = P.ring(2, [128, L], BF16)
    rnr = P.ring(3, [128, 512], F32)
    cw3 = cw[:].rearrange("p (k c) -> p k c", c=24)
    for cc in range(24):
        kind = cc // 8
        xp = xpr.next()
        P.dma("sp", xp[:, 2:L + 2], QKV_T[cc * 128:(cc + 1) * 128, :], writes=[xp])
        dgw = dgw_r.next()
        P.op("dve", "tensor_tensor", [identb2, cw], [dgw], out=dgw[:], in0=identb2[:].unsqueeze(1).broadcast_to([128, KCONV, 128]),
             in1=cw3[:, :, cc:cc + 1].broadcast_to([128, KCONV, 128]), op=ALU.mult)
        sl = slr.next()
        for tb in range(NB):
            pc = pmm.next()
            for k in range(KCONV):
                P.op("pe", "matmul", [dgw, xp], [pc], pc[:], lhsT=dgw[:, k, :], rhs=xp[:, tb * 512 + k:tb * 512 + k + 512],
                     start=(k == 0), stop=(k == KCONV - 1))
            P.op("act", "activation", [pc], [sl], out=sl[:, tb * 512:(tb + 1) * 512], in_=pc[:], func=AF.Silu)
        P.op("dve", "tensor_tensor", [sl, tmask], [sl], out=sl[:], in0=sl[:], in1=tmask[:], op=ALU.mult)
        ob = outr.next()
        if kind == 2:
            P.op("pool", "tensor_copy", [sl], [ob], out=ob[:], in_=sl[:])
            P.dma("pool", VT[(cc - 16) * 128:(cc - 15) * 128, :], ob[:], reads=[ob])
        else:
            sq = sqr.next()
            P.op("pool", "tensor_tensor", [sl], [sq], out=sq[:], in0=sl[:], in1=sl[:], op=ALU.mult)
            for tb in range(NB):
                pm = pmm.next()
                P.op("pe", "matmul", [onesb, sq], [pm], pm[:], lhsT=onesb[:], rhs=sq[:, tb * 512:(tb + 1) * 512], start=True, stop=True)
                rn = rnr.next()
                if kind == 0:
                    P.op("act", "activation", [pm, epst], [rn], out=rn[:], in_=pm[:], func=AF.Sqrt, bias=epst[:, 1:2], scale=float(DK))
                else:
                    P.op("act", "activation", [pm, epst], [rn], out=rn[:], in_=pm[:], func=AF.Sqrt, bias=epst[:, 0:1])
                P.op("dve", "reciprocal", [rn], [rn], out=rn[:], in_=rn[:])
                P.op("dve", "tensor_tensor", [sl, rn], [ob], out=ob[:, tb * 512:(tb + 1) * 512],
                     in0=sl[:, tb * 512:(tb + 1) * 512], in1=rn[:], op=ALU.mult)
            dst = QT if kind == 0 else KT
            hh = cc % 8
            P.dma("pool", dst[hh * 128:(hh + 1) * 128, :], ob[:], reads=[ob])
    P.end()
    if _STOP == 2:
        nc._P = P
        return nc


    P.begin()
    identf = P.sb([128, 128], F32)
    P.dma("sp", identf[:], c_ident, writes=[identf])
    onesb = P.sb([128, 128], BF16)
    P.op("dve", "memset", [], [onesb], onesb[:], 1.0)
    wq = P.sb([128, 12, 1536], BF16)
    P.dma("pool", wq[:], w_q_b.rearrange("(c p) n -> p c n", p=128), writes=[wq])
    wqt = P.sb([128, 2, 12, 256], BF16)
    wq4 = wq[:].rearrange("p c (h r) -> p c h r", r=192)
    P.op("dve", "tensor_copy", [wq], [wqt], out=wqt[:, 0, :, :].rearrange("p c (h r) -> p c h r", r=32), in_=wq4[:, :, :, 128:160])
    P.op("pool", "tensor_copy", [wq], [wqt], out=wqt[:, 1, :, :].rearrange("p c (h r) -> p c h r", r=32), in_=wq4[:, :, :, 160:192])
    wkv = P.sb([128, 4, 2048], BF16)
    P.dma("pool", wkv[:], w_kv_b.rearrange("(c p) n -> p c n", p=128), writes=[wkv])
    pm_r = P.ring(6, [128, 512], F32, psum=True)
    nrm_rows = P.sb([16, 128], F32)
    P.dma("sp", nrm_rows[0:12, :], q_a_norm.rearrange("(c p) -> c p", p=128), writes=[nrm_rows])
    P.dma("sp", nrm_rows[12:16, :], kv_a_norm.rearrange("(c p) -> c p", p=128), writes=[nrm_rows])
    nrm = P.sb([128, 16], F32)
    pm = pm_r.next()
    P.op("pe", "matmul", [nrm_rows, identf], [pm], pm[:, 0:16], lhsT=nrm_rows[:], rhs=identf[0:16, 0:16], start=True, stop=True)
    P.op("dve", "tensor_copy", [pm], [nrm], out=nrm[:], in_=pm[:, 0:16])
    wkv3 = wkv[:].rearrange("p c (h r) -> p c h r", r=256)
    ql_r = P.ring(1, [128, 12, 512], F32)
    sq_r = P.ring(1, [128, 12, 512], BF16)
    cq_r = P.ring(2, [128, 12, 512], BF16)
    kvl_r = P.ring(2, [128, 4, 512], F32)
    ckv_r = P.ring(2, [128, 4, 512], BF16)
    rr_r = P.ring(2, [128, 512], F32)
    cs_r = P.ring(2, [128, 2, 512], F32)
    st_r = P.ring(4, [128, 512], BF16)
    tmp_r = P.ring(4, [128, 512], F32)
    vb_r = P.ring(2, [128, 8, 128], BF16)
    kr_r = P.ring(2, [32, 2, 512], F32)
    evq = 0
    for tb in range(NB):
        blk = slice(tb * 512, (tb + 1) * 512)
        cs = cs_r.next()
        P.dma("sp", cs[:, 0, :], c_cos4[:, blk], writes=[cs])
        P.dma("sp", cs[:, 1, :], c_sin4[:, blk], writes=[cs])

        def latent_norm(src, nch, ncol0, ring_in, ring_out, dim):
            lt = ring_in.next()
            P.dma("act", lt[:], src.rearrange("(c p) t -> p c t", p=128)[:, :, blk], writes=[lt])
            sq = sq_r.next()
            P.op("pool", "tensor_tensor", [lt], [sq], out=sq[:, 0:nch, :], in0=lt[:], in1=lt[:], op=ALU.mult)
            pss = pm_r.next()
            for c in range(nch):
                P.op("pe", "matmul", [onesb, sq], [pss], pss[:], lhsT=onesb[:], rhs=sq[:, c, :], start=(c == 0), stop=(c == nch - 1))
            rr = rr_r.next()
            P.op("dve", "tensor_scalar", [pss], [rr], out=rr[:], in0=pss[:], scalar1=1.0 / dim, scalar2=EPS, op0=ALU.mult, op1=ALU.add)
            P.op("act", "activation", [rr], [rr], out=rr[:], in_=rr[:], func=AF.Sqrt)
            P.op("dve", "reciprocal", [rr], [rr], out=rr[:], in_=rr[:])
            ct_ = ring_out.next()
            for c in range(nch):
                P.op("dve", "scalar_tensor_tensor", [lt, nrm, rr], [ct_], out=ct_[:, c, :], in0=lt[:, c, :],
                     scalar=nrm[:, ncol0 + c:ncol0 + c + 1], in1=rr[:], op0=ALU.mult, op1=ALU.mult)
            return ct_

        def rope(pT1, pT2, np_, dst1, dst2):
            a, b, c_, d_ = tmp_r.next(), tmp_r.next(), tmp_r.next(), tmp_r.next()
            P.op("dve", "tensor_tensor", [pT1[0], cs], [a], out=a[0:np_, :], in0=pT1[1], in1=cs[0:np_, 0, :], op=ALU.mult)
            P.op("dve", "tensor_tensor", [pT2[0], cs], [b], out=b[0:np_, :], in0=pT2[1], in1=cs[0:np_, 1, :], op=ALU.mult)
            P.op("dve", "tensor_tensor", [pT1[0], cs], [c_], out=c_[0:np_, :], in0=pT1[1], in1=cs[0:np_, 1, :], op=ALU.mult)
            P.op("dve", "tensor_tensor", [pT2[0], cs], [d_], out=d_[0:np_, :], in0=pT2[1], in1=cs[0:np_, 0, :], op=ALU.mult)
            o1, o2 = st_r.next(), st_r.next()
            P.op("pool", "tensor_tensor", [a, b], [o1], out=o1[0:np_, :], in0=a[0:np_, :], in1=b[0:np_, :], op=ALU.subtract)
            P.op("pool", "tensor_tensor", [c_, d_], [o2], out=o2[0:np_, :], in0=c_[0:np_, :], in1=d_[0:np_, :], op=ALU.add)
            for (o, dst) in ((o1, dst1), (o2, dst2)):
                for (psl, dap) in dst:
                    P.dma("pool", dap, o[psl, :], reads=[o])

        cq = latent_norm(QL_T, 12, 0, ql_r, cq_r, Q_LORA)
        for h in range(8):
            pn = pm_r.next()
            for c in range(12):
                P.op("pe", "matmul", [wq, cq], [pn], pn[:], lhsT=wq[:, c, h * 192:h * 192 + 128], rhs=cq[:, c, :], start=(c == 0), stop=(c == 11))
            qn = st_r.next()
            evq += 1
            if evq % 2:
                P.op("act", "copy", [pn], [qn], out=qn[:], in_=pn[:])
            else:
                P.op("dve", "tensor_copy", [pn], [qn], out=qn[:], in_=pn[:])
            P.dma("pool", QF_T[h * 192:h * 192 + 128, blk], qn[:], reads=[qn])
        for g in range(2):
            pT1, pT2 = pm_r.next(), pm_r.next()
            for (pt_, half) in ((pT1, 0), (pT2, 1)):
                for c in range(12):
                    P.op("pe", "matmul", [wqt, cq], [pt_], pt_[:], lhsT=wqt[:, half, c, g * 128:(g + 1) * 128], rhs=cq[:, c, :],
                         start=(c == 0), stop=(c == 11))
            QF3 = QF_T.rearrange("(h r) t -> h r t", r=192)
            dst1 = [(slice(32 * hh, 32 * hh + 32), QF3[4 * g + hh, 128:160, blk]) for hh in range(4)]
            dst2 = [(slice(32 * hh, 32 * hh + 32), QF3[4 * g + hh, 160:192, blk]) for hh in range(4)]
            rope((pT1, pT1[:]), (pT2, pT2[:]), 128, dst1, dst2)
        ckv = latent_norm(KVL_T, 4, 12, kvl_r, ckv_r, KV_LORA)
        for h in range(8):
            pn = pm_r.next()
            for c in range(4):
                P.op("pe", "matmul", [wkv, ckv], [pn], pn[:], lhsT=wkv[:, c, h * 256:h * 256 + 128], rhs=ckv[:, c, :], start=(c == 0), stop=(c == 3))
            kn = st_r.next()
            evq += 1
            if evq % 2:
                P.op("act", "copy", [pn], [kn], out=kn[:], in_=pn[:])
            else:
                P.op("dve", "tensor_copy", [pn], [kn], out=kn[:], in_=pn[:])
            P.dma("pool", KN_T[h * 128:(h + 1) * 128, blk], kn[:], reads=[kn])
        for st in range(4):
            vb = vb_r.next()
            for g in range(2):
                pv_ = pm_r.next()
                for c in range(4):
                    P.op("pe", "matmul", [ckv, wkv], [pv_], pv_[:].rearrange("p (h e) -> p h e", e=128), lhsT=ckv[:, c, st * 128:(st + 1) * 128],
                         rhs=wkv3[:, c, 4 * g:4 * g + 4, 128:256], start=(c == 0), stop=(c == 3))
                if g == 0:
                    P.op("act", "copy", [pv_], [vb], out=vb[:, 0:4, :], in_=pv_[:].rearrange("p (h e) -> p h e", e=128))
                else:
                    P.op("dve", "tensor_copy", [pv_], [vb], out=vb[:, 4:8, :], in_=pv_[:].rearrange("p (h e) -> p h e", e=128))
            r0 = tb * 512 + st * 128
            P.dma("pool", VB[r0:r0 + 128, :].rearrange("p (h e) -> p h e", e=128), vb[:], reads=[vb])
        kr = kr_r.next()
        P.dma("act", kr[:, 0, :], KR_T[0:32, blk], writes=[kr])
        P.dma("act", kr[:, 1, :], KR_T[32:64, blk], writes=[kr])
        rope((kr, kr[:, 0, :]), (kr, kr[:, 1, :]), 32, [(slice(0, 32), KPE_T[0:32, blk])], [(slice(0, 32), KPE_T[32:64, blk])])
    P.end()
    if _STOP == 4:
        nc._P = P
        return nc

    OF = dscr("OF", [L, 1024])
    P.begin()
    identf = P.sb([128, 128], F32)
    identb = P.sb([128, 128], BF16)
    onesf = P.sb([128, 128], F32)
    P.dma("sp", identf[:], c_ident, writes=[identf])
    P.op("dve", "tensor_copy", [identf], [identb], out=identb[:], in_=identf[:])
    P.op("dve", "memset", [], [onesf], onesf[:], 1.0)
    tri = P.sb([128, 2, 128], F32)
    P.dma("sp", tri[:], c_tri.rearrange("d p i -> p d i"), writes=[tri])
    negm = P.sb([128, 2, 128], F32)
    P.dma("sp", negm[:], c_negmask.rearrange("d p i -> p d i"), writes=[negm])
    lvl = P.sb([128, 2, 7, 128], BF16)
    P.dma("pool", lvl[:], c_lvl.rearrange("d p s i -> p d s i"), writes=[lvl])
    onorm = P.sb([128, 1], F32)
    P.dma("sp", onorm[:], o_norm_a.rearrange("(p o) -> p o", o=1), writes=[onorm])
    ABt = P.sb([128, NT, 32], F32)
    for t0_ in range(0, NT, 8):
        t1_ = min(NT, t0_ + 8)
        P.dma("sp", ABt[:, t0_:t1_, :], AB[t0_ * 128:t1_ * 128, :].rearrange("(t p) c -> p t c", p=128), writes=[ABt])
    dtb = P.sb([128, 16], F32)
    alg = P.sb([128, 16], F32)
    P.dma("sp", dtb[:], dt_bias.partition_broadcast(128), writes=[dtb])
    P.dma("sp", alg[:], a_log.partition_broadcast(128), writes=[alg])
    negA = P.sb([128, 16], F32)
    P.op("act", "activation", [alg], [negA], out=negA[:], in_=alg[:], func=AF.Exp)
    P.op("dve", "tensor_scalar", [negA], [negA], out=negA[:], in0=negA[:], scalar1=-1.0, scalar2=None, op0=ALU.mult)
    gt = P.sb([128, NT, 16], F32)
    beta = P.sb([128, NT, 16], F32)
    gcs = P.sb([128, NT, 16], F32)
    ngc = P.sb([128, NT, 16], F32)
    eg = P.sb([128, NT, 16], F32)
    kd = P.sb([128, NT, 16], F32)
    egl = P.sb([128, NT, 16], F32)
    pbf = P.ring(2, [128, 8, 128], BF16, psum=True)
    pf = P.ring(6, [128, 4, 128], F32, psum=True)

    def bc16(ap16):
        return ap16.unsqueeze(1).broadcast_to([128, NT, 16])
    P.op("dve", "tensor_tensor", [ABt, dtb], [gt], out=gt[:], in0=ABt[:, :, 0:16], in1=bc16(dtb[:]), op=ALU.add)
    P.op("act", "activation", [gt], [gt], out=gt[:], in_=gt[:], func=AF.Exp)
    P.op("dve", "tensor_scalar", [gt], [gt], out=gt[:], in0=gt[:], scalar1=1.0, scalar2=None, op0=ALU.add)
    P.op("act", "activation", [gt], [gt], out=gt[:], in_=gt[:], func=AF.Ln)
    P.op("dve", "tensor_tensor", [gt, negA], [gt], out=gt[:], in0=gt[:], in1=bc16(negA[:]), op=ALU.mult)
    P.op("act", "activation", [ABt], [beta], out=beta[:], in_=ABt[:, :, 16:32], func=AF.Exp, scale=-1.0)
    P.op("dve", "tensor_scalar", [beta], [beta], out=beta[:], in0=beta[:], scalar1=1.0, scalar2=None, op0=ALU.add)
    P.op("dve", "reciprocal", [beta], [beta], out=beta[:], in_=beta[:])
    pg = pf.next()
    pgv = pg[:].rearrange("p a b -> p (a b)")
    for d in range(2):
        P.op("pe", "matmul", [tri, gt], [pg], pgv[:, d * NT * 8:(d + 1) * NT * 8], lhsT=tri[:, d, :], rhs=gt[:, :, d * 8:(d + 1) * 8],
             start=True, stop=True)
    for d in range(2):
        P.op("dve", "tensor_copy", [pg], [gcs], out=gcs[:, :, d * 8:(d + 1) * 8],
             in_=pgv[:, d * NT * 8:(d + 1) * NT * 8].rearrange("p (t h) -> p t h", h=8))
    ptot = pf.next()
    ptv = ptot[:].rearrange("p a b -> p (a b)")
    gtv = gt[:].rearrange("p t c -> p (t c)")
    for c0_ in range(0, NT * 16, 256):
        c1_ = min(NT * 16, c0_ + 256)
        P.op("pe", "matmul", [onesf, gt], [ptot], ptv[:, c0_:c1_], lhsT=onesf[:], rhs=gtv[:, c0_:c1_], start=True, stop=True)
    ptv3 = ptv[:, 0:NT * 16].rearrange("p (t c) -> p t c", c=16)
    P.op("act", "activation", [ptot], [egl], out=egl[:], in_=ptv3, func=AF.Exp)
    P.op("dve", "tensor_tensor", [ptot, gcs], [kd], out=kd[:], in0=ptv3, in1=gcs[:], op=ALU.subtract)
    P.op("act", "activation", [kd], [kd], out=kd[:], in_=kd[:], func=AF.Exp)
    P.op("act", "activation", [gcs], [eg], out=eg[:], in_=gcs[:], func=AF.Exp)
    P.op("dve", "tensor_scalar", [gcs], [ngc], out=ngc[:], in0=gcs[:], scalar1=-1.0, scalar2=None, op0=ALU.mult)

    S32 = P.sb([128, 8, 128], F32)
    Sb = P.sb([128, 8, 128], BF16)
    kt_r = P.ring(2, [128, 8, 128], BF16)
    qt_r = P.ring(2, [128, 8, 128], BF16)
    vt_r = P.ring(2, [128, 8, 128], BF16)
    kdec_r = P.ring(2, [128, 8, 128], BF16)
    vtok_r = P.ring(2, [128, 8, 128], BF16)
    dg_r = P.ring(2, [128, 8, 128], F32)
    de_r = P.ring(2, [128, 8, 128], F32)
    E_r = P.ring(4, [128, 4, 128], F32)
    AT_r = P.ring(4, [128, 4, 128], BF16)
    attn_r = P.ring(4, [128, 4, 128], BF16)
    kgt_r = P.ring(4, [128, 4, 128], BF16)
    qgt_r = P.ring(4, [128, 4, 128], BF16)
    nat_r = P.ring(4, [128, 4, 7, 128], BF16)
    D_r = P.ring(4, [128, 4, 128], BF16)
    DT_r = P.ring(4, [128, 4, 128], BF16)
    p1_r = P.ring(4, [128, 4, 128], BF16)
    TT_r = P.ring(4, [128, 4, 128], BF16)
    R_r = P.ring(4, [128, 4, 128], BF16)
    VN_r = P.ring(4, [128, 4, 128], BF16)
    O_r = P.ring(2, [128, 8, 128], F32)
    of_r = P.ring(2, [128, 8, 128], F32)
    ga_r = P.ring(2, [128, 8, 128], BF16)
    osum_r = P.ring(2, [128, 4, 128], F32)
    osq_r = P.ring(2, [128, 4, 128], F32)
    on_r = P.ring(2, [128, 4, 128], F32)
    ost_r = P.ring(4, [128, 4, 4], F32)
    oa_r = P.ring(2, [128, 8, 128], BF16)
    identb_bc = identb[:].unsqueeze(1).broadcast_to([128, 4, 128])

    for d in range(2):
        P.op("dve", "memset", [], [S32], S32[:], 0.0)
        P.op("pool", "memset", [], [Sb], Sb[:], 0.0)
        order = range(NT) if d == 0 else range(NT - 1, -1, -1)
        d8 = d * 8
        def tile_gen(t):
            tok = slice(t * 128, (t + 1) * 128)
            KTt, QTt, VTt = kt_r.next(), qt_r.next(), vt_r.next()
            P.dma("sp", KTt[:], KT.rearrange("(h p) t -> p h t", p=128)[:, :, tok], writes=[KTt])
            P.dma("act", QTt[:], QT.rearrange("(h p) t -> p h t", p=128)[:, :, tok], writes=[QTt])
            P.dma("sp", VTt[:], VT.rearrange("(h p) t -> p h t", p=128)[:, :, tok], writes=[VTt])
            if d == 1:
                OFt, GAt = of_r.next(), ga_r.next()
                P.dma("sp", OFt[:], OF[tok, :].rearrange("p (h e) -> p h e", h=8), writes=[OFt])
                P.dma("act", GAt[:], GA_T.rearrange("(h p) t -> p h t", p=128)[:, :, tok], writes=[GAt])
                OAt = oa_r.next()
            else:
                Ot = O_r.next()
            pk, pv = pbf.next(), pbf.next()
            for h in range(8):
                P.op("pe", "transpose", [KTt, identb], [pk], out=pk[:, h, :], in_=KTt[:, h, :], identity=identb[:])
            for h in range(8):
                P.op("pe", "transpose", [VTt, identb], [pv], out=pv[:, h, :], in_=VTt[:, h, :], identity=identb[:])
            kdec, vtok = kdec_r.next(), vtok_r.next()
            P.op("dve", "tensor_tensor", [pk, kd], [kdec], out=kdec[:], in0=pk[:],
                 in1=kd[:, t, d8:d8 + 8].unsqueeze(2).broadcast_to([128, 8, 128]), op=ALU.mult)
            P.op("act", "copy", [pv], [vtok], out=vtok[:], in_=pv[:])
            dg, de = dg_r.next(), de_r.next()
            idbc8 = identf[:].unsqueeze(1).broadcast_to([128, 8, 128])
            P.op("pool", "tensor_tensor", [identf, gcs], [dg], out=dg[:], in0=idbc8,
                 in1=gcs[:, t, d8:d8 + 8].unsqueeze(2).broadcast_to([128, 8, 128]), op=ALU.mult)
            P.op("pool", "tensor_tensor", [identf, eg], [de], out=de[:], in0=idbc8,
                 in1=eg[:, t, d8:d8 + 8].unsqueeze(2).broadcast_to([128, 8, 128]), op=ALU.mult)
            GR = (0, 1)
            hsl = [range(G * 4, G * 4 + 4) for G in GR]
            gsls = [slice(G * 4, G * 4 + 4) for G in GR]
            st_ = [dict() for _ in GR]
            for G in GR:
                hs = hsl[G]
                pKK, pBC = pf.next(), pf.next()
                for hh, h in enumerate(hs):
                    P.op("pe", "matmul", [KTt], [pKK], pKK[:, hh, :], lhsT=KTt[:, h, :], rhs=KTt[:, h, :], start=True, stop=True)
                for hh, h in enumerate(hs):
                    P.op("pe", "matmul", [onesf, dg], [pBC], pBC[:, hh, :], lhsT=onesf[:], rhs=dg[:, h, :], start=True, stop=False)
                    P.op("pe", "matmul", [identf, negm], [pBC], pBC[:, hh, :], lhsT=identf[:], rhs=negm[:, d, :], start=False, stop=True)
                E = E_r.next()
                for hh, h in enumerate(hs):
                    P.op("act", "activation", [pBC, ngc], [E], out=E[:, hh, :], in_=pBC[:, hh, :], func=AF.Exp,
                         bias=ngc[:, t, d8 + h:d8 + h + 1])
                AT = AT_r.next()
                for hh, h in enumerate(hs):
                    P.op("dve", "scalar_tensor_tensor", [pKK, beta, E], [AT], out=AT[:, hh, :], in0=pKK[:, hh, :],
                         scalar=beta[:, t, d8 + h:d8 + h + 1], in1=E[:, hh, :], op0=ALU.mult, op1=ALU.mult)
                NAT = nat_r.next()
                P.op("pool", "tensor_tensor", [AT, lvl], [NAT], out=NAT[:],
                     in0=AT[:].unsqueeze(2).broadcast_to([128, 4, 7, 128]),
                     in1=lvl[:, d, :, :].unsqueeze(1).broadcast_to([128, 4, 7, 128]), op=ALU.mult)
                st_[G].update(E=E, NAT=NAT)
            for G in GR:
                hs = hsl[G]
                gsl = gsls[G]
                E = st_[G]["E"]
                pQK, pEG = pf.next(), pf.next()
                for hh, h in enumerate(hs):
                    P.op("pe", "matmul", [KTt, QTt], [pQK], pQK[:, hh, :], lhsT=KTt[:, h, :], rhs=QTt[:, h, :], start=True, stop=True)
                for hh, h in enumerate(hs):
                    P.op("pe", "matmul", [onesf, de], [pEG], pEG[:, hh, :], lhsT=onesf[:], rhs=de[:, h, :], start=True, stop=True)
                attnT, KGT, QGT = attn_r.next(), kgt_r.next(), qgt_r.next()
                P.op("dve", "tensor_tensor", [pQK, E], [attnT], out=attnT[:], in0=pQK[:], in1=E[:], op=ALU.mult)
                P.op("dve", "tensor_tensor", [KTt, pEG], [KGT], out=KGT[:], in0=KTt[:, gsl, :], in1=pEG[:], op=ALU.mult)
                P.op("dve", "tensor_tensor", [QTt, pEG], [QGT], out=QGT[:], in0=QTt[:, gsl, :], in1=pEG[:], op=ALU.mult)
                st_[G].update(attnT=attnT, KGT=KGT, QGT=QGT)
            for G in GR:
                NAT = st_[G]["NAT"]
                pP1 = pf.next()
                for hh in range(4):
                    P.op("pe", "matmul", [NAT, identb], [pP1], pP1[:, hh, :], lhsT=NAT[:, hh, 0, :], rhs=identb[:], start=True, stop=True)
                Dm, DT = D_r.next(), DT_r.next()
                P.op("dve", "tensor_tensor", [identb, pP1], [Dm], out=Dm[:], in0=identb_bc, in1=pP1[:], op=ALU.add)
                P.op("pool", "tensor_tensor", [identb, NAT], [DT], out=DT[:], in0=identb_bc, in1=NAT[:, :, 0, :], op=ALU.add)
                st_[G].update(Dm=Dm, DT=DT)
            for lv in range(1, 7):
                for G in GR:
                    NAT, Dm = st_[G]["NAT"], st_[G]["Dm"]
                    pP1 = pf.next()
                    for hh in range(4):
                        P.op("pe", "matmul", [NAT, Dm], [pP1], pP1[:, hh, :], lhsT=NAT[:, hh, lv, :], rhs=Dm[:, hh, :], start=True, stop=True)
                    P1s = p1_r.next()
                    P.op("act", "copy", [pP1], [P1s], out=P1s[:], in_=pP1[:])
                    st_[G]["P1s"] = P1s
                for G in GR:
                    Dm, DT, P1s = st_[G]["Dm"], st_[G]["DT"], st_[G]["P1s"]
                    if lv < 6:
                        pY = pf.next()
                        for hh in range(4):
                            P.op("pe", "matmul", [DT, P1s], [pY], pY[:, hh, :], lhsT=DT[:, hh, :], rhs=P1s[:, hh, :], start=True, stop=True)
                    pYT = pf.next()
                    for hh in range(4):
                        P.op("pe", "matmul", [P1s, DT], [pYT], pYT[:, hh, :], lhsT=P1s[:, hh, :], rhs=DT[:, hh, :], start=True, stop=True)
                    if lv < 6:
                        Dn = D_r.next()
                        P.op("dve", "tensor_tensor", [Dm, pY], [Dn], out=Dn[:], in0=Dm[:], in1=pY[:], op=ALU.add)
                        st_[G]["Dm"] = Dn
                    DTn = DT_r.next() if lv < 6 else TT_r.next()
                    P.op("dve", "tensor_tensor", [DT, pYT], [DTn], out=DTn[:], in0=DT[:], in1=pYT[:], op=ALU.add)
                    st_[G]["DT"] = DTn
            yield
            for G in GR:
                hs, gsl = hsl[G], gsls[G]
                KGT = st_[G]["KGT"]
                pR = pf.next()
                for hh, h in enumerate(hs):
                    P.op("pe", "matmul", [KGT, Sb], [pR], pR[:, hh, :], lhsT=KGT[:, hh, :], rhs=Sb[:, h, :], start=True, stop=True)
                Rt = R_r.next()
                P.op("dve", "tensor_tensor", [vtok, pR], [Rt], out=Rt[:], in0=vtok[:, gsl, :], in1=pR[:], op=ALU.subtract)
                st_[G]["Rt"] = Rt
            for G in GR:
                TT, Rt = st_[G]["DT"], st_[G]["Rt"]
                pVN = pf.next()
                for hh in range(4):
                    P.op("pe", "matmul", [TT, Rt], [pVN], pVN[:, hh, :], lhsT=TT[:, hh, :], rhs=Rt[:, hh, :], start=True, stop=True)
                VN = VN_r.next()
                P.op("dve", "tensor_tensor", [pVN, beta], [VN], out=VN[:], in0=pVN[:],
                     in1=beta[:, t, d8 + G * 4:d8 + G * 4 + 4].unsqueeze(2).broadcast_to([128, 4, 128]), op=ALU.mult)
                st_[G]["VN"] = VN
            for G in GR:
                hs, gsl = hsl[G], gsls[G]
                QGT, attnT, VN = st_[G]["QGT"], st_[G]["attnT"], st_[G]["VN"]
                pO = pf.next()
                for hh, h in enumerate(hs):
                    P.op("pe", "matmul", [QGT, Sb], [pO], pO[:, hh, :], lhsT=QGT[:, hh, :], rhs=Sb[:, h, :], start=True, stop=False)
                    P.op("pe", "matmul", [attnT, VN], [pO], pO[:, hh, :], lhsT=attnT[:, hh, :], rhs=VN[:, hh, :], start=False, stop=True)
                pS = pf.next()
                for hh, h in enumerate(hs):
                    P.op("pe", "matmul", [kdec, VN], [pS], pS[:, hh, :], lhsT=kdec[:, h, :], rhs=VN[:, hh, :], start=True, stop=True)
                P.op("dve", "tensor_tensor", [S32, egl], [S32], out=S32[:, gsl, :], in0=S32[:, gsl, :],
                     in1=egl[:, t, d8 + G * 4:d8 + G * 4 + 4].unsqueeze(2).broadcast_to([128, 4, 128]), op=ALU.mult)
                P.op("dve", "tensor_tensor", [S32, pS], [S32], out=S32[:, gsl, :], in0=S32[:, gsl, :], in1=pS[:], op=ALU.add)
                P.op("act", "copy", [S32], [Sb], out=Sb[:, gsl, :], in_=S32[:, gsl, :])
                if d == 0:
                    P.op("act", "copy", [pO], [Ot], out=Ot[:, gsl, :], in_=pO[:])
                else:
                    osum, osq, on, ost = osum_r.next(), osq_r.next(), on_r.next(), ost_r.next()
                    P.op("dve", "tensor_tensor", [pO, OFt], [osum], out=osum[:], in0=pO[:], in1=OFt[:, gsl, :], op=ALU.add)
                    P.op("pool", "tensor_tensor", [osum], [osq], out=osq[:], in0=osum[:], in1=osum[:], op=ALU.mult)
                    P.op("dve", "tensor_reduce", [osq], [ost], out=ost[:, :, 0], in_=osq[:], axis=AX.X, op=ALU.add)
                    P.op("dve", "tensor_scalar", [ost], [ost], out=ost[:, :, 1], in0=ost[:, :, 0], scalar1=1.0 / 128, scalar2=EPS,
                         op0=ALU.mult, op1=ALU.add)
                    P.op("act", "activation", [ost], [ost], out=ost[:, :, 2], in_=ost[:, :, 1], func=AF.Sqrt)
                    P.op("dve", "reciprocal", [ost], [ost], out=ost[:, :, 3], in_=ost[:, :, 2])
                    P.op("dve", "tensor_tensor", [osum, ost], [on], out=on[:], in0=osum[:],
                         in1=ost[:, :, 3:4].broadcast_to([128, 4, 128]), op=ALU.mult)
                    pT = pf.next()
                    for hh in range(4):
                        P.op("pe", "matmul", [on, identf], [pT], pT[:, hh, :], lhsT=on[:, hh, :], rhs=identf[:], start=True, stop=True)
                    P.op("dve", "scalar_tensor_tensor", [pT, onorm, GAt], [OAt], out=OAt[:, gsl, :], in0=pT[:], scalar=onorm[:, 0:1],
                         in1=GAt[:, gsl, :], op0=ALU.mult, op1=ALU.mult)
            if d == 0:
                P.dma("pool", OF[tok, :].rearrange("p (h e) -> p h e", h=8), Ot[:], reads=[Ot])
            else:
                P.dma("pool", OA_T.rearrange("(h p) t -> p h t", p=128)[:, :, tok], OAt[:], reads=[OAt])
        prev_g = None
        for t in order:
            g_ = tile_gen(t)
            next(g_)
            if prev_g is not None:
                for _ in prev_g:
                    pass
            prev_g = g_
        if prev_g is not None:
            for _ in prev_g:
                pass
        if d == 0:
            P.barrier()
    P.end()
    if _STOP == 3:
        nc._P = P
        return nc


    P.begin()
    onesb = P.sb([128, 128], BF16)
    P.op("dve", "memset", [], [onesb], onesb[:], 1.0)
    kb = P.sb([128, LMAX // 128], F32)
    P.dma("sp", kb[:], kbias, writes=[kb])
    kpe = P.sb([128, L], BF16)
    P.op("pool", "memset", [], [kpe], kpe[64:128, :], 0.0)
    P.dma("sp", kpe[0:64, :], KPE_T, writes=[kpe])
    kn_r = P.ring(2, [128, L], BF16)
    vh_r = P.ring(2, [128, NT, 128], BF16)
    qn_r = P.ring(2, [128, 512], BF16)
    qp_r = P.ring(2, [128, 512], BF16)
    for b_ in qp_r.bufs:
        P.op("pool", "memset", [], [b_], b_[64:128, :], 0.0)
    gb_r = P.ring(2, [128, 512], BF16)
    pt_r = P.ring(4, [128, 512], BF16)
    pS_r = P.ring(4, [128, 512], F32, psum=True)
    pO_r = P.ring(2, [128, 512], F32, psum=True)
    pZ_r = P.ring(2, [128, 512], F32, psum=True)
    rs_r = P.ring(2, [128, 512], F32)
    o1_r = P.ring(2, [128, 512], F32)
    ob_r = P.ring(2, [128, 512], BF16)
    sm_scale = float((D_NOPE + D_ROPE) ** -0.5)
    for h in range(8):
        knh, vh = kn_r.next(), vh_r.next()
        P.dma("sp", knh[:], KN_T[h * 128:(h + 1) * 128, :], writes=[knh])
        for t0_ in range(0, NT, 8):
            t1_ = min(NT, t0_ + 8)
            P.dma("act", vh[:, t0_:t1_, :], VB[t0_ * 128:t1_ * 128, h * 128:(h + 1) * 128].rearrange("(t p) e -> p t e", p=128), writes=[vh])
        for qb in range(NB):
            blk = slice(qb * 512, (qb + 1) * 512)
            qn, qp, gb = qn_r.next(), qp_r.next(), gb_r.next()
            P.dma("sp", qn[:], QF_T[h * 192:h * 192 + 128, blk], writes=[qn])
            P.dma("sp", qp[0:64, :], QF_T[h * 192 + 128:h * 192 + 192, blk], writes=[qp])
            P.dma("act", gb[:], GB_T[h * 128:(h + 1) * 128, blk], writes=[gb])
            pO, pZ = pO_r.next(), pZ_r.next()

            def scores(kt):
                ps_ = pS_r.next()
                P.op("pe", "matmul", [knh, qn], [ps_], ps_[:], lhsT=knh[:, kt * 128:(kt + 1) * 128], rhs=qn[:], start=True, stop=False)
                P.op("pe", "matmul", [kpe, qp], [ps_], ps_[:], lhsT=kpe[:, kt * 128:(kt + 1) * 128], rhs=qp[:], start=False, stop=True)
                return ps_
            pend = [scores(0)]
            if NT > 1:
                pend.append(scores(1))
            for kt in range(NT):
                cur = pend.pop(0)
                if kt + 2 < NT:
                    pend.append(scores(kt + 2))
                pt_ = pt_r.next()
                P.op("act", "activation", [cur, kb], [pt_], out=pt_[:], in_=cur[:], func=AF.Exp, bias=kb[:, kt:kt + 1], scale=sm_scale)
                P.op("pe", "matmul", [onesb, pt_], [pZ], pZ[:], lhsT=onesb[:], rhs=pt_[:], start=(kt == 0), stop=(kt == NT - 1))
                P.op("pe", "matmul", [vh, pt_], [pO], pO[:], lhsT=vh[:, kt, :], rhs=pt_[:], start=(kt == 0), stop=(kt == NT - 1))
            rs, o1, ob = rs_r.next(), o1_r.next(), ob_r.next()
            P.op("dve", "reciprocal", [pZ], [rs], out=rs[:], in_=pZ[:])
            P.op("dve", "tensor_tensor", [pO, rs], [o1], out=o1[:], in0=pO[:], in1=rs[:], op=ALU.mult)
            P.op("pool", "tensor_tensor", [o1, gb], [ob], out=ob[:], in0=o1[:], in1=gb[:], op=ALU.mult)
            P.dma("pool", OB_T[h * 128:(h + 1) * 128, blk], ob[:], reads=[ob])
    P.end()
    if _STOP == 5:
        nc._P = P
        return nc

    M_T = dscr("M_T", [NT, 128, 16, 128], BF16)
    P.begin()
    wpa = P.sb([128, 8, 2048], BF16)
    wpb = P.sb([128, 8, 2048], BF16)
    P.dma("pool", wpa[:], w_pa.rearrange("(c p) n -> p c n", p=128), writes=[wpa])
    P.dma("pool", wpb[:], w_pb.rearrange("(c p) n -> p c n", p=128), writes=[wpb])
    oa_r2 = P.ring(2, [128, 8, 512], BF16)
    ob_r2 = P.ring(2, [128, 8, 512], BF16)
    gm_r = P.ring(3, [128, 2, 512], BF16)
    m_r = P.ring(4, [128, 512], F32)
    mt_r = P.ring(3, [128, 512], BF16)
    pm_r = P.ring(6, [128, 512], F32, psum=True)
    for tb in range(NB):
        blk = slice(tb * 512, (tb + 1) * 512)
        oat, obt = oa_r2.next(), ob_r2.next()
        P.dma("sp", oat[:], OA_T.rearrange("(c p) t -> p c t", p=128)[:, :, blk], writes=[oat])
        P.dma("act", obt[:], OB_T.rearrange("(c p) t -> p c t", p=128)[:, :, blk], writes=[obt])
        for ct in range(16):
            gm = gm_r.next()
            P.dma("sp", gm[:, 0, :], GMA_T[ct * 128:(ct + 1) * 128, blk], writes=[gm])
            P.dma("act", gm[:, 1, :], GMB_T[ct * 128:(ct + 1) * 128, blk], writes=[gm])
            pA, pB = pm_r.next(), pm_r.next()
            for c in range(8):
                P.op("pe", "matmul", [wpa, oat], [pA], pA[:], lhsT=wpa[:, c, ct * 128:(ct + 1) * 128], rhs=oat[:, c, :], start=(c == 0), stop=(c == 7))
            for c in range(8):
                P.op("pe", "matmul", [wpb, obt], [pB], pB[:], lhsT=wpb[:, c, ct * 128:(ct + 1) * 128], rhs=obt[:, c, :], start=(c == 0), stop=(c == 7))
            m1, m2, mt = m_r.next(), m_r.next(), mt_r.next()
            P.op("dve", "tensor_tensor", [pA, gm], [m1], out=m1[:], in0=pA[:], in1=gm[:, 0, :], op=ALU.mult)
            P.op("dve", "tensor_tensor", [pB, gm], [m2], out=m2[:], in0=pB[:], in1=gm[:, 1, :], op=ALU.mult)
            P.op("pool", "tensor_tensor", [m1, m2], [mt], out=mt[:], in0=m1[:], in1=m2[:], op=ALU.add)
            P.dma("pool", M_T[tb * 4:tb * 4 + 4, :, ct, :].rearrange("j p t -> p j t"), mt[:].rearrange("p (j t) -> p j t", t=128), reads=[mt])
    P.end()
    if _STOP == 6:
        nc._P = P
        return nc

    P.begin()
    wout = P.sb([128, 16, 2048], BF16)
    P.dma("pool", wout[:], w_out.rearrange("(c p) n -> p c n", p=128), writes=[wout])
    nfb = P.sb([128, D_MODEL], F32)
    P.dma("sp", nfb[:], norm_f.partition_broadcast(128), writes=[nfb])
    mt_r2 = P.ring(2, [128, 16, 128], BF16)
    x_r = P.ring(2, [128, D_MODEL], F32)
    z_r = P.ring(2, [128, D_MODEL], F32)
    y_r = P.ring(2, [128, D_MODEL], F32)
    junk = P.sb([128, D_MODEL], BF16)
    st_r2 = P.ring(4, [128, 4], F32)
    pm_r = P.ring(6, [128, 512], F32, psum=True)
    for tt in range(NT):
        tok = slice(tt * 128, (tt + 1) * 128)
        mtt, xt = mt_r2.next(), x_r.next()
        P.dma("sp", mtt[:], M_T[tt], writes=[mtt])
        P.dma("act", xt[:], x[tok, :], writes=[xt])
        z = z_r.next()
        for cg in range(4):
            py_ = pm_r.next()
            for c in range(16):
                P.op("pe", "matmul", [mtt, wout], [py_], py_[:], lhsT=mtt[:, c, :], rhs=wout[:, c, cg * 512:(cg + 1) * 512], start=(c == 0), stop=(c == 15))
            P.op("dve", "tensor_tensor", [py_, xt], [z], out=z[:, cg * 512:(cg + 1) * 512], in0=py_[:], in1=xt[:, cg * 512:(cg + 1) * 512], op=ALU.add)
        st = st_r2.next()
        P.op("act", "activation", [z], [junk, st], out=junk[:], in_=z[:], func=AF.Square, accum_out=st[:, 0:1])
        P.op("dve", "tensor_scalar", [st], [st], out=st[:, 1:2], in0=st[:, 0:1], scalar1=1.0 / D_MODEL, scalar2=EPS, op0=ALU.mult, op1=ALU.add)
        P.op("act", "activation", [st], [st], out=st[:, 2:3], in_=st[:, 1:2], func=AF.Sqrt)
        P.op("dve", "reciprocal", [st], [st], out=st[:, 3:4], in_=st[:, 2:3])
        yv = y_r.next()
        P.op("dve", "scalar_tensor_tensor", [z, st, nfb], [yv], out=yv[:], in0=z[:], scalar=st[:, 3:4], in1=nfb[:], op0=ALU.mult, op1=ALU.mult)
        P.dma("pool", y[tok, :], yv[:], reads=[yv])
    P.end()
    if _STOP == 7:
        nc._P = P
        return nc

    nc._P = P
    return nc


_NC_CACHE = {}


def _core_map(consts, shared, xseq, L):
    valid = xseq.shape[0]
    xp = np.zeros((L, D_MODEL), np.float32)
    xp[:valid] = xseq
    kb = np.zeros((LMAX,), np.float32)
    kb[valid:] = -BIG
    m = dict(shared)
    m.update(consts)
    m["x"] = xp
    m["kbias"] = np.ascontiguousarray(kb.reshape(LMAX // 128, 128).T)
    tm = np.zeros((128, LMAX), np.float32)
    tm[:, :valid] = 1.0
    m["tmask"] = tm
    return m


def kernel(x_prompt, x_sample, norm_in, w_in, conv_w, a_log_f, dt_bias_f, a_log_b, dt_bias_b, o_norm_a,
           q_a_norm, w_q_b, kv_a_norm, w_kv_b, w_pa, w_pb, w_out, norm_f):
    f = lambda a: np.ascontiguousarray(np.asarray(a, dtype=np.float32))
    x_prompt, x_sample = f(x_prompt), f(x_sample)
    L = LMAX
    shared = {
        "norm_in": f(norm_in)[0], "w_in": f(w_in)[0],
        "conv_w": np.ascontiguousarray(f(conv_w)[0].reshape(KCONV * 24, 128)),
        "a_log": np.concatenate([f(a_log_f)[0], f(a_log_b)[0]]),
        "dt_bias": np.concatenate([f(dt_bias_f)[0], f(dt_bias_b)[0]]),
        "o_norm_a": f(o_norm_a)[0], "q_a_norm": f(q_a_norm)[0], "w_q_b": f(w_q_b)[0],
        "kv_a_norm": f(kv_a_norm)[0], "w_kv_b": f(w_kv_b)[0], "w_pa": f(w_pa)[0], "w_pb": f(w_pb)[0],
        "w_out": f(w_out)[0], "norm_f": f(norm_f),
    }
    consts = host_consts()
    seqs = [x_prompt[i] for i in range(4)] + [x_sample[i] for i in range(4)]
    in_maps = [_core_map(consts, shared, s, L) for s in seqs]
    if L not in _NC_CACHE:
        _NC_CACHE[L] = build(L)
    nc = _NC_CACHE[L]
    res = run_bass_kernel_spmd(nc, in_maps, core_ids=list(range(8)))
    ys = [np.asarray(r["y"], dtype=np.float32) for r in res.results]
    y_prompt = np.stack([ys[i][:x_prompt.shape[1]] for i in range(4)], axis=0)
    y_sample = np.stack([ys[4 + i] for i in range(4)], axis=0)
    return (y_prompt, y_sample)
```

```python
import math
from contextlib import ExitStack

import numpy as np
import ml_dtypes
import concourse.bass as bass
import concourse.mybir as mybir
from concourse.bass_utils import run_bass_kernel_spmd

F32 = mybir.dt.float32
BF16 = mybir.dt.bfloat16
AF = mybir.ActivationFunctionType
ALU = mybir.AluOpType
AX = mybir.AxisListType

D_MODEL = 2048
H = 8
DK = 128
CONV_DIM = 3072
KCONV = 5
Q_LORA = 1536
KV_LORA = 512
D_ROPE = 64
D_NOPE = 128
N_IN = 11360
EPS = 1e-6
LMAX = 4096
BIG = 30000.0

C_QKV = 0
C_GA = 3072
C_SM = 4096
C_QL = 4128
C_KVL = 5664
C_KR = 6176
C_GB = 6240
C_GMA = 7264
C_GMB = 9312

ENGS = ("sp", "act", "pe", "dve", "pool")
EPOCH = 12000


class Buf:
    __slots__ = ("t", "lw", "rd", "dsem", "dcnt", "name", "chain")

    def __init__(self, t, name):
        self.t = t
        self.name = name
        self.lw = None
        self.rd = []
        self.dsem = None
        self.dcnt = 0
        self.chain = None

    def __getitem__(self, idx):
        return self.t[idx]


class Ring:
    def __init__(self, bufs):
        self.bufs = bufs
        self.i = 0

    def next(self):
        b = self.bufs[self.i % len(self.bufs)]
        self.i += 1
        return b


class Prog:
    def __init__(self, nc):
        self.nc = nc
        self.ops = {e: [] for e in ENGS}
        self.cnt = {e: 0 for e in ENGS}
        self.sems = {}
        self.seen = {e: {} for e in ENGS}
        self.dma_bufs = []
        self.nbuf = 0
        self.stack = None
        self.ndsem = 0
        self.free_dsems = {"d": [], "w": []}
        self.mute = False
        self.phase_no = 0
        self.only = 0
        self.mute_set = set()

    def sb(self, shape, dtype=F32, name=None):
        self.nbuf += 1
        name = name or f"sb{self.nbuf}"
        t = self.stack.enter_context(self.nc.sbuf_tensor(name, list(shape), dtype))
        return Buf(t, name)

    def ps(self, shape, dtype=F32, name=None):
        self.nbuf += 1
        name = name or f"ps{self.nbuf}"
        t = self.stack.enter_context(self.nc.psum_tensor(name, list(shape), dtype))
        return Buf(t, name)

    def ring(self, n, shape, dtype=F32, psum=False):
        return Ring([(self.ps if psum else self.sb)(shape, dtype) for _ in range(n)])

    def _sem(self, key):
        if key not in self.sems:
            self.sems[key] = self.nc.alloc_semaphore("s_" + "_".join(str(k) for k in key))
        return self.sems[key]

    def _engkey(self, e, n):
        ep = (n - 1) // EPOCH
        return (("e", e, ep), n - ep * EPOCH)

    def _deps(self, eng, reads, writes, skip_key=None, is_dma=False):
        deps = {}

        def add(ev, kind):
            if ev is None:
                return
            key, val, src = ev
            if kind == "waw" and skip_key is not None and key == skip_key:
                return
            if src == eng and not is_dma:
                if eng == "pe":
                    return
                if kind != "raw":
                    return
            if deps.get(key, 0) < val:
                deps[key] = val
        for b in reads:
            add(b.lw, "raw")
        for b in writes:
            add(b.lw, "waw")
            for r in b.rd:
                add(r, "war")
        return deps

    def _prune(self, eng, deps):
        waits = []
        seen = self.seen[eng]
        for key, val in deps.items():
            if seen.get(key, 0) >= val:
                continue
            seen[key] = val
            waits.append((key, val))
        return waits

    def _collect(self, eng, reads, writes):
        return self._prune(eng, self._deps(eng, reads, writes))

    def op(self, eng, meth, reads, writes, *args, **kw):
        if self.mute:
            return None
        waits = self._collect(eng, reads, writes)
        self.cnt[eng] += 1
        key, val = self._engkey(eng, self.cnt[eng])
        self._sem(key)
        self.ops[eng].append((waits, meth, args, kw, (key, 1)))
        ev = (key, val, eng)
        for b in writes:
            b.lw = ev
            b.rd = []
            b.chain = None
        for b in reads:
            if b not in writes:
                b.rd.append(ev)
        return ev

    def dma(self, eng, out_ap, in_ap, reads=(), writes=(), owner=None, **kw):
        if self.mute:
            return None
        if owner is None:
            owner = (list(writes) + list(reads))[0]
        kind = "w" if eng == "pool" else "d"
        st = owner.dsem
        if st is None:
            st = owner.dsem = {}
        if kind not in st:
            fl = self.free_dsems[kind]
            if fl:
                st[kind] = list(fl.pop())
            else:
                self.ndsem += 1
                key = (kind, self.ndsem)
                self._sem(key)
                st[kind] = [key, 0]
            self.dma_bufs.append((owner, kind))
        deps = self._deps(eng, reads, writes, skip_key=st[kind][0], is_dma=True)
        for b in writes:
            if b.lw is not None and b.lw[0] == st[kind][0] and b.chain:
                for k_, v_ in b.chain.items():
                    if deps.get(k_, 0) < v_:
                        deps[k_] = v_
        for b in writes:
            b.chain = dict(deps)
        waits = self._prune(eng, deps)
        st[kind][1] += 16
        key = st[kind][0]
        ev = (key, st[kind][1], "dma")
        kw = dict(kw)
        kw["out"] = out_ap
        kw["in_"] = in_ap
        self.ops[eng].append((waits, "dma_start", (), kw, (key, 16)))
        for b in writes:
            b.lw = ev
            b.rd = []
        for b in reads:
            b.rd.append(ev)
        return ev

    def barrier(self):
        evs = []
        for e in ENGS:
            if self.cnt[e] > 0:
                k, v = self._engkey(e, self.cnt[e])
                evs.append((k, v))
        for b, kind in self.dma_bufs:
            evs.append(tuple(b.dsem[kind]))
        for e in ENGS:
            waits = []
            for k, v in evs:
                if k[0] == "e" and k[1] == e:
                    continue
                if self.seen[e].get(k, 0) >= v:
                    continue
                self.seen[e][k] = v
                waits.append((k, v))
            if waits:
                self.ops[e].append((waits, None, None, None, None))

    def flush(self):
        nc = self.nc
        engobj = {"sp": "sync", "act": "scalar", "pe": "tensor", "dve": "vector", "pool": "gpsimd"}
        with nc.Block() as block:
            for e in ENGS:
                lst = self.ops[e]

                def body(engine, lst=lst):
                    for waits, meth, args, kw, inc in lst:
                        for k, v in waits:
                            engine.wait_ge(self.sems[k], v)
                        if meth is not None:
                            ins = getattr(engine, meth)(*args, **kw)
                            ins.then_inc(self.sems[inc[0]], inc[1])
                getattr(block, engobj[e])(body)
        self.ops = {e: [] for e in ENGS}

    def begin(self):
        self.stack = ExitStack()
        self.phase_no += 1
        self.mute = (bool(self.only) and self.phase_no != self.only) or (self.phase_no in self.mute_set)

    def end(self):
        self.barrier()
        self.flush()
        for b, kind in self.dma_bufs:
            self.free_dsems[kind].append(tuple(b.dsem[kind]))
        self.dma_bufs = []
        self.stack.close()
        self.stack = None


def host_consts():
    c = {}
    c["ident"] = np.eye(128, dtype=np.float32)
    j = np.arange(128)[:, None]
    i = np.arange(128)[None, :]
    tri = np.zeros((2, 128, 128), np.float32)
    tri[0] = (j <= i)
    tri[1] = (j >= i)
    c["tri"] = tri
    nm = np.zeros((2, 128, 128), np.float32)
    nm[0] = np.where(i >= j, 0.0, -BIG)
    nm[1] = np.where(i <= j, 0.0, -BIG)
    c["negmask"] = nm
    lv = np.zeros((2, 128, 7, 128), np.float32)
    for si, s in enumerate([1, 2, 4, 8, 16, 32, 64]):
        same2 = (i // (2 * s)) == (j // (2 * s))
        diff1 = (i // s) != (j // s)
        lv[0, :, si, :] = -1.0 * (same2 & diff1 & (i > j))
        lv[1, :, si, :] = -1.0 * (same2 & diff1 & (i < j))
    c["lvl"] = lv
    pos = np.arange(LMAX, dtype=np.float32)
    inv_freq = (10000.0 ** (-np.arange(0, D_ROPE, 2, dtype=np.float32) / D_ROPE)).astype(np.float32)
    ang = pos[None, :] * inv_freq[:, None]
    c["cos4"] = np.tile(np.cos(ang).astype(np.float32), (4, 1))
    c["sin4"] = np.tile(np.sin(ang).astype(np.float32), (4, 1))
    return c


def build(L, dbg=(), _STOP=0):
    nc = bass.Bass("TRN2", target_bir_lowering=False)
    NT = L // 128
    NB = L // 512
    PART = min(L, 2048)
    NPART = L // PART

    def din(name, shape, dt=F32):
        return nc.dram_tensor(name, list(shape), dt, kind="ExternalInput").ap()

    def dscr(name, shape, dt=F32):
        kind = "ExternalOutput" if name in dbg else "Internal"
        return nc.dram_tensor(name, list(shape), dt, kind=kind).ap()

    x = din("x", [L, D_MODEL])
    norm_in = din("norm_in", [D_MODEL])
    w_in = din("w_in", [D_MODEL, N_IN])
    conv_w = din("conv_w", [KCONV * 24, 128])
    a_log = din("a_log", [16])
    dt_bias = din("dt_bias", [16])
    o_norm_a = din("o_norm_a", [128])
    q_a_norm = din("q_a_norm", [Q_LORA])
    w_q_b = din("w_q_b", [Q_LORA, 1536])
    kv_a_norm = din("kv_a_norm", [KV_LORA])
    w_kv_b = din("w_kv_b", [KV_LORA, 2048])
    w_pa = din("w_pa", [1024, D_MODEL])
    w_pb = din("w_pb", [1024, D_MODEL])
    w_out = din("w_out", [D_MODEL, D_MODEL])
    norm_f = din("norm_f", [D_MODEL])
    kbias = din("kbias", [128, LMAX // 128])
    tmask_d = din("tmask", [128, LMAX])
    c_ident = din("ident", [128, 128])
    c_tri = din("tri", [2, 128, 128])
    c_negmask = din("negmask", [2, 128, 128])
    c_lvl = din("lvl", [2, 128, 7, 128])
    c_cos4 = din("cos4", [128, LMAX])
    c_sin4 = din("sin4", [128, LMAX])

    y = nc.dram_tensor("y", [L, D_MODEL], F32, kind="ExternalOutput").ap()

    QF_T = dscr("QF_T", [8 * 192, L], BF16)
    KN_T = dscr("KN_T", [1024, L], BF16)
    KPE_T = dscr("KPE_T", [64, L], BF16)
    VB = dscr("VB", [L, 1024], BF16)
    OB_T = dscr("OB_T", [1024, L], BF16)
    QKV_T = dscr("QKV_T", [CONV_DIM, L], BF16)
    GA_T = dscr("GA_T", [1024, L], BF16)
    GB_T = dscr("GB_T", [1024, L], BF16)
    GMA_T = dscr("GMA_T", [2048, L], BF16)
    GMB_T = dscr("GMB_T", [2048, L], BF16)
    QL_T = dscr("QL_T", [Q_LORA, L])
    KVL_T = dscr("KVL_T", [KV_LORA, L])
    KR_T = dscr("KR_T", [64, L])
    AB = dscr("AB", [L, 32])
    QT = dscr("QT", [1024, L], BF16)
    KT = dscr("KT", [1024, L], BF16)
    VT = dscr("VT", [1024, L], BF16)
    OA_T = dscr("OA_T", [1024, L], BF16)

    P = Prog(nc)

    P.begin()
    identf = P.sb([128, 128], F32)
    identb = P.sb([128, 128], BF16)
    gbc = P.sb([128, D_MODEL], F32)
    P.dma("sp", identf[:], c_ident, writes=[identf])
    P.dma("sp", gbc[:], norm_in.partition_broadcast(128), writes=[gbc])
    P.op("dve", "tensor_copy", [identf], [identb], out=identb[:], in_=identf[:])
    xnT = P.sb([128, 16, PART], BF16)
    xring = P.ring(2, [128, D_MODEL], F32)
    xsring = P.ring(2, [128, D_MODEL], BF16)
    junk = P.sb([128, D_MODEL], BF16)
    stat = P.ring(4, [128, 4], F32)
    ptr = P.ring(2, [128, 8, 128], BF16, psum=True)
    pmm = P.ring(4, [128, 512], F32, psum=True)
    wring = P.ring(2, [128, 16, 512], BF16)
    oring = P.ring(3, [128, PART], F32)
    oringb = P.ring(2, [128, PART], BF16)
    wsm = P.sb([128, 16, 32], BF16)
    absg = P.sb([128, PART // 128, 32], F32)
    P.dma("pool", wsm[:], w_in[:, C_SM:C_SM + 32].rearrange("(c p) n -> p c n", p=128), writes=[wsm])

    groups = []

    def add_group(c0, n, dest, func):
        o = 0
        while o < n:
            w = min(512, n - o)
            groups.append((c0 + o, w, dest, o, func))
            o += w
    add_group(C_QKV, 3072, QKV_T, None)
    add_group(C_GA, 1024, GA_T, AF.Silu)
    add_group(C_QL, Q_LORA, QL_T, None)
    add_group(C_KVL, KV_LORA, KVL_T, None)
    add_group(C_KR, 64, KR_T, None)
    add_group(C_GB, 1024, GB_T, AF.Silu)
    add_group(C_GMA, 2048, GMA_T, AF.Sigmoid)
    add_group(C_GMB, 2048, GMB_T, AF.Sigmoid)
    groups.sort(key=lambda g: {None: 0, AF.Silu: 1, AF.Sigmoid: 2}[g[4]])

    evq = 0
    for part in range(NPART):
        t0 = part * PART
        for tt in range(PART // 128):
            xt = xring.next()
            P.dma("sp", xt[:], x[t0 + tt * 128: t0 + (tt + 1) * 128, :], writes=[xt])
            st = stat.next()
            P.op("act", "activation", [xt], [junk, st], out=junk[:], in_=xt[:], func=AF.Square, accum_out=st[:, 0:1])
            P.op("dve", "tensor_scalar", [st], [st], out=st[:, 1:2], in0=st[:, 0:1], scalar1=1.0 / D_MODEL, scalar2=EPS,
                 op0=ALU.mult, op1=ALU.add)
            P.op("act", "activation", [st], [st], out=st[:, 2:3], in_=st[:, 1:2], func=AF.Sqrt)
            P.op("dve", "reciprocal", [st], [st], out=st[:, 3:4], in_=st[:, 2:3])
            xs = xsring.next()
            P.op("dve", "scalar_tensor_tensor", [xt, st, gbc], [xs], out=xs[:], in0=xt[:], scalar=st[:, 3:4], in1=gbc[:],
                 op0=ALU.mult, op1=ALU.mult)
            for hh in range(2):
                pt = ptr.next()
                for c in range(8):
                    cc = hh * 8 + c
                    P.op("pe", "transpose", [xs, identb], [pt], out=pt[:, c, :], in_=xs[:, cc * 128:(cc + 1) * 128],
                         identity=identb[:])
                if hh == 0:
                    P.op("act", "copy", [pt], [xnT], out=xnT[:, 0:8, tt * 128:(tt + 1) * 128], in_=pt[:])
                else:
                    P.op("dve", "tensor_copy", [pt], [xnT], out=xnT[:, 8:16, tt * 128:(tt + 1) * 128], in_=pt[:])
        for tt in range(PART // 128):
            pm = pmm.next()
            for c in range(16):
                P.op("pe", "matmul", [xnT, wsm], [pm], pm[:, 0:32], lhsT=xnT[:, c, tt * 128:(tt + 1) * 128], rhs=wsm[:, c, :],
                     start=(c == 0), stop=(c == 15))
            P.op("dve", "tensor_copy", [pm], [absg], out=absg[:, tt, :], in_=pm[:, 0:32])
        for a0 in range(0, PART // 128, 8):
            a1 = min(PART // 128, a0 + 8)
            P.dma("sp", AB[t0 + a0 * 128:t0 + a1 * 128, :].rearrange("(t p) c -> p t c", p=128), absg[:, a0:a1, :], reads=[absg])
        for (c0, ncol, dest, r0, func) in groups:
            wt = wring.next()
            P.dma("pool", wt[:, :, 0:ncol], w_in[:, c0:c0 + ncol].rearrange("(c p) n -> p c n", p=128), writes=[wt])
            for ct in range((ncol + 127) // 128):
                m = min(128, ncol - ct * 128)
                ot = oringb.next() if (func is not None or dest is QKV_T) else oring.next()
                for tb in range(PART // 512):
                    pm = pmm.next()
                    for c in range(16):
                        P.op("pe", "matmul", [xnT, wt], [pm], pm[0:m, :], lhsT=wt[:, c, ct * 128:ct * 128 + m],
                             rhs=xnT[:, c, tb * 512:(tb + 1) * 512], start=(c == 0), stop=(c == 15))
                    if func is not None:
                        P.op("act", "activation", [pm], [ot], out=ot[0:m, tb * 512:(tb + 1) * 512], in_=pm[0:m, :], func=func)
                    else:
                        evq += 1
                        if evq % 2 == 0:
                            P.op("act", "copy", [pm], [ot], out=ot[0:m, tb * 512:(tb + 1) * 512], in_=pm[0:m, :])
                        else:
                            P.op("dve", "tensor_copy", [pm], [ot], out=ot[0:m, tb * 512:(tb + 1) * 512], in_=pm[0:m, :])
                rr = r0 + ct * 128
                P.dma("act", dest[rr:rr + m, t0:t0 + PART], ot[0:m, :], reads=[ot])
    P.end()
    if _STOP == 1:
        nc._P = P
        return nc

    P.begin()
    identf = P.sb([128, 128], F32)
    P.dma("sp", identf[:], c_ident, writes=[identf])
    onesb = P.sb([128, 128], BF16)
    P.op("dve", "memset", [], [onesb], onesb[:], 1.0)
    cwr = P.sb([120, 128], F32)
    P.dma("sp", cwr[:], conv_w, writes=[cwr])
    cw = P.sb([128, 120], F32)
    pmm = P.ring(4, [128, 512], F32, psum=True)
    pm = pmm.next()
    P.op("pe", "matmul", [cwr, identf], [pm], pm[:, 0:120], lhsT=cwr[:], rhs=identf[0:120, 0:120], start=True, stop=True)
    P.op("dve", "tensor_copy", [pm], [cw], out=cw[:], in_=pm[:, 0:120])
    tmask = P.sb([128, L], F32)
    P.dma("act", tmask[:], tmask_d[:, 0:L], writes=[tmask])
    epst = P.sb([128, 2], F32)
    P.op("dve", "memset", [], [epst], epst[:, 0:1], EPS)
    P.op("dve", "memset", [], [epst], epst[:, 1:2], EPS * DK)
    identb2 = P.sb([128, 128], BF16)
    P.op("dve", "tensor_copy", [identf], [identb2], out=identb2[:], in_=identf[:])
    xpr = P.ring(2, [128, L + 4], BF16)
    for b in xpr.bufs:
        P.op("pool", "memset", [], [b], b[:, 0:2], 0.0)
        P.op("pool", "memset", [], [b], b[:, L + 2:L + 4], 0.0)
    dgw_r = P.ring(2, [128, KCONV, 128], BF16)
    slr = P.ring(2, [128, L], F32)
    sqr = P.ring(2, [128, L], BF16)
    outr = P.ring(2, [128, L], BF16)
    rnr = P.ring(3, [128, 512], F32)
    cw3 = cw[:].rearrange("p (k c) -> p k c", c=24)
    for cc in range(24):
        kind = cc // 8
        xp = xpr.next()
        P.dma("sp", xp[:, 2:L + 2], QKV_T[cc * 128:(cc + 1) * 128, :], writes=[xp])
        dgw = dgw_r.next()
        P.op("dve", "tensor_tensor", [identb2, cw], [dgw], out=dgw[:], in0=identb2[:].unsqueeze(1).broadcast_to([128, KCONV, 128]),
             in1=cw3[:, :, cc:cc + 1].broadcast_to([128, KCONV, 128]), op=ALU.mult)
        sl = slr.next()
        for tb in range(NB):
            pc = pmm.next()
            for k in range(KCONV):
                P.op("pe", "matmul", [dgw, xp], [pc], pc[:], lhsT=dgw[:, k, :], rhs=xp[:, tb * 512 + k:tb * 512 + k + 512],
                     start=(k == 0), stop=(k == KCONV - 1))
            P.op("act", "activation", [pc], [sl], out=sl[:, tb * 512:(tb + 1) * 512], in_=pc[:], func=AF.Silu)
        P.op("dve", "tensor_tensor", [sl, tmask], [sl], out=sl[:], in0=sl[:], in1=tmask[:], op=ALU.mult)
        ob = outr.next()
        if kind == 2:
            P.op("pool", "tensor_copy", [sl], [ob], out=ob[:], in_=sl[:])
            P.dma("pool", VT[(cc - 16) * 128:(cc - 15) * 128, :], ob[:], reads=[ob])
        else:
            sq = sqr.next()
            P.op("pool", "tensor_tensor", [sl], [sq], out=sq[:], in0=sl[:], in1=sl[:], op=ALU.mult)
            for tb in range(NB):
                pm = pmm.next()
                P.op("pe", "matmul", [onesb, sq], [pm], pm[:], lhsT=onesb[:], rhs=sq[:, tb * 512:(tb + 1) * 512], start=True, stop=True)
                rn = rnr.next()
                if kind == 0:
                    P.op("act", "activation", [pm, epst], [rn], out=rn[:], in_=pm[:], func=AF.Sqrt, bias=epst[:, 1:2], scale=float(DK))
                else:
                    P.op("act", "activation", [pm, epst], [rn], out=rn[:], in_=pm[:], func=AF.Sqrt, bias=epst[:, 0:1])
                P.op("dve", "reciprocal", [rn], [rn], out=rn[:], in_=rn[:])
                P.op("dve", "tensor_tensor", [sl, rn], [ob], out=ob[:, tb * 512:(tb + 1) * 512],
                     in0=sl[:, tb * 512:(tb + 1) * 512], in1=rn[:], op=ALU.mult)
            dst = QT if kind == 0 else KT
            hh = cc % 8
            P.dma("pool", dst[hh * 128:(hh + 1) * 128, :], ob[:], reads=[ob])
    P.end()
    if _STOP == 2:
        nc._P = P
        return nc


    P.begin()
    identf = P.sb([128, 128], F32)
    P.dma("sp", identf[:], c_ident, writes=[identf])
    onesb = P.sb([128, 128], BF16)
    P.op("dve", "memset", [], [onesb], onesb[:], 1.0)
    wq = P.sb([128, 12, 1536], BF16)
    P.dma("pool", wq[:], w_q_b.rearrange("(c p) n -> p c n", p=128), writes=[wq])
    wqt = P.sb([128, 2, 12, 256], BF16)
    wq4 = wq[:].rearrange("p c (h r) -> p c h r", r=192)
    P.op("dve", "tensor_copy", [wq], [wqt], out=wqt[:, 0, :, :].rearrange("p c (h r) -> p c h r", r=32), in_=wq4[:, :, :, 128:160])
    P.op("pool", "tensor_copy", [wq], [wqt], out=wqt[:, 1, :, :].rearrange("p c (h r) -> p c h r", r=32), in_=wq4[:, :, :, 160:192])
    wkv = P.sb([128, 4, 2048], BF16)
    P.dma("pool", wkv[:], w_kv_b.rearrange("(c p) n -> p c n", p=128), writes=[wkv])
    pm_r = P.ring(6, [128, 512], F32, psum=True)
    nrm_rows = P.sb([16, 128], F32)
    P.dma("sp", nrm_rows[0:12, :], q_a_norm.rearrange("(c p) -> c p", p=128), writes=[nrm_rows])
    P.dma("sp", nrm_rows[12:16, :], kv_a_norm.rearrange("(c p) -> c p", p=128), writes=[nrm_rows])
    nrm = P.sb([128, 16], F32)
    pm = pm_r.next()
    P.op("pe", "matmul", [nrm_rows, identf], [pm], pm[:, 0:16], lhsT=nrm_rows[:], rhs=identf[0:16, 0:16], start=True, stop=True)
    P.op("dve", "tensor_copy", [pm], [nrm], out=nrm[:], in_=pm[:, 0:16])
    wkv3 = wkv[:].rearrange("p c (h r) -> p c h r", r=256)
    ql_r = P.ring(1, [128, 12, 512], F32)
    sq_r = P.ring(1, [128, 12, 512], BF16)
    cq_r = P.ring(2, [128, 12, 512], BF16)
    kvl_r = P.ring(2, [128, 4, 512], F32)
    ckv_r = P.ring(2, [128, 4, 512], BF16)
    rr_r = P.ring(2, [128, 512], F32)
    cs_r = P.ring(2, [128, 2, 512], F32)
    st_r = P.ring(4, [128, 512], BF16)
    tmp_r = P.ring(4, [128, 512], F32)
    vb_r = P.ring(2, [128, 8, 128], BF16)
    kr_r = P.ring(2, [32, 2, 512], F32)
    evq = 0
    for tb in range(NB):
        blk = slice(tb * 512, (tb + 1) * 512)
        cs = cs_r.next()
        P.dma("sp", cs[:, 0, :], c_cos4[:, blk], writes=[cs])
        P.dma("sp", cs[:, 1, :], c_sin4[:, blk], writes=[cs])

        def latent_norm(src, nch, ncol0, ring_in, ring_out, dim):
            lt = ring_in.next()
            P.dma("act", lt[:], src.rearrange("(c p) t -> p c t", p=128)[:, :, blk], writes=[lt])
            sq = sq_r.next()
            P.op("pool", "tensor_tensor", [lt], [sq], out=sq[:, 0:nch, :], in0=lt[:], in1=lt[:], op=ALU.mult)
            pss = pm_r.next()
            for c in range(nch):
                P.op("pe", "matmul", [onesb, sq], [pss], pss[:], lhsT=onesb[:], rhs=sq[:, c, :], start=(c == 0), stop=(c == nch - 1))
            rr = rr_r.next()
            P.op("dve", "tensor_scalar", [pss], [rr], out=rr[:], in0=pss[:], scalar1=1.0 / dim, scalar2=EPS, op0=ALU.mult, op1=ALU.add)
            P.op("act", "activation", [rr], [rr], out=rr[:], in_=rr[:], func=AF.Sqrt)
            P.op("dve", "reciprocal", [rr], [rr], out=rr[:], in_=rr[:])
            ct_ = ring_out.next()
            for c in range(nch):
                P.op("dve", "scalar_tensor_tensor", [lt, nrm, rr], [ct_], out=ct_[:, c, :], in0=lt[:, c, :],
                     scalar=nrm[:, ncol0 + c:ncol0 + c + 1], in1=rr[:], op0=ALU.mult, op1=ALU.mult)
            return ct_

        def rope(pT1, pT2, np_, dst1, dst2):
            a, b, c_, d_ = tmp_r.next(), tmp_r.next(), tmp_r.next(), tmp_r.next()
            P.op("dve", "tensor_tensor", [pT1[0], cs], [a], out=a[0:np_, :], in0=pT1[1], in1=cs[0:np_, 0, :], op=ALU.mult)
            P.op("dve", "tensor_tensor", [pT2[0], cs], [b], out=b[0:np_, :], in0=pT2[1], in1=cs[0:np_, 1, :], op=ALU.mult)
            P.op("dve", "tensor_tensor", [pT1[0], cs], [c_], out=c_[0:np_, :], in0=pT1[1], in1=cs[0:np_, 1, :], op=ALU.mult)
            P.op("dve", "tensor_tensor", [pT2[0], cs], [d_], out=d_[0:np_, :], in0=pT2[1], in1=cs[0:np_, 0, :], op=ALU.mult)
            o1, o2 = st_r.next(), st_r.next()
            P.op("pool", "tensor_tensor", [a, b], [o1], out=o1[0:np_, :], in0=a[0:np_, :], in1=b[0:np_, :], op=ALU.subtract)
            P.op("pool", "tensor_tensor", [c_, d_], [o2], out=o2[0:np_, :], in0=c_[0:np_, :], in1=d_[0:np_, :], op=ALU.add)
            for (o, dst) in ((o1, dst1), (o2, dst2)):
                for (psl, dap) in dst:
                    P.dma("sp", dap, o[psl, :], reads=[o])

        cq = latent_norm(QL_T, 12, 0, ql_r, cq_r, Q_LORA)
        for h in range(8):
            pn = pm_r.next()
            for c in range(12):
                P.op("pe", "matmul", [wq, cq], [pn], pn[:], lhsT=wq[:, c, h * 192:h * 192 + 128], rhs=cq[:, c, :], start=(c == 0), stop=(c == 11))
            qn = st_r.next()
            evq += 1
            if evq % 2:
                P.op("act", "copy", [pn], [qn], out=qn[:], in_=pn[:])
            else:
                P.op("dve", "tensor_copy", [pn], [qn], out=qn[:], in_=pn[:])
            P.dma("pool", QF_T[h * 192:h * 192 + 128, blk], qn[:], reads=[qn])
        for g in range(2):
            pT1, pT2 = pm_r.next(), pm_r.next()
            for (pt_, half) in ((pT1, 0), (pT2, 1)):
                for c in range(12):
                    P.op("pe", "matmul", [wqt, cq], [pt_], pt_[:], lhsT=wqt[:, half, c, g * 128:(g + 1) * 128], rhs=cq[:, c, :],
                         start=(c == 0), stop=(c == 11))
            QF3 = QF_T.rearrange("(h r) t -> h r t", r=192)
            dst1 = [(slice(32 * hh, 32 * hh + 32), QF3[4 * g + hh, 128:160, blk]) for hh in range(4)]
            dst2 = [(slice(32 * hh, 32 * hh + 32), QF3[4 * g + hh, 160:192, blk]) for hh in range(4)]
            rope((pT1, pT1[:]), (pT2, pT2[:]), 128, dst1, dst2)
        ckv = latent_norm(KVL_T, 4, 12, kvl_r, ckv_r, KV_LORA)
        for h in range(8):
            pn = pm_r.next()
            for c in range(4):
                P.op("pe", "matmul", [wkv, ckv], [pn], pn[:], lhsT=wkv[:, c, h * 256:h * 256 + 128], rhs=ckv[:, c, :], start=(c == 0), stop=(c == 3))
            kn = st_r.next()
            evq += 1
            if evq % 2:
                P.op("act", "copy", [pn], [kn], out=kn[:], in_=pn[:])
            else:
                P.op("dve", "tensor_copy", [pn], [kn], out=kn[:], in_=pn[:])
            P.dma("pool", KN_T[h * 128:(h + 1) * 128, blk], kn[:], reads=[kn])
        for st in range(4):
            vb = vb_r.next()
            for g in range(2):
                pv_ = pm_r.next()
                for c in range(4):
                    P.op("pe", "matmul", [ckv, wkv], [pv_], pv_[:].rearrange("p (h e) -> p h e", e=128), lhsT=ckv[:, c, st * 128:(st + 1) * 128],
                         rhs=wkv3[:, c, 4 * g:4 * g + 4, 128:256], start=(c == 0), stop=(c == 3))
                if g == 0:
                    P.op("act", "copy", [pv_], [vb], out=vb[:, 0:4, :], in_=pv_[:].rearrange("p (h e) -> p h e", e=128))
                else:
                    P.op("dve", "tensor_copy", [pv_], [vb], out=vb[:, 4:8, :], in_=pv_[:].rearrange("p (h e) -> p h e", e=128))
            r0 = tb * 512 + st * 128
            P.dma("pool", VB[r0:r0 + 128, :].rearrange("p (h e) -> p h e", e=128), vb[:], reads=[vb])
        kr = kr_r.next()
        P.dma("act", kr[:, 0, :], KR_T[0:32, blk], writes=[kr])
        P.dma("act", kr[:, 1, :], KR_T[32:64, blk], writes=[kr])
        rope((kr, kr[:, 0, :]), (kr, kr[:, 1, :]), 32, [(slice(0, 32), KPE_T[0:32, blk])], [(slice(0, 32), KPE_T[32:64, blk])])
    P.end()
    if _STOP == 4:
        nc._P = P
        return nc

    OF = dscr("OF", [L, 1024])
    P.begin()
    identf = P.sb([128, 128], F32)
    identb = P.sb([128, 128], BF16)
    onesf = P.sb([128, 128], F32)
    P.dma("sp", identf[:], c_ident, writes=[identf])
    P.op("dve", "tensor_copy", [identf], [identb], out=identb[:], in_=identf[:])
    P.op("dve", "memset", [], [onesf], onesf[:], 1.0)
    tri = P.sb([128, 2, 128], F32)
    P.dma("sp", tri[:], c_tri.rearrange("d p i -> p d i"), writes=[tri])
    negm = P.sb([128, 2, 128], F32)
    P.dma("sp", negm[:], c_negmask.rearrange("d p i -> p d i"), writes=[negm])
    lvl = P.sb([128, 2, 7, 128], BF16)
    P.dma("pool", lvl[:], c_lvl.rearrange("d p s i -> p d s i"), writes=[lvl])
    onorm = P.sb([128, 1], F32)
    P.dma("sp", onorm[:], o_norm_a.rearrange("(p o) -> p o", o=1), writes=[onorm])
    ABt = P.sb([128, NT, 32], F32)
    for t0_ in range(0, NT, 8):
        t1_ = min(NT, t0_ + 8)
        P.dma("sp", ABt[:, t0_:t1_, :], AB[t0_ * 128:t1_ * 128, :].rearrange("(t p) c -> p t c", p=128), writes=[ABt])
    dtb = P.sb([128, 16], F32)
    alg = P.sb([128, 16], F32)
    P.dma("sp", dtb[:], dt_bias.partition_broadcast(128), writes=[dtb])
    P.dma("sp", alg[:], a_log.partition_broadcast(128), writes=[alg])
    negA = P.sb([128, 16], F32)
    P.op("act", "activation", [alg], [negA], out=negA[:], in_=alg[:], func=AF.Exp)
    P.op("dve", "tensor_scalar", [negA], [negA], out=negA[:], in0=negA[:], scalar1=-1.0, scalar2=None, op0=ALU.mult)
    gt = P.sb([128, NT, 16], F32)
    beta = P.sb([128, NT, 16], F32)
    gcs = P.sb([128, NT, 16], F32)
    ngc = P.sb([128, NT, 16], F32)
    eg = P.sb([128, NT, 16], F32)
    kd = P.sb([128, NT, 16], F32)
    egl = P.sb([128, NT, 16], F32)
    pbf = P.ring(2, [128, 8, 128], BF16, psum=True)
    pf = P.ring(6, [128, 4, 128], F32, psum=True)

    def bc16(ap16):
        return ap16.unsqueeze(1).broadcast_to([128, NT, 16])
    P.op("dve", "tensor_tensor", [ABt, dtb], [gt], out=gt[:], in0=ABt[:, :, 0:16], in1=bc16(dtb[:]), op=ALU.add)
    P.op("act", "activation", [gt], [gt], out=gt[:], in_=gt[:], func=AF.Exp)
    P.op("dve", "tensor_scalar", [gt], [gt], out=gt[:], in0=gt[:], scalar1=1.0, scalar2=None, op0=ALU.add)
    P.op("act", "activation", [gt], [gt], out=gt[:], in_=gt[:], func=AF.Ln)
    P.op("dve", "tensor_tensor", [gt, negA], [gt], out=gt[:], in0=gt[:], in1=bc16(negA[:]), op=ALU.mult)
    P.op("act", "activation", [ABt], [beta], out=beta[:], in_=ABt[:, :, 16:32], func=AF.Exp, scale=-1.0)
    P.op("dve", "tensor_scalar", [beta], [beta], out=beta[:], in0=beta[:], scalar1=1.0, scalar2=None, op0=ALU.add)
    P.op("dve", "reciprocal", [beta], [beta], out=beta[:], in_=beta[:])
    pg = pf.next()
    pgv = pg[:].rearrange("p a b -> p (a b)")
    for d in range(2):
        P.op("pe", "matmul", [tri, gt], [pg], pgv[:, d * NT * 8:(d + 1) * NT * 8], lhsT=tri[:, d, :], rhs=gt[:, :, d * 8:(d + 1) * 8],
             start=True, stop=True)
    for d in range(2):
        P.op("dve", "tensor_copy", [pg], [gcs], out=gcs[:, :, d * 8:(d + 1) * 8],
             in_=pgv[:, d * NT * 8:(d + 1) * NT * 8].rearrange("p (t h) -> p t h", h=8))
    ptot = pf.next()
    ptv = ptot[:].rearrange("p a b -> p (a b)")
    gtv = gt[:].rearrange("p t c -> p (t c)")
    for c0_ in range(0, NT * 16, 256):
        c1_ = min(NT * 16, c0_ + 256)
        P.op("pe", "matmul", [onesf, gt], [ptot], ptv[:, c0_:c1_], lhsT=onesf[:], rhs=gtv[:, c0_:c1_], start=True, stop=True)
    ptv3 = ptv[:, 0:NT * 16].rearrange("p (t c) -> p t c", c=16)
    P.op("act", "activation", [ptot], [egl], out=egl[:], in_=ptv3, func=AF.Exp)
    P.op("dve", "tensor_tensor", [ptot, gcs], [kd], out=kd[:], in0=ptv3, in1=gcs[:], op=ALU.subtract)
    P.op("act", "activation", [kd], [kd], out=kd[:], in_=kd[:], func=AF.Exp)
    P.op("act", "activation", [gcs], [eg], out=eg[:], in_=gcs[:], func=AF.Exp)
    P.op("dve", "tensor_scalar", [gcs], [ngc], out=ngc[:], in0=gcs[:], scalar1=-1.0, scalar2=None, op0=ALU.mult)

    S32 = P.sb([128, 8, 128], F32)
    Sb = P.sb([128, 8, 128], BF16)
    kt_r = P.ring(2, [128, 8, 128], BF16)
    qt_r = P.ring(2, [128, 8, 128], BF16)
    vt_r = P.ring(2, [128, 8, 128], BF16)
    kdec_r = P.ring(2, [128, 8, 128], BF16)
    vtok_r = P.ring(2, [128, 8, 128], BF16)
    dg_r = P.ring(2, [128, 8, 128], F32)
    de_r = P.ring(2, [128, 8, 128], F32)
    E_r = P.ring(4, [128, 4, 128], F32)
    AT_r = P.ring(4, [128, 4, 128], BF16)
    attn_r = P.ring(4, [128, 4, 128], BF16)
    kgt_r = P.ring(4, [128, 4, 128], BF16)
    qgt_r = P.ring(4, [128, 4, 128], BF16)
    nat_r = P.ring(4, [128, 4, 7, 128], BF16)
    D_r = P.ring(4, [128, 4, 128], BF16)
    DT_r = P.ring(4, [128, 4, 128], BF16)
    p1_r = P.ring(4, [128, 4, 128], BF16)
    TT_r = P.ring(4, [128, 4, 128], BF16)
    R_r = P.ring(4, [128, 4, 128], BF16)
    VN_r = P.ring(4, [128, 4, 128], BF16)
    O_r = P.ring(2, [128, 8, 128], F32)
    of_r = P.ring(2, [128, 8, 128], F32)
    ga_r = P.ring(2, [128, 8, 128], BF16)
    osum_r = P.ring(2, [128, 4, 128], F32)
    osq_r = P.ring(2, [128, 4, 128], F32)
    on_r = P.ring(2, [128, 4, 128], F32)
    ost_r = P.ring(4, [128, 4, 4], F32)
    oa_r = P.ring(2, [128, 8, 128], BF16)
    identb_bc = identb[:].unsqueeze(1).broadcast_to([128, 4, 128])

    for d in range(2):
        P.op("dve", "memset", [], [S32], S32[:], 0.0)
        P.op("pool", "memset", [], [Sb], Sb[:], 0.0)
        order = range(NT) if d == 0 else range(NT - 1, -1, -1)
        d8 = d * 8
        def tile_gen(t):
            tok = slice(t * 128, (t + 1) * 128)
            KTt, QTt, VTt = kt_r.next(), qt_r.next(), vt_r.next()
            P.dma("sp", KTt[:], KT.rearrange("(h p) t -> p h t", p=128)[:, :, tok], writes=[KTt])
            P.dma("act", QTt[:], QT.rearrange("(h p) t -> p h t", p=128)[:, :, tok], writes=[QTt])
            P.dma("sp", VTt[:], VT.rearrange("(h p) t -> p h t", p=128)[:, :, tok], writes=[VTt])
            if d == 1:
                OFt, GAt = of_r.next(), ga_r.next()
                P.dma("sp", OFt[:], OF[tok, :].rearrange("p (h e) -> p h e", h=8), writes=[OFt])
                P.dma("act", GAt[:], GA_T.rearrange("(h p) t -> p h t", p=128)[:, :, tok], writes=[GAt])
                OAt = oa_r.next()
            else:
                Ot = O_r.next()
            pk, pv = pbf.next(), pbf.next()
            for h in range(8):
                P.op("pe", "transpose", [KTt, identb], [pk], out=pk[:, h, :], in_=KTt[:, h, :], identity=identb[:])
            for h in range(8):
                P.op("pe", "transpose", [VTt, identb], [pv], out=pv[:, h, :], in_=VTt[:, h, :], identity=identb[:])
            kdec, vtok = kdec_r.next(), vtok_r.next()
            P.op("dve", "tensor_tensor", [pk, kd], [kdec], out=kdec[:], in0=pk[:],
                 in1=kd[:, t, d8:d8 + 8].unsqueeze(2).broadcast_to([128, 8, 128]), op=ALU.mult)
            P.op("act", "copy", [pv], [vtok], out=vtok[:], in_=pv[:])
            dg, de = dg_r.next(), de_r.next()
            idbc8 = identf[:].unsqueeze(1).broadcast_to([128, 8, 128])
            P.op("pool", "tensor_tensor", [identf, gcs], [dg], out=dg[:], in0=idbc8,
                 in1=gcs[:, t, d8:d8 + 8].unsqueeze(2).broadcast_to([128, 8, 128]), op=ALU.mult)
            P.op("pool", "tensor_tensor", [identf, eg], [de], out=de[:], in0=idbc8,
                 in1=eg[:, t, d8:d8 + 8].unsqueeze(2).broadcast_to([128, 8, 128]), op=ALU.mult)
            GR = (0, 1)
            hsl = [range(G * 4, G * 4 + 4) for G in GR]
            gsls = [slice(G * 4, G * 4 + 4) for G in GR]
            st_ = [dict() for _ in GR]
            for G in GR:
                hs = hsl[G]
                pKK, pBC = pf.next(), pf.next()
                for hh, h in enumerate(hs):
                    P.op("pe", "matmul", [KTt], [pKK], pKK[:, hh, :], lhsT=KTt[:, h, :], rhs=KTt[:, h, :], start=True, stop=True)
                for hh, h in enumerate(hs):
                    P.op("pe", "matmul", [onesf, dg], [pBC], pBC[:, hh, :], lhsT=onesf[:], rhs=dg[:, h, :], start=True, stop=False)
                    P.op("pe", "matmul", [identf, negm], [pBC], pBC[:, hh, :], lhsT=identf[:], rhs=negm[:, d, :], start=False, stop=True)
                E = E_r.next()
                for hh, h in enumerate(hs):
                    P.op("act", "activation", [pBC, ngc], [E], out=E[:, hh, :], in_=pBC[:, hh, :], func=AF.Exp,
                         bias=ngc[:, t, d8 + h:d8 + h + 1])
                AT = AT_r.next()
                for hh, h in enumerate(hs):
                    P.op("dve", "scalar_tensor_tensor", [pKK, beta, E], [AT], out=AT[:, hh, :], in0=pKK[:, hh, :],
                         scalar=beta[:, t, d8 + h:d8 + h + 1], in1=E[:, hh, :], op0=ALU.mult, op1=ALU.mult)
                NAT = nat_r.next()
                P.op("pool", "tensor_tensor", [AT, lvl], [NAT], out=NAT[:],
                     in0=AT[:].unsqueeze(2).broadcast_to([128, 4, 7, 128]),
                     in1=lvl[:, d, :, :].unsqueeze(1).broadcast_to([128, 4, 7, 128]), op=ALU.mult)
                st_[G].update(E=E, NAT=NAT)
            for G in GR:
                hs = hsl[G]
                gsl = gsls[G]
                E = st_[G]["E"]
                pQK, pEG = pf.next(), pf.next()
                for hh, h in enumerate(hs):
                    P.op("pe", "matmul", [KTt, QTt], [pQK], pQK[:, hh, :], lhsT=KTt[:, h, :], rhs=QTt[:, h, :], start=True, stop=True)
                for hh, h in enumerate(hs):
                    P.op("pe", "matmul", [onesf, de], [pEG], pEG[:, hh, :], lhsT=onesf[:], rhs=de[:, h, :], start=True, stop=True)
                attnT, KGT, QGT = attn_r.next(), kgt_r.next(), qgt_r.next()
                P.op("dve", "tensor_tensor", [pQK, E], [attnT], out=attnT[:], in0=pQK[:], in1=E[:], op=ALU.mult)
                P.op("dve", "tensor_tensor", [KTt, pEG], [KGT], out=KGT[:], in0=KTt[:, gsl, :], in1=pEG[:], op=ALU.mult)
                P.op("dve", "tensor_tensor", [QTt, pEG], [QGT], out=QGT[:], in0=QTt[:, gsl, :], in1=pEG[:], op=ALU.mult)
                st_[G].update(attnT=attnT, KGT=KGT, QGT=QGT)
            for G in GR:
                NAT = st_[G]["NAT"]
                pP1 = pf.next()
                for hh in range(4):
                    P.op("pe", "matmul", [NAT, identb], [pP1], pP1[:, hh, :], lhsT=NAT[:, hh, 0, :], rhs=identb[:], start=True, stop=True)
                Dm, DT = D_r.next(), DT_r.next()
                P.op("dve", "tensor_tensor", [identb, pP1], [Dm], out=Dm[:], in0=identb_bc, in1=pP1[:], op=ALU.add)
                P.op("pool", "tensor_tensor", [identb, NAT], [DT], out=DT[:], in0=identb_bc, in1=NAT[:, :, 0, :], op=ALU.add)
                st_[G].update(Dm=Dm, DT=DT)
            for lv in range(1, 7):
                for G in GR:
                    NAT, Dm = st_[G]["NAT"], st_[G]["Dm"]
                    pP1 = pf.next()
                    for hh in range(4):
                        P.op("pe", "matmul", [NAT, Dm], [pP1], pP1[:, hh, :], lhsT=NAT[:, hh, lv, :], rhs=Dm[:, hh, :], start=True, stop=True)
                    P1s = p1_r.next()
                    P.op("act", "copy", [pP1], [P1s], out=P1s[:], in_=pP1[:])
                    st_[G]["P1s"] = P1s
                for G in GR:
                    Dm, DT, P1s = st_[G]["Dm"], st_[G]["DT"], st_[G]["P1s"]
                    if lv < 6:
                        pY = pf.next()
                        for hh in range(4):
                            P.op("pe", "matmul", [DT, P1s], [pY], pY[:, hh, :], lhsT=DT[:, hh, :], rhs=P1s[:, hh, :], start=True, stop=True)
                    pYT = pf.next()
                    for hh in range(4):
                        P.op("pe", "matmul", [P1s, DT], [pYT], pYT[:, hh, :], lhsT=P1s[:, hh, :], rhs=DT[:, hh, :], start=True, stop=True)
                    if lv < 6:
                        Dn = D_r.next()
                        P.op("dve", "tensor_tensor", [Dm, pY], [Dn], out=Dn[:], in0=Dm[:], in1=pY[:], op=ALU.add)
                        st_[G]["Dm"] = Dn
                    DTn = DT_r.next() if lv < 6 else TT_r.next()
                    P.op("dve", "tensor_tensor", [DT, pYT], [DTn], out=DTn[:], in0=DT[:], in1=pYT[:], op=ALU.add)
                    st_[G]["DT"] = DTn
            yield
            for G in GR:
                hs, gsl = hsl[G], gsls[G]
                KGT = st_[G]["KGT"]
                pR = pf.next()
                for hh, h in enumerate(hs):
                    P.op("pe", "matmul", [KGT, Sb], [pR], pR[:, hh, :], lhsT=KGT[:, hh, :], rhs=Sb[:, h, :], start=True, stop=True)
                Rt = R_r.next()
                P.op("dve", "tensor_tensor", [vtok, pR], [Rt], out=Rt[:], in0=vtok[:, gsl, :], in1=pR[:], op=ALU.subtract)
                st_[G]["Rt"] = Rt
            for G in GR:
                TT, Rt = st_[G]["DT"], st_[G]["Rt"]
                pVN = pf.next()
                for hh in range(4):
                    P.op("pe", "matmul", [TT, Rt], [pVN], pVN[:, hh, :], lhsT=TT[:, hh, :], rhs=Rt[:, hh, :], start=True, stop=True)
                VN = VN_r.next()
                P.op("dve", "tensor_tensor", [pVN, beta], [VN], out=VN[:], in0=pVN[:],
                     in1=beta[:, t, d8 + G * 4:d8 + G * 4 + 4].unsqueeze(2).broadcast_to([128, 4, 128]), op=ALU.mult)
                st_[G]["VN"] = VN
            for G in GR:
                hs, gsl = hsl[G], gsls[G]
                QGT, attnT, VN = st_[G]["QGT"], st_[G]["attnT"], st_[G]["VN"]
                pO = pf.next()
                for hh, h in enumerate(hs):
                    P.op("pe", "matmul", [QGT, Sb], [pO], pO[:, hh, :], lhsT=QGT[:, hh, :], rhs=Sb[:, h, :], start=True, stop=False)
                    P.op("pe", "matmul", [attnT, VN], [pO], pO[:, hh, :], lhsT=attnT[:, hh, :], rhs=VN[:, hh, :], start=False, stop=True)
                pS = pf.next()
                for hh, h in enumerate(hs):
                    P.op("pe", "matmul", [kdec, VN], [pS], pS[:, hh, :], lhsT=kdec[:, h, :], rhs=VN[:, hh, :], start=True, stop=True)
                P.op("dve", "tensor_tensor", [S32, egl], [S32], out=S32[:, gsl, :], in0=S32[:, gsl, :],
                     in1=egl[:, t, d8 + G * 4:d8 + G * 4 + 4].unsqueeze(2).broadcast_to([128, 4, 128]), op=ALU.mult)
                P.op("dve", "tensor_tensor", [S32, pS], [S32], out=S32[:, gsl, :], in0=S32[:, gsl, :], in1=pS[:], op=ALU.add)
                P.op("act", "copy", [S32], [Sb], out=Sb[:, gsl, :], in_=S32[:, gsl, :])
                if d == 0:
                    P.op("act", "copy", [pO], [Ot], out=Ot[:, gsl, :], in_=pO[:])
                else:
                    osum, osq, on, ost = osum_r.next(), osq_r.next(), on_r.next(), ost_r.next()
                    P.op("dve", "tensor_tensor", [pO, OFt], [osum], out=osum[:], in0=pO[:], in1=OFt[:, gsl, :], op=ALU.add)
                    P.op("pool", "tensor_tensor", [osum], [osq], out=osq[:], in0=osum[:], in1=osum[:], op=ALU.mult)
                    P.op("dve", "tensor_reduce", [osq], [ost], out=ost[:, :, 0], in_=osq[:], axis=AX.X, op=ALU.add)
                    P.op("dve", "tensor_scalar", [ost], [ost], out=ost[:, :, 1], in0=ost[:, :, 0], scalar1=1.0 / 128, scalar2=EPS,
                         op0=ALU.mult, op1=ALU.add)
                    P.op("act", "activation", [ost], [ost], out=ost[:, :, 2], in_=ost[:, :, 1], func=AF.Sqrt)
                    P.op("dve", "reciprocal", [ost], [ost], out=ost[:, :, 3], in_=ost[:, :, 2])
                    P.op("dve", "tensor_tensor", [osum, ost], [on], out=on[:], in0=osum[:],
                         in1=ost[:, :, 3:4].broadcast_to([128, 4, 128]), op=ALU.mult)
                    pT = pf.next()
                    for hh in range(4):
                        P.op("pe", "matmul", [on, identf], [pT], pT[:, hh, :], lhsT=on[:, hh, :], rhs=identf[:], start=True, stop=True)
                    P.op("dve", "scalar_tensor_tensor", [pT, onorm, GAt], [OAt], out=OAt[:, gsl, :], in0=pT[:], scalar=onorm[:, 0:1],
                         in1=GAt[:, gsl, :], op0=ALU.mult, op1=ALU.mult)
            if d == 0:
                P.dma("pool", OF[tok, :].rearrange("p (h e) -> p h e", h=8), Ot[:], reads=[Ot])
            else:
                P.dma("pool", OA_T.rearrange("(h p) t -> p h t", p=128)[:, :, tok], OAt[:], reads=[OAt])
        prev_g = None
        for t in order:
            g_ = tile_gen(t)
            next(g_)
            if prev_g is not None:
                for _ in prev_g:
                    pass
            prev_g = g_
        if prev_g is not None:
            for _ in prev_g:
                pass
        if d == 0:
            P.barrier()
    P.end()
    if _STOP == 3:
        nc._P = P
        return nc


    P.begin()
    onesb = P.sb([128, 128], BF16)
    P.op("dve", "memset", [], [onesb], onesb[:], 1.0)
    kb = P.sb([128, LMAX // 128], F32)
    P.dma("sp", kb[:], kbias, writes=[kb])
    kpe = P.sb([128, L], BF16)
    P.op("pool", "memset", [], [kpe], kpe[64:128, :], 0.0)
    P.dma("sp", kpe[0:64, :], KPE_T, writes=[kpe])
    kn_r = P.ring(2, [128, L], BF16)
    vh_r = P.ring(2, [128, NT, 128], BF16)
    qn_r = P.ring(2, [128, 512], BF16)
    qp_r = P.ring(2, [128, 512], BF16)
    for b_ in qp_r.bufs:
        P.op("pool", "memset", [], [b_], b_[64:128, :], 0.0)
    gb_r = P.ring(2, [128, 512], BF16)
    pt_r = P.ring(4, [128, 512], BF16)
    pS_r = P.ring(4, [128, 512], F32, psum=True)
    pO_r = P.ring(2, [128, 512], F32, psum=True)
    pZ_r = P.ring(2, [128, 512], F32, psum=True)
    rs_r = P.ring(2, [128, 512], F32)
    o1_r = P.ring(2, [128, 512], F32)
    ob_r = P.ring(2, [128, 512], BF16)
    sm_scale = float((D_NOPE + D_ROPE) ** -0.5)
    for h in range(8):
        knh, vh = kn_r.next(), vh_r.next()
        P.dma("sp", knh[:], KN_T[h * 128:(h + 1) * 128, :], writes=[knh])
        for t0_ in range(0, NT, 8):
            t1_ = min(NT, t0_ + 8)
            P.dma("act", vh[:, t0_:t1_, :], VB[t0_ * 128:t1_ * 128, h * 128:(h + 1) * 128].rearrange("(t p) e -> p t e", p=128), writes=[vh])
        for qb in range(NB):
            blk = slice(qb * 512, (qb + 1) * 512)
            qn, qp, gb = qn_r.next(), qp_r.next(), gb_r.next()
            P.dma("sp", qn[:], QF_T[h * 192:h * 192 + 128, blk], writes=[qn])
            P.dma("sp", qp[0:64, :], QF_T[h * 192 + 128:h * 192 + 192, blk], writes=[qp])
            P.dma("sp", gb[:], GB_T[h * 128:(h + 1) * 128, blk], writes=[gb])
            pO, pZ = pO_r.next(), pZ_r.next()

            def scores(kt):
                ps_ = pS_r.next()
                P.op("pe", "matmul", [knh, qn], [ps_], ps_[:], lhsT=knh[:, kt * 128:(kt + 1) * 128], rhs=qn[:], start=True, stop=False)
                P.op("pe", "matmul", [kpe, qp], [ps_], ps_[:], lhsT=kpe[:, kt * 128:(kt + 1) * 128], rhs=qp[:], start=False, stop=True)
                return ps_
            pend = [scores(0)]
            if NT > 1:
                pend.append(scores(1))
            for kt in range(NT):
                cur = pend.pop(0)
                if kt + 2 < NT:
                    pend.append(scores(kt + 2))
                pt_ = pt_r.next()
                P.op("act", "activation", [cur, kb], [pt_], out=pt_[:], in_=cur[:], func=AF.Exp, bias=kb[:, kt:kt + 1], scale=sm_scale)
                P.op("pe", "matmul", [onesb, pt_], [pZ], pZ[:], lhsT=onesb[:], rhs=pt_[:], start=(kt == 0), stop=(kt == NT - 1))
                P.op("pe", "matmul", [vh, pt_], [pO], pO[:], lhsT=vh[:, kt, :], rhs=pt_[:], start=(kt == 0), stop=(kt == NT - 1))
            rs, o1, ob = rs_r.next(), o1_r.next(), ob_r.next()
            P.op("dve", "reciprocal", [pZ], [rs], out=rs[:], in_=pZ[:])
            P.op("dve", "tensor_tensor", [pO, rs], [o1], out=o1[:], in0=pO[:], in1=rs[:], op=ALU.mult)
            P.op("pool", "tensor_tensor", [o1, gb], [ob], out=ob[:], in0=o1[:], in1=gb[:], op=ALU.mult)
            P.dma("pool", OB_T[h * 128:(h + 1) * 128, blk], ob[:], reads=[ob])
    P.end()
    if _STOP == 5:
        nc._P = P
        return nc

    M_T = dscr("M_T", [NT, 128, 16, 128], BF16)
    P.begin()
    wpa = P.sb([128, 8, 2048], BF16)
    wpb = P.sb([128, 8, 2048], BF16)
    P.dma("pool", wpa[:], w_pa.rearrange("(c p) n -> p c n", p=128), writes=[wpa])
    P.dma("pool", wpb[:], w_pb.rearrange("(c p) n -> p c n", p=128), writes=[wpb])
    oa_r2 = P.ring(2, [128, 8, 512], BF16)
    ob_r2 = P.ring(2, [128, 8, 512], BF16)
    gm_r = P.ring(3, [128, 2, 512], BF16)
    m_r = P.ring(4, [128, 512], F32)
    mt_r = P.ring(3, [128, 512], BF16)
    pm_r = P.ring(6, [128, 512], F32, psum=True)
    for tb in range(NB):
        blk = slice(tb * 512, (tb + 1) * 512)
        oat, obt = oa_r2.next(), ob_r2.next()
        P.dma("sp", oat[:], OA_T.rearrange("(c p) t -> p c t", p=128)[:, :, blk], writes=[oat])
        P.dma("act", obt[:], OB_T.rearrange("(c p) t -> p c t", p=128)[:, :, blk], writes=[obt])
        for ct in range(16):
            gm = gm_r.next()
            P.dma("sp", gm[:, 0, :], GMA_T[ct * 128:(ct + 1) * 128, blk], writes=[gm])
            P.dma("act", gm[:, 1, :], GMB_T[ct * 128:(ct + 1) * 128, blk], writes=[gm])
            pA, pB = pm_r.next(), pm_r.next()
            for c in range(8):
                P.op("pe", "matmul", [wpa, oat], [pA], pA[:], lhsT=wpa[:, c, ct * 128:(ct + 1) * 128], rhs=oat[:, c, :], start=(c == 0), stop=(c == 7))
            for c in range(8):
                P.op("pe", "matmul", [wpb, obt], [pB], pB[:], lhsT=wpb[:, c, ct * 128:(ct + 1) * 128], rhs=obt[:, c, :], start=(c == 0), stop=(c == 7))
            m1, m2, mt = m_r.next(), m_r.next(), mt_r.next()
            P.op("dve", "tensor_tensor", [pA, gm], [m1], out=m1[:], in0=pA[:], in1=gm[:, 0, :], op=ALU.mult)
            P.op("dve", "tensor_tensor", [pB, gm], [m2], out=m2[:], in0=pB[:], in1=gm[:, 1, :], op=ALU.mult)
            P.op("pool", "tensor_tensor", [m1, m2], [mt], out=mt[:], in0=m1[:], in1=m2[:], op=ALU.add)
            P.dma("pool", M_T[tb * 4:tb * 4 + 4, :, ct, :].rearrange("j p t -> p j t"), mt[:].rearrange("p (j t) -> p j t", t=128), reads=[mt])
    P.end()
    if _STOP == 6:
        nc._P = P
        return nc

    P.begin()
    wout = P.sb([128, 16, 2048], BF16)
    P.dma("pool", wout[:], w_out.rearrange("(c p) n -> p c n", p=128), writes=[wout])
    nfb = P.sb([128, D_MODEL], F32)
    P.dma("sp", nfb[:], norm_f.partition_broadcast(128), writes=[nfb])
    mt_r2 = P.ring(2, [128, 16, 128], BF16)
    x_r = P.ring(2, [128, D_MODEL], F32)
    z_r = P.ring(2, [128, D_MODEL], F32)
    y_r = P.ring(2, [128, D_MODEL], F32)
    junk = P.sb([128, D_MODEL], BF16)
    st_r2 = P.ring(4, [128, 4], F32)
    pm_r = P.ring(6, [128, 512], F32, psum=True)
    for tt in range(NT):
        tok = slice(tt * 128, (tt + 1) * 128)
        mtt, xt = mt_r2.next(), x_r.next()
        P.dma("sp", mtt[:], M_T[tt], writes=[mtt])
        P.dma("act", xt[:], x[tok, :], writes=[xt])
        z = z_r.next()
        for cg in range(4):
            py_ = pm_r.next()
            for c in range(16):
                P.op("pe", "matmul", [mtt, wout], [py_], py_[:], lhsT=mtt[:, c, :], rhs=wout[:, c, cg * 512:(cg + 1) * 512], start=(c == 0), stop=(c == 15))
            P.op("dve", "tensor_tensor", [py_, xt], [z], out=z[:, cg * 512:(cg + 1) * 512], in0=py_[:], in1=xt[:, cg * 512:(cg + 1) * 512], op=ALU.add)
        st = st_r2.next()
        P.op("act", "activation", [z], [junk, st], out=junk[:], in_=z[:], func=AF.Square, accum_out=st[:, 0:1])
        P.op("dve", "tensor_scalar", [st], [st], out=st[:, 1:2], in0=st[:, 0:1], scalar1=1.0 / D_MODEL, scalar2=EPS, op0=ALU.mult, op1=ALU.add)
        P.op("act", "activation", [st], [st], out=st[:, 2:3], in_=st[:, 1:2], func=AF.Sqrt)
        P.op("dve", "reciprocal", [st], [st], out=st[:, 3:4], in_=st[:, 2:3])
        yv = y_r.next()
        P.op("dve", "scalar_tensor_tensor", [z, st, nfb], [yv], out=yv[:], in0=z[:], scalar=st[:, 3:4], in1=nfb[:], op0=ALU.mult, op1=ALU.mult)
        P.dma("pool", y[tok, :], yv[:], reads=[yv])
    P.end()
    if _STOP == 7:
        nc._P = P
        return nc

    nc._P = P
    return nc


_NC_CACHE = {}


def _core_map(consts, shared, xseq, L):
    valid = xseq.shape[0]
    xp = np.zeros((L, D_MODEL), np.float32)
    xp[:valid] = xseq
    kb = np.zeros((LMAX,), np.float32)
    kb[valid:] = -BIG
    m = dict(shared)
    m.update(consts)
    m["x"] = xp
    m["kbias"] = np.ascontiguousarray(kb.reshape(LMAX // 128, 128).T)
    tm = np.zeros((128, LMAX), np.float32)
    tm[:, :valid] = 1.0
    m["tmask"] = tm
    return m


def kernel(x_prompt, x_sample, norm_in, w_in, conv_w, a_log_f, dt_bias_f, a_log_b, dt_bias_b, o_norm_a,
           q_a_norm, w_q_b, kv_a_norm, w_kv_b, w_pa, w_pb, w_out, norm_f):
    f = lambda a: np.ascontiguousarray(np.asarray(a, dtype=np.float32))
    x_prompt, x_sample = f(x_prompt), f(x_sample)
    L = LMAX
    shared = {
        "norm_in": f(norm_in)[0], "w_in": f(w_in)[0],
        "conv_w": np.ascontiguousarray(f(conv_w)[0].reshape(KCONV * 24, 128)),
        "a_log": np.concatenate([f(a_log_f)[0], f(a_log_b)[0]]),
        "dt_bias": np.concatenate([f(dt_bias_f)[0], f(dt_bias_b)[0]]),
        "o_norm_a": f(o_norm_a)[0], "q_a_norm": f(q_a_norm)[0], "w_q_b": f(w_q_b)[0],
        "kv_a_norm": f(kv_a_norm)[0], "w_kv_b": f(w_kv_b)[0], "w_pa": f(w_pa)[0], "w_pb": f(w_pb)[0],
        "w_out": f(w_out)[0], "norm_f": f(norm_f),
    }
    consts = host_consts()
    seqs = [x_prompt[i] for i in range(4)] + [x_sample[i] for i in range(4)]
    in_maps = [_core_map(consts, shared, s, L) for s in seqs]
    if L not in _NC_CACHE:
        _NC_CACHE[L] = build(L)
    nc = _NC_CACHE[L]
    res = run_bass_kernel_spmd(nc, in_maps, core_ids=list(range(8)))
    ys = [np.asarray(r["y"], dtype=np.float32) for r in res.results]
    y_prompt = np.stack([ys[i][:x_prompt.shape[1]] for i in range(4)], axis=0)
    y_sample = np.stack([ys[4 + i] for i in range(4)], axis=0)
    return (y_prompt, y_sample)
```

```python
import math
from contextlib import ExitStack

import numpy as np
import ml_dtypes
import concourse.bass as bass
import concourse.mybir as mybir
from concourse.bass_utils import run_bass_kernel_spmd

F32 = mybir.dt.float32
BF16 = mybir.dt.bfloat16
AF = mybir.ActivationFunctionType
ALU = mybir.AluOpType
AX = mybir.AxisListType

D_MODEL = 2048
H = 8
DK = 128
CONV_DIM = 3072
KCONV = 5
Q_LORA = 1536
KV_LORA = 512
D_ROPE = 64
D_NOPE = 128
N_IN = 11360
EPS = 1e-6
LMAX = 4096
BIG = 30000.0

C_QKV = 0
C_GA = 3072
C_SM = 4096
C_QL = 4128
C_KVL = 5664
C_KR = 6176
C_GB = 6240
C_GMA = 7264
C_GMB = 9312

ENGS = ("sp", "act", "pe", "dve", "pool")
EPOCH = 12000


class Buf:
    __slots__ = ("t", "lw", "rd", "dsem", "dcnt", "name", "chain")

    def __init__(self, t, name):
        self.t = t
        self.name = name
        self.lw = None
        self.rd = []
        self.dsem = None
        self.dcnt = 0
        self.chain = None

    def __getitem__(self, idx):
        return self.t[idx]


class Ring:
    def __init__(self, bufs):
        self.bufs = bufs
        self.i = 0

    def next(self):
        b = self.bufs[self.i % len(self.bufs)]
        self.i += 1
        return b


class Prog:
    def __init__(self, nc):
        self.nc = nc
        self.ops = {e: [] for e in ENGS}
        self.cnt = {e: 0 for e in ENGS}
        self.sems = {}
        self.seen = {e: {} for e in ENGS}
        self.dma_bufs = []
        self.nbuf = 0
        self.stack = None
        self.ndsem = 0
        self.free_dsems = {"d": [], "w": []}
        self.mute = False
        self.phase_no = 0
        self.only = 0
        self.mute_set = set()

    def sb(self, shape, dtype=F32, name=None):
        self.nbuf += 1
        name = name or f"sb{self.nbuf}"
        t = self.stack.enter_context(self.nc.sbuf_tensor(name, list(shape), dtype))
        return Buf(t, name)

    def ps(self, shape, dtype=F32, name=None):
        self.nbuf += 1
        name = name or f"ps{self.nbuf}"
        t = self.stack.enter_context(self.nc.psum_tensor(name, list(shape), dtype))
        return Buf(t, name)

    def ring(self, n, shape, dtype=F32, psum=False):
        return Ring([(self.ps if psum else self.sb)(shape, dtype) for _ in range(n)])

    def _sem(self, key):
        if key not in self.sems:
            self.sems[key] = self.nc.alloc_semaphore("s_" + "_".join(str(k) for k in key))
        return self.sems[key]

    def _engkey(self, e, n):
        ep = (n - 1) // EPOCH
        return (("e", e, ep), n - ep * EPOCH)

    def _deps(self, eng, reads, writes, skip_key=None, is_dma=False):
        deps = {}

        def add(ev, kind):
            if ev is None:
                return
            key, val, src = ev
            if kind == "waw" and skip_key is not None and key == skip_key:
                return
            if src == eng and not is_dma:
                if eng == "pe":
                    return
                if kind != "raw":
                    return
            if deps.get(key, 0) < val:
                deps[key] = val
        for b in reads:
            add(b.lw, "raw")
        for b in writes:
            add(b.lw, "waw")
            for r in b.rd:
                add(r, "war")
        return deps

    def _prune(self, eng, deps):
        waits = []
        seen = self.seen[eng]
        for key, val in deps.items():
            if seen.get(key, 0) >= val:
                continue
            seen[key] = val
            waits.append((key, val))
        return waits

    def _collect(self, eng, reads, writes):
        return self._prune(eng, self._deps(eng, reads, writes))

    def op(self, eng, meth, reads, writes, *args, **kw):
        if self.mute:
            return None
        waits = self._collect(eng, reads, writes)
        self.cnt[eng] += 1
        key, val = self._engkey(eng, self.cnt[eng])
        self._sem(key)
        self.ops[eng].append((waits, meth, args, kw, (key, 1)))
        ev = (key, val, eng)
        for b in writes:
            b.lw = ev
            b.rd = []
            b.chain = None
        for b in reads:
            if b not in writes:
                b.rd.append(ev)
        return ev

    def dma(self, eng, out_ap, in_ap, reads=(), writes=(), owner=None, **kw):
        if self.mute:
            return None
        if owner is None:
            owner = (list(writes) + list(reads))[0]
        kind = "w" if eng == "pool" else "d"
        st = owner.dsem
        if st is None:
            st = owner.dsem = {}
        if kind not in st:
            fl = self.free_dsems[kind]
            if fl:
                st[kind] = list(fl.pop())
            else:
                self.ndsem += 1
                key = (kind, self.ndsem)
                self._sem(key)
                st[kind] = [key, 0]
            self.dma_bufs.append((owner, kind))
        deps = self._deps(eng, reads, writes, skip_key=st[kind][0], is_dma=True)
        for b in writes:
            if b.lw is not None and b.lw[0] == st[kind][0] and b.chain:
                for k_, v_ in b.chain.items():
                    if deps.get(k_, 0) < v_:
                        deps[k_] = v_
        for b in writes:
            b.chain = dict(deps)
        waits = self._prune(eng, deps)
        st[kind][1] += 16
        key = st[kind][0]
        ev = (key, st[kind][1], "dma")
        kw = dict(kw)
        kw["out"] = out_ap
        kw["in_"] = in_ap
        self.ops[eng].append((waits, "dma_start", (), kw, (key, 16)))
        for b in writes:
            b.lw = ev
            b.rd = []
        for b in reads:
            b.rd.append(ev)
        return ev

    def barrier(self):
        evs = []
        for e in ENGS:
            if self.cnt[e] > 0:
                k, v = self._engkey(e, self.cnt[e])
                evs.append((k, v))
        for b, kind in self.dma_bufs:
            evs.append(tuple(b.dsem[kind]))
        for e in ENGS:
            waits = []
            for k, v in evs:
                if k[0] == "e" and k[1] == e:
                    continue
                if self.seen[e].get(k, 0) >= v:
                    continue
                self.seen[e][k] = v
                waits.append((k, v))
            if waits:
                self.ops[e].append((waits, None, None, None, None))

    def flush(self):
        nc = self.nc
        engobj = {"sp": "sync", "act": "scalar", "pe": "tensor", "dve": "vector", "pool": "gpsimd"}
        with nc.Block() as block:
            for e in ENGS:
                lst = self.ops[e]

                def body(engine, lst=lst):
                    for waits, meth, args, kw, inc in lst:
                        for k, v in waits:
                            engine.wait_ge(self.sems[k], v)
                        if meth is not None:
                            ins = getattr(engine, meth)(*args, **kw)
                            ins.then_inc(self.sems[inc[0]], inc[1])
                getattr(block, engobj[e])(body)
        self.ops = {e: [] for e in ENGS}

    def begin(self):
        self.stack = ExitStack()
        self.phase_no += 1
        self.mute = (bool(self.only) and self.phase_no != self.only) or (self.phase_no in self.mute_set)

    def end(self):
        self.barrier()
        self.flush()
        for b, kind in self.dma_bufs:
            self.free_dsems[kind].append(tuple(b.dsem[kind]))
        self.dma_bufs = []
        self.stack.close()
        self.stack = None


def host_consts():
    c = {}
    c["ident"] = np.eye(128, dtype=np.float32)
    j = np.arange(128)[:, None]
    i = np.arange(128)[None, :]
    tri = np.zeros((2, 128, 128), np.float32)
    tri[0] = (j <= i)
    tri[1] = (j >= i)
    c["tri"] = tri
    nm = np.zeros((2, 128, 128), np.float32)
    nm[0] = np.where(i >= j, 0.0, -BIG)
    nm[1] = np.where(i <= j, 0.0, -BIG)
    c["negmask"] = nm
    lv = np.zeros((2, 128, 7, 128), np.float32)
    for si, s in enumerate([1, 2, 4, 8, 16, 32, 64]):
        same2 = (i // (2 * s)) == (j // (2 * s))
        diff1 = (i // s) != (j // s)
        lv[0, :, si, :] = -1.0 * (same2 & diff1 & (i > j))
        lv[1, :, si, :] = -1.0 * (same2 & diff1 & (i < j))
    c["lvl"] = lv
    pos = np.arange(LMAX, dtype=np.float32)
    inv_freq = (10000.0 ** (-np.arange(0, D_ROPE, 2, dtype=np.float32) / D_ROPE)).astype(np.float32)
    ang = pos[None, :] * inv_freq[:, None]
    c["cos4"] = np.tile(np.cos(ang).astype(np.float32), (4, 1))
    c["sin4"] = np.tile(np.sin(ang).astype(np.float32), (4, 1))
    return c


def build(L, dbg=(), _STOP=0):
    nc = bass.Bass("TRN2", target_bir_lowering=False)
    NT = L // 128
    NB = L // 512
    PART = min(L, 2048)
    NPART = L // PART

    def din(name, shape, dt=F32):
        return nc.dram_tensor(name, list(shape), dt, kind="ExternalInput").ap()

    def dscr(name, shape, dt=F32):
        kind = "ExternalOutput" if name in dbg else "Internal"
        return nc.dram_tensor(name, list(shape), dt, kind=kind).ap()

    x = din("x", [L, D_MODEL])
    norm_in = din("norm_in", [D_MODEL])
    w_in = din("w_in", [D_MODEL, N_IN])
    conv_w = din("conv_w", [KCONV * 24, 128])
    a_log = din("a_log", [16])
    dt_bias = din("dt_bias", [16])
    o_norm_a = din("o_norm_a", [128])
    q_a_norm = din("q_a_norm", [Q_LORA])
    w_q_b = din("w_q_b", [Q_LORA, 1536])
    kv_a_norm = din("kv_a_norm", [KV_LORA])
    w_kv_b = din("w_kv_b", [KV_LORA, 2048])
    w_pa = din("w_pa", [1024, D_MODEL])
    w_pb = din("w_pb", [1024, D_MODEL])
    w_out = din("w_out", [D_MODEL, D_MODEL])
    norm_f = din("norm_f", [D_MODEL])
    kbias = din("kbias", [128, LMAX // 128])
    tmask_d = din("tmask", [128, LMAX])
    c_ident = din("ident", [128, 128])
    c_tri = din("tri", [2, 128, 128])
    c_negmask = din("negmask", [2, 128, 128])
    c_lvl = din("lvl", [2, 128, 7, 128])
    c_cos4 = din("cos4", [128, LMAX])
    c_sin4 = din("sin4", [128, LMAX])

    y = nc.dram_tensor("y", [L, D_MODEL], F32, kind="ExternalOutput").ap()

    QF_T = dscr("QF_T", [8 * 192, L], BF16)
    KN_T = dscr("KN_T", [1024, L], BF16)
    KPE_T = dscr("KPE_T", [64, L], BF16)
    VB = dscr("VB", [L, 1024], BF16)
    OB_T = dscr("OB_T", [1024, L], BF16)
    QKV_T = dscr("QKV_T", [CONV_DIM, L], BF16)
    GA_T = dscr("GA_T", [1024, L], BF16)
    GB_T = dscr("GB_T", [1024, L], BF16)
    GMA_T = dscr("GMA_T", [2048, L], BF16)
    GMB_T = dscr("GMB_T", [2048, L], BF16)
    QL_T = dscr("QL_T", [Q_LORA, L])
    KVL_T = dscr("KVL_T", [KV_LORA, L])
    KR_T = dscr("KR_T", [64, L])
    AB = dscr("AB", [L, 32])
    QT = dscr("QT", [1024, L], BF16)
    KT = dscr("KT", [1024, L], BF16)
    VT = dscr("VT", [1024, L], BF16)
    OA_T = dscr("OA_T", [1024, L], BF16)

    P = Prog(nc)

    P.begin()
    identf = P.sb([128, 128], F32)
    identb = P.sb([128, 128], BF16)
    gbc = P.sb([128, D_MODEL], F32)
    P.dma("sp", identf[:], c_ident, writes=[identf])
    P.dma("sp", gbc[:], norm_in.partition_broadcast(128), writes=[gbc])
    P.op("dve", "tensor_copy", [identf], [identb], out=identb[:], in_=identf[:])
    xnT = P.sb([128, 16, PART], BF16)
    xring = P.ring(2, [128, D_MODEL], F32)
    xsring = P.ring(2, [128, D_MODEL], BF16)
    junk = P.sb([128, D_MODEL], BF16)
    stat = P.ring(4, [128, 4], F32)
    ptr = P.ring(2, [128, 8, 128], BF16, psum=True)
    pmm = P.ring(4, [128, 512], F32, psum=True)
    wring = P.ring(2, [128, 16, 512], BF16)
    oring = P.ring(3, [128, PART], F32)
    oringb = P.ring(2, [128, PART], BF16)
    wsm = P.sb([128, 16, 32], BF16)
    absg = P.sb([128, PART // 128, 32], F32)
    P.dma("pool", wsm[:], w_in[:, C_SM:C_SM + 32].rearrange("(c p) n -> p c n", p=128), writes=[wsm])

    groups = []

    def add_group(c0, n, dest, func):
        o = 0
        while o < n:
            w = min(512, n - o)
            groups.append((c0 + o, w, dest, o, func))
            o += w
    add_group(C_QKV, 3072, QKV_T, None)
    add_group(C_GA, 1024, GA_T, AF.Silu)
    add_group(C_QL, Q_LORA, QL_T, None)
    add_group(C_KVL, KV_LORA, KVL_T, None)
    add_group(C_KR, 64, KR_T, None)
    add_group(C_GB, 1024, GB_T, AF.Silu)
    add_group(C_GMA, 2048, GMA_T, AF.Sigmoid)
    add_group(C_GMB, 2048, GMB_T, AF.Sigmoid)
    groups.sort(key=lambda g: {None: 0, AF.Silu: 1, AF.Sigmoid: 2}[g[4]])

    evq = 0
    for part in range(NPART):
        t0 = part * PART
        for tt in range(PART // 128):
            xt = xring.next()
            P.dma("sp", xt[:], x[t0 + tt * 128: t0 + (tt + 1) * 128, :], writes=[xt])
            st = stat.next()
            P.op("act", "activation", [xt], [junk, st], out=junk[:], in_=xt[:], func=AF.Square, accum_out=st[:, 0:1])
            P.op("dve", "tensor_scalar", [st], [st], out=st[:, 1:2], in0=st[:, 0:1], scalar1=1.0 / D_MODEL, scalar2=EPS,
                 op0=ALU.mult, op1=ALU.add)
            P.op("act", "activation", [st], [st], out=st[:, 2:3], in_=st[:, 1:2], func=AF.Sqrt)
            P.op("dve", "reciprocal", [st], [st], out=st[:, 3:4], in_=st[:, 2:3])
            xs = xsring.next()
            P.op("dve", "scalar_tensor_tensor", [xt, st, gbc], [xs], out=xs[:], in0=xt[:], scalar=st[:, 3:4], in1=gbc[:],
                 op0=ALU.mult, op1=ALU.mult)
            for hh in range(2):
                pt = ptr.next()
                for c in range(8):
                    cc = hh * 8 + c
                    P.op("pe", "transpose", [xs, identb], [pt], out=pt[:, c, :], in_=xs[:, cc * 128:(cc + 1) * 128],
                         identity=identb[:])
                if hh == 0:
                    P.op("act", "copy", [pt], [xnT], out=xnT[:, 0:8, tt * 128:(tt + 1) * 128], in_=pt[:])
                else:
                    P.op("dve", "tensor_copy", [pt], [xnT], out=xnT[:, 8:16, tt * 128:(tt + 1) * 128], in_=pt[:])
        for tt in range(PART // 128):
            pm = pmm.next()
            for c in range(16):
                P.op("pe", "matmul", [xnT, wsm], [pm], pm[:, 0:32], lhsT=xnT[:, c, tt * 128:(tt + 1) * 128], rhs=wsm[:, c, :],
                     start=(c == 0), stop=(c == 15))
            P.op("dve", "tensor_copy", [pm], [absg], out=absg[:, tt, :], in_=pm[:, 0:32])
        for a0 in range(0, PART // 128, 8):
            a1 = min(PART // 128, a0 + 8)
            P.dma("sp", AB[t0 + a0 * 128:t0 + a1 * 128, :].rearrange("(t p) c -> p t c", p=128), absg[:, a0:a1, :], reads=[absg])
        for (c0, ncol, dest, r0, func) in groups:
            wt = wring.next()
            P.dma("pool", wt[:, :, 0:ncol], w_in[:, c0:c0 + ncol].rearrange("(c p) n -> p c n", p=128), writes=[wt])
            for ct in range((ncol + 127) // 128):
                m = min(128, ncol - ct * 128)
                ot = oringb.next() if (func is not None or dest is QKV_T) else oring.next()
                for tb in range(PART // 512):
                    pm = pmm.next()
                    for c in range(16):
                        P.op("pe", "matmul", [xnT, wt], [pm], pm[0:m, :], lhsT=wt[:, c, ct * 128:ct * 128 + m],
                             rhs=xnT[:, c, tb * 512:(tb + 1) * 512], start=(c == 0), stop=(c == 15))
                    if func is not None:
                        P.op("act", "activation", [pm], [ot], out=ot[0:m, tb * 512:(tb + 1) * 512], in_=pm[0:m, :], func=func)
                    else:
                        evq += 1
                        if evq % 2 == 0:
                            P.op("act", "copy", [pm], [ot], out=ot[0:m, tb * 512:(tb + 1) * 512], in_=pm[0:m, :])
                        else:
                            P.op("dve", "tensor_copy", [pm], [ot], out=ot[0:m, tb * 512:(tb + 1) * 512], in_=pm[0:m, :])
                rr = r0 + ct * 128
                P.dma("act", dest[rr:rr + m, t0:t0 + PART], ot[0:m, :], reads=[ot])
    P.end()
    if _STOP == 1:
        nc._P = P
        return nc

    P.begin()
    identf = P.sb([128, 128], F32)
    P.dma("sp", identf[:], c_ident, writes=[identf])
    onesb = P.sb([128, 128], BF16)
    P.op("dve", "memset", [], [onesb], onesb[:], 1.0)
    cwr = P.sb([120, 128], F32)
    P.dma("sp", cwr[:], conv_w, writes=[cwr])
    cw = P.sb([128, 120], F32)
    pmm = P.ring(4, [128, 512], F32, psum=True)
    pm = pmm.next()
    P.op("pe", "matmul", [cwr, identf], [pm], pm[:, 0:120], lhsT=cwr[:], rhs=identf[0:120, 0:120], start=True, stop=True)
    P.op("dve", "tensor_copy", [pm], [cw], out=cw[:], in_=pm[:, 0:120])
    tmask = P.sb([128, L], F32)
    P.dma("act", tmask[:], tmask_d[:, 0:L], writes=[tmask])
    epst = P.sb([128, 2], F32)
    P.op("dve", "memset", [], [epst], epst[:, 0:1], EPS)
    P.op("dve", "memset", [], [epst], epst[:, 1:2], EPS * DK)
    identb2 = P.sb([128, 128], BF16)
    P.op("dve", "tensor_copy", [identf], [identb2], out=identb2[:], in_=identf[:])
    xpr = P.ring(2, [128, L + 4], BF16)
    for b in xpr.bufs:
        P.op("pool", "memset", [], [b], b[:, 0:2], 0.0)
        P.op("pool", "memset", [], [b], b[:, L + 2:L + 4], 0.0)
    dgw_r = P.ring(2, [128, KCONV, 128], BF16)
    slr = P.ring(2, [128, L], F32)
    sqr = P.ring(2, [128, L], BF16)
    outr = P.ring(2, [128, L], BF16)
    rnr = P.ring(3, [128, 512], F32)
    cw3 = cw[:].rearrange("p (k c) -> p k c", c=24)
    for cc in range(24):
        kind = cc // 8
        xp = xpr.next()
        P.dma("sp", xp[:, 2:L + 2], QKV_T[cc * 128:(cc + 1) * 128, :], writes=[xp])
        dgw = dgw_r.next()
        P.op("dve", "tensor_tensor", [identb2, cw], [dgw], out=dgw[:], in0=identb2[:].unsqueeze(1).broadcast_to([128, KCONV, 128]),
             in1=cw3[:, :, cc:cc + 1].broadcast_to([128, KCONV, 128]), op=ALU.mult)
        sl = slr.next()
        for tb in range(NB):
            pc = pmm.next()
            for k in range(KCONV):
                P.op("pe", "matmul", [dgw, xp], [pc], pc[:], lhsT=dgw[:, k, :], rhs=xp[:, tb * 512 + k:tb * 512 + k + 512],
                     start=(k == 0), stop=(k == KCONV - 1))
            P.op("act", "activation", [pc], [sl], out=sl[:, tb * 512:(tb + 1) * 512], in_=pc[:], func=AF.Silu)
        P.op("dve", "tensor_tensor", [sl, tmask], [sl], out=sl[:], in0=sl[:], in1=tmask[:], op=ALU.mult)
        ob = outr.next()
        if kind == 2:
            P.op("pool", "tensor_copy", [sl], [ob], out=ob[:], in_=sl[:])
            P.dma("pool", VT[(cc - 16) * 128:(cc - 15) * 128, :], ob[:], reads=[ob])
        else:
            sq = sqr.next()
            P.op("pool", "tensor_tensor", [sl], [sq], out=sq[:], in0=sl[:], in1=sl[:], op=ALU.mult)
            for tb in range(NB):
                pm = pmm.next()
                P.op("pe", "matmul", [onesb, sq], [pm], pm[:], lhsT=onesb[:], rhs=sq[:, tb * 512:(tb + 1) * 512], start=True, stop=True)
                rn = rnr.next()
                if kind == 0:
                    P.op("act", "activation", [pm, epst], [rn], out=rn[:], in_=pm[:], func=AF.Sqrt, bias=epst[:, 1:2], scale=float(DK))
                else:
                    P.op("act", "activation", [pm, epst], [rn], out=rn[:], in_=pm[:], func=AF.Sqrt, bias=epst[:, 0:1])
                P.op("dve", "reciprocal", [rn], [rn], out=rn[:], in_=rn[:])
                P.op("dve", "tensor_tensor", [sl, rn], [ob], out=ob[:, tb * 512:(tb + 1) * 512],
                     in0=sl[:, tb * 512:(tb + 1) * 512], in1=rn[:], op=ALU.mult)
            dst = QT if kind == 0 else KT
            hh = cc % 8
            P.dma("pool", dst[hh * 128:(hh + 1) * 128, :], ob[:], reads=[ob])
    P.end()
    if _STOP == 2:
        nc._P = P
        return nc


    P.begin()
    identf = P.sb([128, 128], F32)
    P.dma("sp", identf[:], c_ident, writes=[identf])
    onesb = P.sb([128, 128], BF16)
    P.op("dve", "memset", [], [onesb], onesb[:], 1.0)
    wq = P.sb([128, 12, 1536], BF16)
    P.dma("pool", wq[:], w_q_b.rearrange("(c p) n -> p c n", p=128), writes=[wq])
    wqt = P.sb([128, 2, 12, 256], BF16)
    wq4 = wq[:].rearrange("p c (h r) -> p c h r", r=192)
    P.op("dve", "tensor_copy", [wq], [wqt], out=wqt[:, 0, :, :].rearrange("p c (h r) -> p c h r", r=32), in_=wq4[:, :, :, 128:160])
    P.op("pool", "tensor_copy", [wq], [wqt], out=wqt[:, 1, :, :].rearrange("p c (h r) -> p c h r", r=32), in_=wq4[:, :, :, 160:192])
    wkv = P.sb([128, 4, 2048], BF16)
    P.dma("pool", wkv[:], w_kv_b.rearrange("(c p) n -> p c n", p=128), writes=[wkv])
    pm_r = P.ring(6, [128, 512], F32, psum=True)
    nrm_rows = P.sb([16, 128], F32)
    P.dma("sp", nrm_rows[0:12, :], q_a_norm.rearrange("(c p) -> c p", p=128), writes=[nrm_rows])
    P.dma("sp", nrm_rows[12:16, :], kv_a_norm.rearrange("(c p) -> c p", p=128), writes=[nrm_rows])
    nrm = P.sb([128, 16], F32)
    pm = pm_r.next()
    P.op("pe", "matmul", [nrm_rows, identf], [pm], pm[:, 0:16], lhsT=nrm_rows[:], rhs=identf[0:16, 0:16], start=True, stop=True)
    P.op("dve", "tensor_copy", [pm], [nrm], out=nrm[:], in_=pm[:, 0:16])
    wkv3 = wkv[:].rearrange("p c (h r) -> p c h r", r=256)
    ql_r = P.ring(1, [128, 12, 512], F32)
    sq_r = P.ring(1, [128, 12, 512], BF16)
    cq_r = P.ring(2, [128, 12, 512], BF16)
    kvl_r = P.ring(2, [128, 4, 512], F32)
    ckv_r = P.ring(2, [128, 4, 512], BF16)
    rr_r = P.ring(2, [128, 512], F32)
    cs_r = P.ring(2, [128, 2, 512], F32)
    st_r = P.ring(4, [128, 512], BF16)
    tmp_r = P.ring(4, [128, 512], F32)
    vb_r = P.ring(2, [128, 8, 128], BF16)
    kr_r = P.ring(2, [32, 2, 512], F32)
    evq = 0
    for tb in range(NB):
        blk = slice(tb * 512, (tb + 1) * 512)
        cs = cs_r.next()
        P.dma("sp", cs[:, 0, :], c_cos4[:, blk], writes=[cs])
        P.dma("sp", cs[:, 1, :], c_sin4[:, blk], writes=[cs])

        def latent_norm(src, nch, ncol0, ring_in, ring_out, dim):
            lt = ring_in.next()
            P.dma("act", lt[:], src.rearrange("(c p) t -> p c t", p=128)[:, :, blk], writes=[lt])
            sq = sq_r.next()
            P.op("pool", "tensor_tensor", [lt], [sq], out=sq[:, 0:nch, :], in0=lt[:], in1=lt[:], op=ALU.mult)
            pss = pm_r.next()
            for c in range(nch):
                P.op("pe", "matmul", [onesb, sq], [pss], pss[:], lhsT=onesb[:], rhs=sq[:, c, :], start=(c == 0), stop=(c == nch - 1))
            rr = rr_r.next()
            P.op("dve", "tensor_scalar", [pss], [rr], out=rr[:], in0=pss[:], scalar1=1.0 / dim, scalar2=EPS, op0=ALU.mult, op1=ALU.add)
            P.op("act", "activation", [rr], [rr], out=rr[:], in_=rr[:], func=AF.Sqrt)
            P.op("dve", "reciprocal", [rr], [rr], out=rr[:], in_=rr[:])
            ct_ = ring_out.next()
            for c in range(nch):
                P.op("dve", "scalar_tensor_tensor", [lt, nrm, rr], [ct_], out=ct_[:, c, :], in0=lt[:, c, :],
                     scalar=nrm[:, ncol0 + c:ncol0 + c + 1], in1=rr[:], op0=ALU.mult, op1=ALU.mult)
            return ct_

        def rope(pT1, pT2, np_, dst1, dst2):
            a, b, c_, d_ = tmp_r.next(), tmp_r.next(), tmp_r.next(), tmp_r.next()
            P.op("dve", "tensor_tensor", [pT1[0], cs], [a], out=a[0:np_, :], in0=pT1[1], in1=cs[0:np_, 0, :], op=ALU.mult)
            P.op("dve", "tensor_tensor", [pT2[0], cs], [b], out=b[0:np_, :], in0=pT2[1], in1=cs[0:np_, 1, :], op=ALU.mult)
            P.op("dve", "tensor_tensor", [pT1[0], cs], [c_], out=c_[0:np_, :], in0=pT1[1], in1=cs[0:np_, 1, :], op=ALU.mult)
            P.op("dve", "tensor_tensor", [pT2[0], cs], [d_], out=d_[0:np_, :], in0=pT2[1], in1=cs[0:np_, 0, :], op=ALU.mult)
            o1, o2 = st_r.next(), st_r.next()
            P.op("pool", "tensor_tensor", [a, b], [o1], out=o1[0:np_, :], in0=a[0:np_, :], in1=b[0:np_, :], op=ALU.subtract)
            P.op("pool", "tensor_tensor", [c_, d_], [o2], out=o2[0:np_, :], in0=c_[0:np_, :], in1=d_[0:np_, :], op=ALU.add)
            for (o, dst) in ((o1, dst1), (o2, dst2)):
                for (psl, dap) in dst:
                    P.dma("sp", dap, o[psl, :], reads=[o])

        cq = latent_norm(QL_T, 12, 0, ql_r, cq_r, Q_LORA)
        for h in range(8):
            pn = pm_r.next()
            for c in range(12):
                P.op("pe", "matmul", [wq, cq], [pn], pn[:], lhsT=wq[:, c, h * 192:h * 192 + 128], rhs=cq[:, c, :], start=(c == 0), stop=(c == 11))
            qn = st_r.next()
            evq += 1
            if evq % 2:
                P.op("act", "copy", [pn], [qn], out=qn[:], in_=pn[:])
            else:
                P.op("dve", "tensor_copy", [pn], [qn], out=qn[:], in_=pn[:])
            P.dma("pool", QF_T[h * 192:h * 192 + 128, blk], qn[:], reads=[qn])
        for g in range(2):
            pT1, pT2 = pm_r.next(), pm_r.next()
            for (pt_, half) in ((pT1, 0), (pT2, 1)):
                for c in range(12):
                    P.op("pe", "matmul", [wqt, cq], [pt_], pt_[:], lhsT=wqt[:, half, c, g * 128:(g + 1) * 128], rhs=cq[:, c, :],
                         start=(c == 0), stop=(c == 11))
            QF3 = QF_T.rearrange("(h r) t -> h r t", r=192)
            dst1 = [(slice(32 * hh, 32 * hh + 32), QF3[4 * g + hh, 128:160, blk]) for hh in range(4)]
            dst2 = [(slice(32 * hh, 32 * hh + 32), QF3[4 * g + hh, 160:192, blk]) for hh in range(4)]
            rope((pT1, pT1[:]), (pT2, pT2[:]), 128, dst1, dst2)
        ckv = latent_norm(KVL_T, 4, 12, kvl_r, ckv_r, KV_LORA)
        for h in range(8):
            pn = pm_r.next()
            for c in range(4):
                P.op("pe", "matmul", [wkv, ckv], [pn], pn[:], lhsT=wkv[:, c, h * 256:h * 256 + 128], rhs=ckv[:, c, :], start=(c == 0), stop=(c == 3))
            kn = st_r.next()
            evq += 1
            if evq % 2:
                P.op("act", "copy", [pn], [kn], out=kn[:], in_=pn[:])
            else:
                P.op("dve", "tensor_copy", [pn], [kn], out=kn[:], in_=pn[:])
            P.dma("pool", KN_T[h * 128:(h + 1) * 128, blk], kn[:], reads=[kn])
        for st in range(4):
            vb = vb_r.next()
            for g in range(2):
                pv_ = pm_r.next()
                for c in range(4):
                    P.op("pe", "matmul", [ckv, wkv], [pv_], pv_[:].rearrange("p (h e) -> p h e", e=128), lhsT=ckv[:, c, st * 128:(st + 1) * 128],
                         rhs=wkv3[:, c, 4 * g:4 * g + 4, 128:256], start=(c == 0), stop=(c == 3))
                if g == 0:
                    P.op("act", "copy", [pv_], [vb], out=vb[:, 0:4, :], in_=pv_[:].rearrange("p (h e) -> p h e", e=128))
                else:
                    P.op("dve", "tensor_copy", [pv_], [vb], out=vb[:, 4:8, :], in_=pv_[:].rearrange("p (h e) -> p h e", e=128))
            r0 = tb * 512 + st * 128
            P.dma("pool", VB[r0:r0 + 128, :].rearrange("p (h e) -> p h e", e=128), vb[:], reads=[vb])
        kr = kr_r.next()
        P.dma("act", kr[:, 0, :], KR_T[0:32, blk], writes=[kr])
        P.dma("act", kr[:, 1, :], KR_T[32:64, blk], writes=[kr])
        rope((kr, kr[:, 0, :]), (kr, kr[:, 1, :]), 32, [(slice(0, 32), KPE_T[0:32, blk])], [(slice(0, 32), KPE_T[32:64, blk])])
    P.end()
    if _STOP == 4:
        nc._P = P
        return nc

    OF = dscr("OF", [L, 1024])
    P.begin()
    identf = P.sb([128, 128], F32)
    identb = P.sb([128, 128], BF16)
    onesf = P.sb([128, 128], F32)
    P.dma("sp", identf[:], c_ident, writes=[identf])
    P.op("dve", "tensor_copy", [identf], [identb], out=identb[:], in_=identf[:])
    P.op("dve", "memset", [], [onesf], onesf[:], 1.0)
    tri = P.sb([128, 2, 128], F32)
    P.dma("sp", tri[:], c_tri.rearrange("d p i -> p d i"), writes=[tri])
    negm = P.sb([128, 2, 128], F32)
    P.dma("sp", negm[:], c_negmask.rearrange("d p i -> p d i"), writes=[negm])
    lvl = P.sb([128, 2, 7, 128], BF16)
    P.dma("pool", lvl[:], c_lvl.rearrange("d p s i -> p d s i"), writes=[lvl])
    onorm = P.sb([128, 1], F32)
    P.dma("sp", onorm[:], o_norm_a.rearrange("(p o) -> p o", o=1), writes=[onorm])
    ABt = P.sb([128, NT, 32], F32)
    for t0_ in range(0, NT, 8):
        t1_ = min(NT, t0_ + 8)
        P.dma("sp", ABt[:, t0_:t1_, :], AB[t0_ * 128:t1_ * 128, :].rearrange("(t p) c -> p t c", p=128), writes=[ABt])
    dtb = P.sb([128, 16], F32)
    alg = P.sb([128, 16], F32)
    P.dma("sp", dtb[:], dt_bias.partition_broadcast(128), writes=[dtb])
    P.dma("sp", alg[:], a_log.partition_broadcast(128), writes=[alg])
    negA = P.sb([128, 16], F32)
    P.op("act", "activation", [alg], [negA], out=negA[:], in_=alg[:], func=AF.Exp)
    P.op("dve", "tensor_scalar", [negA], [negA], out=negA[:], in0=negA[:], scalar1=-1.0, scalar2=None, op0=ALU.mult)
    gt = P.sb([128, NT, 16], F32)
    beta = P.sb([128, NT, 16], F32)
    gcs = P.sb([128, NT, 16], F32)
    ngc = P.sb([128, NT, 16], F32)
    eg = P.sb([128, NT, 16], F32)
    kd = P.sb([128, NT, 16], F32)
    egl = P.sb([128, NT, 16], F32)
    pbf = P.ring(2, [128, 8, 128], BF16, psum=True)
    pf = P.ring(6, [128, 4, 128], F32, psum=True)

    def bc16(ap16):
        return ap16.unsqueeze(1).broadcast_to([128, NT, 16])
    P.op("dve", "tensor_tensor", [ABt, dtb], [gt], out=gt[:], in0=ABt[:, :, 0:16], in1=bc16(dtb[:]), op=ALU.add)
    P.op("act", "activation", [gt], [gt], out=gt[:], in_=gt[:], func=AF.Exp)
    P.op("dve", "tensor_scalar", [gt], [gt], out=gt[:], in0=gt[:], scalar1=1.0, scalar2=None, op0=ALU.add)
    P.op("act", "activation", [gt], [gt], out=gt[:], in_=gt[:], func=AF.Ln)
    P.op("dve", "tensor_tensor", [gt, negA], [gt], out=gt[:], in0=gt[:], in1=bc16(negA[:]), op=ALU.mult)
    P.op("act", "activation", [ABt], [beta], out=beta[:], in_=ABt[:, :, 16:32], func=AF.Exp, scale=-1.0)
    P.op("dve", "tensor_scalar", [beta], [beta], out=beta[:], in0=beta[:], scalar1=1.0, scalar2=None, op0=ALU.add)
    P.op("dve", "reciprocal", [beta], [beta], out=beta[:], in_=beta[:])
    pg = pf.next()
    pgv = pg[:].rearrange("p a b -> p (a b)")
    for d in range(2):
        P.op("pe", "matmul", [tri, gt], [pg], pgv[:, d * NT * 8:(d + 1) * NT * 8], lhsT=tri[:, d, :], rhs=gt[:, :, d * 8:(d + 1) * 8],
             start=True, stop=True)
    for d in range(2):
        P.op("dve", "tensor_copy", [pg], [gcs], out=gcs[:, :, d * 8:(d + 1) * 8],
             in_=pgv[:, d * NT * 8:(d + 1) * NT * 8].rearrange("p (t h) -> p t h", h=8))
    ptot = pf.next()
    ptv = ptot[:].rearrange("p a b -> p (a b)")
    gtv = gt[:].rearrange("p t c -> p (t c)")
    for c0_ in range(0, NT * 16, 256):
        c1_ = min(NT * 16, c0_ + 256)
        P.op("pe", "matmul", [onesf, gt], [ptot], ptv[:, c0_:c1_], lhsT=onesf[:], rhs=gtv[:, c0_:c1_], start=True, stop=True)
    ptv3 = ptv[:, 0:NT * 16].rearrange("p (t c) -> p t c", c=16)
    P.op("act", "activation", [ptot], [egl], out=egl[:], in_=ptv3, func=AF.Exp)
    P.op("dve", "tensor_tensor", [ptot, gcs], [kd], out=kd[:], in0=ptv3, in1=gcs[:], op=ALU.subtract)
    P.op("act", "activation", [kd], [kd], out=kd[:], in_=kd[:], func=AF.Exp)
    P.op("act", "activation", [gcs], [eg], out=eg[:], in_=gcs[:], func=AF.Exp)
    P.op("dve", "tensor_scalar", [gcs], [ngc], out=ngc[:], in0=gcs[:], scalar1=-1.0, scalar2=None, op0=ALU.mult)

    S32 = P.sb([128, 8, 128], F32)
    Sb = P.sb([128, 8, 128], BF16)
    kt_r = P.ring(3, [128, 8, 128], BF16)
    qt_r = P.ring(3, [128, 8, 128], BF16)
    vt_r = P.ring(3, [128, 8, 128], BF16)
    kdec_r = P.ring(2, [128, 8, 128], BF16)
    vtok_r = P.ring(2, [128, 8, 128], BF16)
    dg_r = P.ring(2, [128, 8, 128], F32)
    de_r = P.ring(2, [128, 8, 128], F32)
    E_r = P.ring(4, [128, 4, 128], F32)
    AT_r = P.ring(4, [128, 4, 128], BF16)
    attn_r = P.ring(4, [128, 4, 128], BF16)
    kgt_r = P.ring(4, [128, 4, 128], BF16)
    qgt_r = P.ring(4, [128, 4, 128], BF16)
    nat_r = P.ring(4, [128, 4, 7, 128], BF16)
    D_r = P.ring(4, [128, 4, 128], BF16)
    DT_r = P.ring(4, [128, 4, 128], BF16)
    p1_r = P.ring(4, [128, 4, 128], BF16)
    TT_r = P.ring(4, [128, 4, 128], BF16)
    R_r = P.ring(4, [128, 4, 128], BF16)
    VN_r = P.ring(4, [128, 4, 128], BF16)
    O_r = P.ring(2, [128, 8, 128], F32)
    of_r = P.ring(3, [128, 8, 128], F32)
    ga_r = P.ring(3, [128, 8, 128], BF16)
    osum_r = P.ring(2, [128, 4, 128], F32)
    osq_r = P.ring(2, [128, 4, 128], F32)
    on_r = P.ring(2, [128, 4, 128], F32)
    ost_r = P.ring(4, [128, 4, 4], F32)
    oa_r = P.ring(2, [128, 8, 128], BF16)
    identb_bc = identb[:].unsqueeze(1).broadcast_to([128, 4, 128])

    for d in range(2):
        P.op("dve", "memset", [], [S32], S32[:], 0.0)
        P.op("pool", "memset", [], [Sb], Sb[:], 0.0)
        order = range(NT) if d == 0 else range(NT - 1, -1, -1)
        d8 = d * 8
        def tile_gen(t):
            tok = slice(t * 128, (t + 1) * 128)
            KTt, QTt, VTt = kt_r.next(), qt_r.next(), vt_r.next()
            P.dma("sp", KTt[:], KT.rearrange("(h p) t -> p h t", p=128)[:, :, tok], writes=[KTt])
            P.dma("act", QTt[:], QT.rearrange("(h p) t -> p h t", p=128)[:, :, tok], writes=[QTt])
            P.dma("sp", VTt[:], VT.rearrange("(h p) t -> p h t", p=128)[:, :, tok], writes=[VTt])
            if d == 1:
                OFt, GAt = of_r.next(), ga_r.next()
                P.dma("sp", OFt[:], OF[tok, :].rearrange("p (h e) -> p h e", h=8), writes=[OFt])
                P.dma("act", GAt[:], GA_T.rearrange("(h p) t -> p h t", p=128)[:, :, tok], writes=[GAt])
                OAt = oa_r.next()
            else:
                Ot = O_r.next()
            pk, pv = pbf.next(), pbf.next()
            for h in range(8):
                P.op("pe", "transpose", [KTt, identb], [pk], out=pk[:, h, :], in_=KTt[:, h, :], identity=identb[:])
            for h in range(8):
                P.op("pe", "transpose", [VTt, identb], [pv], out=pv[:, h, :], in_=VTt[:, h, :], identity=identb[:])
            kdec, vtok = kdec_r.next(), vtok_r.next()
            P.op("dve", "tensor_tensor", [pk, kd], [kdec], out=kdec[:], in0=pk[:],
                 in1=kd[:, t, d8:d8 + 8].unsqueeze(2).broadcast_to([128, 8, 128]), op=ALU.mult)
            P.op("act", "copy", [pv], [vtok], out=vtok[:], in_=pv[:])
            dg, de = dg_r.next(), de_r.next()
            idbc8 = identf[:].unsqueeze(1).broadcast_to([128, 8, 128])
            P.op("pool", "tensor_tensor", [identf, gcs], [dg], out=dg[:], in0=idbc8,
                 in1=gcs[:, t, d8:d8 + 8].unsqueeze(2).broadcast_to([128, 8, 128]), op=ALU.mult)
            P.op("pool", "tensor_tensor", [identf, eg], [de], out=de[:], in0=idbc8,
                 in1=eg[:, t, d8:d8 + 8].unsqueeze(2).broadcast_to([128, 8, 128]), op=ALU.mult)
            GR = (0, 1)
            hsl = [range(G * 4, G * 4 + 4) for G in GR]
            gsls = [slice(G * 4, G * 4 + 4) for G in GR]
            st_ = [dict() for _ in GR]
            for G in GR:
                hs = hsl[G]
                pKK, pBC = pf.next(), pf.next()
                for hh, h in enumerate(hs):
                    P.op("pe", "matmul", [KTt], [pKK], pKK[:, hh, :], lhsT=KTt[:, h, :], rhs=KTt[:, h, :], start=True, stop=True)
                for hh, h in enumerate(hs):
                    P.op("pe", "matmul", [onesf, dg], [pBC], pBC[:, hh, :], lhsT=onesf[:], rhs=dg[:, h, :], start=True, stop=False)
                    P.op("pe", "matmul", [identf, negm], [pBC], pBC[:, hh, :], lhsT=identf[:], rhs=negm[:, d, :], start=False, stop=True)
                E = E_r.next()
                for hh, h in enumerate(hs):
                    P.op("act", "activation", [pBC, ngc], [E], out=E[:, hh, :], in_=pBC[:, hh, :], func=AF.Exp,
                         bias=ngc[:, t, d8 + h:d8 + h + 1])
                AT = AT_r.next()
                for hh, h in enumerate(hs):
                    P.op("dve", "scalar_tensor_tensor", [pKK, beta, E], [AT], out=AT[:, hh, :], in0=pKK[:, hh, :],
                         scalar=beta[:, t, d8 + h:d8 + h + 1], in1=E[:, hh, :], op0=ALU.mult, op1=ALU.mult)
                NAT = nat_r.next()
                P.op("pool", "tensor_tensor", [AT, lvl], [NAT], out=NAT[:],
                     in0=AT[:].unsqueeze(2).broadcast_to([128, 4, 7, 128]),
                     in1=lvl[:, d, :, :].unsqueeze(1).broadcast_to([128, 4, 7, 128]), op=ALU.mult)
                st_[G].update(E=E, NAT=NAT)
            for G in GR:
                hs = hsl[G]
                gsl = gsls[G]
                E = st_[G]["E"]
                pQK, pEG = pf.next(), pf.next()
                for hh, h in enumerate(hs):
                    P.op("pe", "matmul", [KTt, QTt], [pQK], pQK[:, hh, :], lhsT=KTt[:, h, :], rhs=QTt[:, h, :], start=True, stop=True)
                for hh, h in enumerate(hs):
                    P.op("pe", "matmul", [onesf, de], [pEG], pEG[:, hh, :], lhsT=onesf[:], rhs=de[:, h, :], start=True, stop=True)
                attnT, KGT, QGT = attn_r.next(), kgt_r.next(), qgt_r.next()
                P.op("dve", "tensor_tensor", [pQK, E], [attnT], out=attnT[:], in0=pQK[:], in1=E[:], op=ALU.mult)
                P.op("dve", "tensor_tensor", [KTt, pEG], [KGT], out=KGT[:], in0=KTt[:, gsl, :], in1=pEG[:], op=ALU.mult)
                P.op("dve", "tensor_tensor", [QTt, pEG], [QGT], out=QGT[:], in0=QTt[:, gsl, :], in1=pEG[:], op=ALU.mult)
                st_[G].update(attnT=attnT, KGT=KGT, QGT=QGT)
            for G in GR:
                NAT = st_[G]["NAT"]
                pP1 = pf.next()
                for hh in range(4):
                    P.op("pe", "matmul", [NAT, identb], [pP1], pP1[:, hh, :], lhsT=NAT[:, hh, 0, :], rhs=identb[:], start=True, stop=True)
                Dm, DT = D_r.next(), DT_r.next()
                P.op("dve", "tensor_tensor", [identb, pP1], [Dm], out=Dm[:], in0=identb_bc, in1=pP1[:], op=ALU.add)
                P.op("pool", "tensor_tensor", [identb, NAT], [DT], out=DT[:], in0=identb_bc, in1=NAT[:, :, 0, :], op=ALU.add)
                st_[G].update(Dm=Dm, DT=DT)
            for lv in range(1, 7):
                for G in GR:
                    NAT, Dm = st_[G]["NAT"], st_[G]["Dm"]
                    pP1 = pf.next()
                    for hh in range(4):
                        P.op("pe", "matmul", [NAT, Dm], [pP1], pP1[:, hh, :], lhsT=NAT[:, hh, lv, :], rhs=Dm[:, hh, :], start=True, stop=True)
                    P1s = p1_r.next()
                    P.op("act", "copy", [pP1], [P1s], out=P1s[:], in_=pP1[:])
                    st_[G]["P1s"] = P1s
                for G in GR:
                    Dm, DT, P1s = st_[G]["Dm"], st_[G]["DT"], st_[G]["P1s"]
                    if lv < 6:
                        pY = pf.next()
                        for hh in range(4):
                            P.op("pe", "matmul", [DT, P1s], [pY], pY[:, hh, :], lhsT=DT[:, hh, :], rhs=P1s[:, hh, :], start=True, stop=True)
                    pYT = pf.next()
                    for hh in range(4):
                        P.op("pe", "matmul", [P1s, DT], [pYT], pYT[:, hh, :], lhsT=P1s[:, hh, :], rhs=DT[:, hh, :], start=True, stop=True)
                    if lv < 6:
                        Dn = D_r.next()
                        P.op("dve", "tensor_tensor", [Dm, pY], [Dn], out=Dn[:], in0=Dm[:], in1=pY[:], op=ALU.add)
                        st_[G]["Dm"] = Dn
                    DTn = DT_r.next() if lv < 6 else TT_r.next()
                    P.op("dve", "tensor_tensor", [DT, pYT], [DTn], out=DTn[:], in0=DT[:], in1=pYT[:], op=ALU.add)
                    st_[G]["DT"] = DTn
            yield
            for G in GR:
                hs, gsl = hsl[G], gsls[G]
                KGT = st_[G]["KGT"]
                pR = pf.next()
                for hh, h in enumerate(hs):
                    P.op("pe", "matmul", [KGT, Sb], [pR], pR[:, hh, :], lhsT=KGT[:, hh, :], rhs=Sb[:, h, :], start=True, stop=True)
                Rt = R_r.next()
                P.op("dve", "tensor_tensor", [vtok, pR], [Rt], out=Rt[:], in0=vtok[:, gsl, :], in1=pR[:], op=ALU.subtract)
                st_[G]["Rt"] = Rt
            for G in GR:
                TT, Rt = st_[G]["DT"], st_[G]["Rt"]
                pVN = pf.next()
                for hh in range(4):
                    P.op("pe", "matmul", [TT, Rt], [pVN], pVN[:, hh, :], lhsT=TT[:, hh, :], rhs=Rt[:, hh, :], start=True, stop=True)
                VN = VN_r.next()
                P.op("dve", "tensor_tensor", [pVN, beta], [VN], out=VN[:], in0=pVN[:],
                     in1=beta[:, t, d8 + G * 4:d8 + G * 4 + 4].unsqueeze(2).broadcast_to([128, 4, 128]), op=ALU.mult)
                st_[G]["VN"] = VN
            for G in GR:
                hs, gsl = hsl[G], gsls[G]
                QGT, attnT, VN = st_[G]["QGT"], st_[G]["attnT"], st_[G]["VN"]
                pO = pf.next()
                for hh, h in enumerate(hs):
                    P.op("pe", "matmul", [QGT, Sb], [pO], pO[:, hh, :], lhsT=QGT[:, hh, :], rhs=Sb[:, h, :], start=True, stop=False)
                    P.op("pe", "matmul", [attnT, VN], [pO], pO[:, hh, :], lhsT=attnT[:, hh, :], rhs=VN[:, hh, :], start=False, stop=True)
                pS = pf.next()
                for hh, h in enumerate(hs):
                    P.op("pe", "matmul", [kdec, VN], [pS], pS[:, hh, :], lhsT=kdec[:, h, :], rhs=VN[:, hh, :], start=True, stop=True)
                P.op("dve", "tensor_tensor", [S32, egl], [S32], out=S32[:, gsl, :], in0=S32[:, gsl, :],
                     in1=egl[:, t, d8 + G * 4:d8 + G * 4 + 4].unsqueeze(2).broadcast_to([128, 4, 128]), op=ALU.mult)
                P.op("dve", "tensor_tensor", [S32, pS], [S32], out=S32[:, gsl, :], in0=S32[:, gsl, :], in1=pS[:], op=ALU.add)
                P.op("act", "copy", [S32], [Sb], out=Sb[:, gsl, :], in_=S32[:, gsl, :])
                if d == 0:
                    P.op("act", "copy", [pO], [Ot], out=Ot[:, gsl, :], in_=pO[:])
                else:
                    osum, osq, on, ost = osum_r.next(), osq_r.next(), on_r.next(), ost_r.next()
                    P.op("dve", "tensor_tensor", [pO, OFt], [osum], out=osum[:], in0=pO[:], in1=OFt[:, gsl, :], op=ALU.add)
                    P.op("pool", "tensor_tensor", [osum], [osq], out=osq[:], in0=osum[:], in1=osum[:], op=ALU.mult)
                    P.op("dve", "tensor_reduce", [osq], [ost], out=ost[:, :, 0], in_=osq[:], axis=AX.X, op=ALU.add)
                    P.op("dve", "tensor_scalar", [ost], [ost], out=ost[:, :, 1], in0=ost[:, :, 0], scalar1=1.0 / 128, scalar2=EPS,
                         op0=ALU.mult, op1=ALU.add)
                    P.op("act", "activation", [ost], [ost], out=ost[:, :, 2], in_=ost[:, :, 1], func=AF.Sqrt)
                    P.op("dve", "reciprocal", [ost], [ost], out=ost[:, :, 3], in_=ost[:, :, 2])
                    P.op("dve", "tensor_tensor", [osum, ost], [on], out=on[:], in0=osum[:],
                         in1=ost[:, :, 3:4].broadcast_to([128, 4, 128]), op=ALU.mult)
                    pT = pf.next()
                    for hh in range(4):
                        P.op("pe", "matmul", [on, identf], [pT], pT[:, hh, :], lhsT=on[:, hh, :], rhs=identf[:], start=True, stop=True)
                    P.op("dve", "scalar_tensor_tensor", [pT, onorm, GAt], [OAt], out=OAt[:, gsl, :], in0=pT[:], scalar=onorm[:, 0:1],
                         in1=GAt[:, gsl, :], op0=ALU.mult, op1=ALU.mult)
            if d == 0:
                P.dma("pool", OF[tok, :].rearrange("p (h e) -> p h e", h=8), Ot[:], reads=[Ot])
            else:
                P.dma("pool", OA_T.rearrange("(h p) t -> p h t", p=128)[:, :, tok], OAt[:], reads=[OAt])
        prev_g = None
        for t in order:
            g_ = tile_gen(t)
            next(g_)
            if prev_g is not None:
                for _ in prev_g:
                    pass
            prev_g = g_
        if prev_g is not None:
            for _ in prev_g:
                pass
        if d == 0:
            P.barrier()
    P.end()
    if _STOP == 3:
        nc._P = P
        return nc


    P.begin()
    onesb = P.sb([128, 128], BF16)
    P.op("dve", "memset", [], [onesb], onesb[:], 1.0)
    kb = P.sb([128, LMAX // 128], F32)
    P.dma("sp", kb[:], kbias, writes=[kb])
    kpe = P.sb([128, L], BF16)
    P.op("pool", "memset", [], [kpe], kpe[64:128, :], 0.0)
    P.dma("sp", kpe[0:64, :], KPE_T, writes=[kpe])
    kn_r = P.ring(2, [128, L], BF16)
    vh_r = P.ring(2, [128, NT, 128], BF16)
    qn_r = P.ring(2, [128, 512], BF16)
    qp_r = P.ring(2, [128, 512], BF16)
    for b_ in qp_r.bufs:
        P.op("pool", "memset", [], [b_], b_[64:128, :], 0.0)
    gb_r = P.ring(2, [128, 512], BF16)
    pt_r = P.ring(4, [128, 512], BF16)
    pS_r = P.ring(4, [128, 512], F32, psum=True)
    pO_r = P.ring(2, [128, 512], F32, psum=True)
    pZ_r = P.ring(2, [128, 512], F32, psum=True)
    rs_r = P.ring(2, [128, 512], F32)
    o1_r = P.ring(2, [128, 512], F32)
    ob_r = P.ring(2, [128, 512], BF16)
    sm_scale = float((D_NOPE + D_ROPE) ** -0.5)
    for h in range(8):
        knh, vh = kn_r.next(), vh_r.next()
        P.dma("sp", knh[:], KN_T[h * 128:(h + 1) * 128, :], writes=[knh])
        for t0_ in range(0, NT, 8):
            t1_ = min(NT, t0_ + 8)
            P.dma("act", vh[:, t0_:t1_, :], VB[t0_ * 128:t1_ * 128, h * 128:(h + 1) * 128].rearrange("(t p) e -> p t e", p=128), writes=[vh])
        for qb in range(NB):
            blk = slice(qb * 512, (qb + 1) * 512)
            qn, qp, gb = qn_r.next(), qp_r.next(), gb_r.next()
            P.dma("sp", qn[:], QF_T[h * 192:h * 192 + 128, blk], writes=[qn])
            P.dma("sp", qp[0:64, :], QF_T[h * 192 + 128:h * 192 + 192, blk], writes=[qp])
            P.dma("act", gb[:], GB_T[h * 128:(h + 1) * 128, blk], writes=[gb])
            pO, pZ = pO_r.next(), pZ_r.next()

            def scores(kt):
                ps_ = pS_r.next()
                P.op("pe", "matmul", [knh, qn], [ps_], ps_[:], lhsT=knh[:, kt * 128:(kt + 1) * 128], rhs=qn[:], start=True, stop=False)
                P.op("pe", "matmul", [kpe, qp], [ps_], ps_[:], lhsT=kpe[:, kt * 128:(kt + 1) * 128], rhs=qp[:], start=False, stop=True)
                return ps_
            pend = [scores(0)]
            if NT > 1:
                pend.append(scores(1))
            for kt in range(NT):
                cur = pend.pop(0)
                if kt + 2 < NT:
                    pend.append(scores(kt + 2))
                pt_ = pt_r.next()
                P.op("act", "activation", [cur, kb], [pt_], out=pt_[:], in_=cur[:], func=AF.Exp, bias=kb[:, kt:kt + 1], scale=sm_scale)
                P.op("pe", "matmul", [onesb, pt_], [pZ], pZ[:], lhsT=onesb[:], rhs=pt_[:], start=(kt == 0), stop=(kt == NT - 1))
                P.op("pe", "matmul", [vh, pt_], [pO], pO[:], lhsT=vh[:, kt, :], rhs=pt_[:], start=(kt == 0), stop=(kt == NT - 1))
            rs, o1, ob = rs_r.next(), o1_r.next(), ob_r.next()
            P.op("dve", "reciprocal", [pZ], [rs], out=rs[:], in_=pZ[:])
            P.op("dve", "tensor_tensor", [pO, rs], [o1], out=o1[:], in0=pO[:], in1=rs[:], op=ALU.mult)
            P.op("pool", "tensor_tensor", [o1, gb], [ob], out=ob[:], in0=o1[:], in1=gb[:], op=ALU.mult)
            P.dma("pool", OB_T[h * 128:(h + 1) * 128, blk], ob[:], reads=[ob])
    P.end()
    if _STOP == 5:
        nc._P = P
        return nc

    M_T = dscr("M_T", [NT, 128, 16, 128], BF16)
    P.begin()
    wpa = P.sb([128, 8, 2048], BF16)
    wpb = P.sb([128, 8, 2048], BF16)
    P.dma("pool", wpa[:], w_pa.rearrange("(c p) n -> p c n", p=128), writes=[wpa])
    P.dma("pool", wpb[:], w_pb.rearrange("(c p) n -> p c n", p=128), writes=[wpb])
    oa_r2 = P.ring(2, [128, 8, 512], BF16)
    ob_r2 = P.ring(2, [128, 8, 512], BF16)
    gm_r = P.ring(3, [128, 2, 512], BF16)
    m_r = P.ring(4, [128, 512], F32)
    mt_r = P.ring(3, [128, 512], BF16)
    pm_r = P.ring(6, [128, 512], F32, psum=True)
    for tb in range(NB):
        blk = slice(tb * 512, (tb + 1) * 512)
        oat, obt = oa_r2.next(), ob_r2.next()
        P.dma("sp", oat[:], OA_T.rearrange("(c p) t -> p c t", p=128)[:, :, blk], writes=[oat])
        P.dma("act", obt[:], OB_T.rearrange("(c p) t -> p c t", p=128)[:, :, blk], writes=[obt])
        for ct in range(16):
            gm = gm_r.next()
            P.dma("sp", gm[:, 0, :], GMA_T[ct * 128:(ct + 1) * 128, blk], writes=[gm])
            P.dma("act", gm[:, 1, :], GMB_T[ct * 128:(ct + 1) * 128, blk], writes=[gm])
            pA, pB = pm_r.next(), pm_r.next()
            for c in range(8):
                P.op("pe", "matmul", [wpa, oat], [pA], pA[:], lhsT=wpa[:, c, ct * 128:(ct + 1) * 128], rhs=oat[:, c, :], start=(c == 0), stop=(c == 7))
            for c in range(8):
                P.op("pe", "matmul", [wpb, obt], [pB], pB[:], lhsT=wpb[:, c, ct * 128:(ct + 1) * 128], rhs=obt[:, c, :], start=(c == 0), stop=(c == 7))
            m1, m2, mt = m_r.next(), m_r.next(), mt_r.next()
            P.op("dve", "tensor_tensor", [pA, gm], [m1], out=m1[:], in0=pA[:], in1=gm[:, 0, :], op=ALU.mult)
            P.op("dve", "tensor_tensor", [pB, gm], [m2], out=m2[:], in0=pB[:], in1=gm[:, 1, :], op=ALU.mult)
            P.op("pool", "tensor_tensor", [m1, m2], [mt], out=mt[:], in0=m1[:], in1=m2[:], op=ALU.add)
            P.dma("pool", M_T[tb * 4:tb * 4 + 4, :, ct, :].rearrange("j p t -> p j t"), mt[:].rearrange("p (j t) -> p j t", t=128), reads=[mt])
    P.end()
    if _STOP == 6:
        nc._P = P
        return nc

    P.begin()
    wout = P.sb([128, 16, 2048], BF16)
    P.dma("pool", wout[:], w_out.rearrange("(c p) n -> p c n", p=128), writes=[wout])
    nfb = P.sb([128, D_MODEL], F32)
    P.dma("sp", nfb[:], norm_f.partition_broadcast(128), writes=[nfb])
    mt_r2 = P.ring(2, [128, 16, 128], BF16)
    x_r = P.ring(2, [128, D_MODEL], F32)
    z_r = P.ring(2, [128, D_MODEL], F32)
    y_r = P.ring(2, [128, D_MODEL], F32)
    junk = P.sb([128, D_MODEL], BF16)
    st_r2 = P.ring(4, [128, 4], F32)
    pm_r = P.ring(6, [128, 512], F32, psum=True)
    for tt in range(NT):
        tok = slice(tt * 128, (tt + 1) * 128)
        mtt, xt = mt_r2.next(), x_r.next()
        P.dma("sp", mtt[:], M_T[tt], writes=[mtt])
        P.dma("act", xt[:], x[tok, :], writes=[xt])
        z = z_r.next()
        for cg in range(4):
            py_ = pm_r.next()
            for c in range(16):
                P.op("pe", "matmul", [mtt, wout], [py_], py_[:], lhsT=mtt[:, c, :], rhs=wout[:, c, cg * 512:(cg + 1) * 512], start=(c == 0), stop=(c == 15))
            P.op("dve", "tensor_tensor", [py_, xt], [z], out=z[:, cg * 512:(cg + 1) * 512], in0=py_[:], in1=xt[:, cg * 512:(cg + 1) * 512], op=ALU.add)
        st = st_r2.next()
        P.op("act", "activation", [z], [junk, st], out=junk[:], in_=z[:], func=AF.Square, accum_out=st[:, 0:1])
        P.op("dve", "tensor_scalar", [st], [st], out=st[:, 1:2], in0=st[:, 0:1], scalar1=1.0 / D_MODEL, scalar2=EPS, op0=ALU.mult, op1=ALU.add)
        P.op("act", "activation", [st], [st], out=st[:, 2:3], in_=st[:, 1:2], func=AF.Sqrt)
        P.op("dve", "reciprocal", [st], [st], out=st[:, 3:4], in_=st[:, 2:3])
        yv = y_r.next()
        P.op("dve", "scalar_tensor_tensor", [z, st, nfb], [yv], out=yv[:], in0=z[:], scalar=st[:, 3:4], in1=nfb[:], op0=ALU.mult, op1=ALU.mult)
        P.dma("pool", y[tok, :], yv[:], reads=[yv])
    P.end()
    if _STOP == 7:
        nc._P = P
        return nc

    nc._P = P
    return nc


_NC_CACHE = {}


def _core_map(consts, shared, xseq, L):
    valid = xseq.shape[0]
    xp = np.zeros((L, D_MODEL), np.float32)
    xp[:valid] = xseq
    kb = np.zeros((LMAX,), np.float32)
    kb[valid:] = -BIG
    m = dict(shared)
    m.update(consts)
    m["x"] = xp
    m["kbias"] = np.ascontiguousarray(kb.reshape(LMAX // 128, 128).T)
    tm = np.zeros((128, LMAX), np.float32)
    tm[:, :valid] = 1.0
    m["tmask"] = tm
    return m


def kernel(x_prompt, x_sample, norm_in, w_in, conv_w, a_log_f, dt_bias_f, a_log_b, dt_bias_b, o_norm_a,
           q_a_norm, w_q_b, kv_a_norm, w_kv_b, w_pa, w_pb, w_out, norm_f):
    f = lambda a: np.ascontiguousarray(np.asarray(a, dtype=np.float32))
    x_prompt, x_sample = f(x_prompt), f(x_sample)
    L = LMAX
    shared = {
        "norm_in": f(norm_in)[0], "w_in": f(w_in)[0],
        "conv_w": np.ascontiguousarray(f(conv_w)[0].reshape(KCONV * 24, 128)),
        "a_log": np.concatenate([f(a_log_f)[0], f(a_log_b)[0]]),
        "dt_bias": np.concatenate([f(dt_bias_f)[0], f(dt_bias_b)[0]]),
        "o_norm_a": f(o_norm_a)[0], "q_a_norm": f(q_a_norm)[0], "w_q_b": f(w_q_b)[0],
        "kv_a_norm": f(kv_a_norm)[0], "w_kv_b": f(w_kv_b)[0], "w_pa": f(w_pa)[0], "w_pb": f(w_pb)[0],
        "w_out": f(w_out)[0], "norm_f": f(norm_f),
    }
    consts = host_consts()
    seqs = [x_prompt[i] for i in range(4)] + [x_sample[i] for i in range(4)]
    in_maps = [_core_map(consts, shared, s, L) for s in seqs]
    if L not in _NC_CACHE:
        _NC_CACHE[L] = build(L)
    nc = _NC_CACHE[L]
    res = run_bass_kernel_spmd(nc, in_maps, core_ids=list(range(8)))
    ys = [np.asarray(r["y"], dtype=np.float32) for r in res.results]
    y_prompt = np.stack([ys[i][:x_prompt.shape[1]] for i in range(4)], axis=0)
    y_sample = np.stack([ys[4 + i] for i in range(4)], axis=0)
    return (y_prompt, y_sample)
```
